# Optimizing a Trainium2 kernel written in Bass

```python
import math
import jax, jax.numpy as jnp
from jax import lax
import numpy as np

D_MODEL = 2048
BATCH = 4
SEQ = 8192
DEPTH = 2

N_EVEN = (DEPTH + 1) // 2
N_ODD = DEPTH // 2
D_FF = 5632
ROPE_THETA = 10000.0
NORM_EPS = 1e-6
MASK_VALUE = -1e30
FORCE_SCORE = 1e4

NSA_HEAD_DIM = 64
NSA_HEADS = (D_MODEL // 2) // NSA_HEAD_DIM
NSA_KV_GROUPS = 4
NSA_CMP_STRIDE = 16
NSA_CMP_LEN = 2 * NSA_CMP_STRIDE
NSA_CMP_HIDDEN = 256
NSA_SEL_BLOCK = 64
NSA_N_SEL = 16
NSA_WINDOW = 512
NSA_Q_BLOCK = 64
NSA_Q_WIDTH = NSA_HEADS * NSA_HEAD_DIM
NSA_KV_WIDTH = NSA_KV_GROUPS * NSA_HEAD_DIM

SSM_HEAD_DIM = 64
SSM_HEADS = (D_MODEL // 2) // SSM_HEAD_DIM
SSM_D_INNER = SSM_HEADS * SSM_HEAD_DIM
SSM_GROUPS = 4
SSM_D_STATE = 128
SSM_CONV = 4
SSM_CHUNK = 128
SSM_CONV_DIM = SSM_D_INNER + 2 * SSM_GROUPS * SSM_D_STATE

EVEN_SIZES = (NSA_Q_WIDTH, NSA_KV_WIDTH, NSA_KV_WIDTH, NSA_KV_WIDTH, NSA_KV_WIDTH, NSA_KV_WIDTH, NSA_KV_WIDTH, 3 * NSA_HEADS, SSM_D_INNER, SSM_CONV_DIM, SSM_HEADS)
EVEN_IN = NSA_Q_WIDTH + 6 * NSA_KV_WIDTH + 3 * NSA_HEADS + SSM_D_INNER + SSM_CONV_DIM + SSM_HEADS
EVEN_MIX_OUT = NSA_Q_WIDTH + SSM_D_INNER

RET_HEADS = 8
RET_KEY_DIM = D_MODEL // RET_HEADS
RET_VALUE_DIM = 2 * RET_KEY_DIM
RET_CHUNK = 128
RET_QK_WIDTH = RET_HEADS * RET_KEY_DIM
RET_V_WIDTH = RET_HEADS * RET_VALUE_DIM
ODD_SIZES = (RET_QK_WIDTH, RET_QK_WIDTH, RET_V_WIDTH, RET_V_WIDTH)
ODD_IN = 2 * RET_QK_WIDTH + 2 * RET_V_WIDTH

kernel_name = "hybrid_nsa_mamba2_retention_macaron"


def _split(t, sizes):
    offs, acc = [], 0
    for n in sizes[:-1]:
        acc += n
        offs.append(acc)
    return jnp.split(t, offs, axis=-1)


def rms_norm(x, g):
    xf = x.astype(jnp.float32)
    y = xf * lax.rsqrt(jnp.mean(xf * xf, axis=-1, keepdims=True) + NORM_EPS)
    return (y * g.astype(jnp.float32)).astype(x.dtype)


def rope_tables(seq, dim):
    inv = 1.0 / (ROPE_THETA ** (jnp.arange(0, dim, 2, dtype=jnp.float32) / dim))
    ang = jnp.arange(seq, dtype=jnp.float32)[:, None] * inv[None, :]
    return jnp.cos(ang), jnp.sin(ang)


def apply_rope(x, cos, sin):
    xf = x.astype(jnp.float32)
    x1, x2 = jnp.split(xf, 2, axis=-1)
    c = cos[None, :, None, :]
    s = sin[None, :, None, :]
    return jnp.concatenate([x1 * c - x2 * s, x2 * c + x1 * s], axis=-1).astype(x.dtype)


def swiglu(x, w_gate, w_up, w_down):
    return (jax.nn.silu(x @ w_gate) * (x @ w_up)) @ w_down


def masked_softmax(scores, mask):
    p = jax.nn.softmax(jnp.where(mask, scores, MASK_VALUE), axis=-1)
    return p * mask


def nsa_compress(t, pos_emb, w1, w2):
    b, s, g, d = t.shape
    nc = s // NSA_CMP_STRIDE
    ch = t.reshape(b, nc, NSA_CMP_STRIDE, g, d)
    nxt = jnp.concatenate([ch[:, 1:], jnp.zeros_like(ch[:, :1])], axis=1)
    blk = jnp.concatenate([ch, nxt], axis=2) + pos_emb[None, None, :, None, :].astype(t.dtype)
    blk = jnp.moveaxis(blk, 3, 2).reshape(b, nc, g, NSA_CMP_LEN * d)
    return jax.nn.silu(blk @ w1) @ w2


def nsa_attention(q, k_cmp, v_cmp, k_slc, v_slc, k_win, v_win, gates):
    f32 = jnp.float32
    b, s, h, d = q.shape
    g = NSA_KV_GROUPS
    hg = h // g
    scale = d ** -0.5
    nc = s // NSA_CMP_STRIDE
    ns = s // NSA_SEL_BLOCK
    n_sel = min(NSA_N_SEL, ns)
    nq = s // NSA_Q_BLOCK
    qn = NSA_Q_BLOCK
    cmp_end = jnp.arange(nc) * NSA_CMP_STRIDE + NSA_CMP_LEN - 1
    sel_ids = jnp.arange(ns)
    kcf = k_cmp.astype(f32)
    vcf = v_cmp.astype(f32)
    kb = jnp.moveaxis(k_slc.reshape(b, ns, NSA_SEL_BLOCK, g, d), 3, 1)
    vb = jnp.moveaxis(v_slc.reshape(b, ns, NSA_SEL_BLOCK, g, d), 3, 1)
    pad = jnp.zeros((b, NSA_WINDOW, g, d), k_win.dtype)
    kw = jnp.concatenate([pad, k_win], axis=1)
    vw = jnp.concatenate([pad, v_win], axis=1)
    gather = jax.vmap(jax.vmap(lambda t, i: t[i]))
    qb = jnp.moveaxis(q.reshape(b, nq, qn, g, hg, d), 1, 0)
    gb = jnp.moveaxis(gates.reshape(b, nq, qn, g, hg, 3), 1, 0)
    starts = jnp.arange(nq, dtype=jnp.int32) * qn

    def block(args):
        qi, gi, start = args
        qf = qi.astype(f32) * scale
        t = start + jnp.arange(qn)
        sc = jnp.einsum('bqgkd,bngd->bgkqn', qf, kcf)
        p_c = masked_softmax(sc, cmp_end[None, :] <= t[:, None])
        o_c = jnp.einsum('bgkqn,bngd->bqgkd', p_c, vcf)
        imp = p_c.sum(axis=2).reshape(b, g, qn, ns, NSA_SEL_BLOCK // NSA_CMP_STRIDE).sum(-1)
        cur = (t // NSA_SEL_BLOCK)[:, None]
        j = sel_ids[None, :]
        forced = (j == 0) | (j == cur) | (j == cur - 1)
        imp = jnp.where(forced, FORCE_SCORE, jnp.where(j <= cur, imp, -1.0))
        _, idx = lax.top_k(imp, n_sel)
        ks = gather(kb, idx).astype(f32)
        vs = gather(vb, idx).astype(f32)
        ss = jnp.einsum('bqgkd,bgqnld->bgkqnl', qf, ks).reshape(b, g, hg, qn, n_sel * NSA_SEL_BLOCK)
        kpos = (idx[..., None] * NSA_SEL_BLOCK + jnp.arange(NSA_SEL_BLOCK)).reshape(b, g, 1, qn, n_sel * NSA_SEL_BLOCK)
        p_s = masked_softmax(ss, kpos <= t[None, None, None, :, None])
        p_s = p_s.reshape(b, g, hg, qn, n_sel, NSA_SEL_BLOCK)
        o_s = jnp.einsum('bgkqnl,bgqnld->bqgkd', p_s, vs)
        kwi = lax.dynamic_slice_in_dim(kw, start, NSA_WINDOW + qn, axis=1).astype(f32)
        vwi = lax.dynamic_slice_in_dim(vw, start, NSA_WINDOW + qn, axis=1).astype(f32)
        wpos = start - NSA_WINDOW + jnp.arange(NSA_WINDOW + qn)
        diff = t[:, None] - wpos[None, :]
        m_w = (diff >= 0) & (diff < NSA_WINDOW) & (wpos[None, :] >= 0)
        sw = jnp.einsum('bqgkd,bngd->bgkqn', qf, kwi)
        o_w = jnp.einsum('bgkqn,bngd->bqgkd', masked_softmax(sw, m_w), vwi)
        gf = gi.astype(f32)
        o = gf[..., 0:1] * o_c + gf[..., 1:2] * o_s + gf[..., 2:3] * o_w
        return o.astype(q.dtype)

    out = lax.map(block, (qb, gb, starts))
    return jnp.moveaxis(out, 0, 1).reshape(b, s, h * d)


def ssd_scan(x, dt, a, bm, cm):
    f32 = jnp.float32
    b, s, h, p = x.shape
    g, n = bm.shape[2], bm.shape[3]
    hg = h // g
    L = SSM_CHUNK
    nc = s // L
    xd = (x.astype(f32) * dt[..., None]).reshape(b, nc, L, g, hg, p)
    la = (dt * a).reshape(b, nc, L, g, hg)
    bmc = bm.astype(f32).reshape(b, nc, L, g, n)
    cmc = cm.astype(f32).reshape(b, nc, L, g, n)
    causal = jnp.tril(jnp.ones((L, L), dtype=bool))[None, :, :, None, None]

    def step(state, inp):
        xc, lac, bc, cc = inp
        cum = jnp.cumsum(lac, axis=1)
        seg = cum[:, :, None] - cum[:, None, :]
        decay = jnp.exp(jnp.where(causal, seg, -jnp.inf))
        cb = jnp.einsum('btgn,bsgn->btsg', cc, bc)
        y = jnp.einsum('btsg,btsgk,bsgkp->btgkp', cb, decay, xc)
        y = y + jnp.einsum('btgn,bgkpn,btgk->btgkp', cc, state, jnp.exp(cum))
        tail = jnp.exp(cum[:, -1:] - cum)
        state = state * jnp.exp(cum[:, -1])[..., None, None] + jnp.einsum('bsgn,bsgk,bsgkp->bgkpn', bc, tail, xc)
        return state, y

    init = jnp.zeros((b, g, hg, p, n), f32)
    seqs = (jnp.moveaxis(xd, 1, 0), jnp.moveaxis(la, 1, 0), jnp.moveaxis(bmc, 1, 0), jnp.moveaxis(cmc, 1, 0))
    _, ys = lax.scan(step, init, seqs)
    return jnp.moveaxis(ys, 0, 1).reshape(b, s, h, p)


def mamba2_mixer(z, xbc, dt_raw, conv_w, conv_b, dt_bias, a_log, d_skip, norm_w):
    f32 = jnp.float32
    b, s, _ = xbc.shape
    xbc = lax.conv_general_dilated(xbc, conv_w[:, None, :], window_strides=(1,), padding=[(SSM_CONV - 1, 0)],
                                   dimension_numbers=('NWC', 'WIO', 'NWC'), feature_group_count=SSM_CONV_DIM)
    xbc = jax.nn.silu(xbc + conv_b)
    xs, bm, cm = _split(xbc, (SSM_D_INNER, SSM_GROUPS * SSM_D_STATE, SSM_GROUPS * SSM_D_STATE))
    xs = xs.reshape(b, s, SSM_HEADS, SSM_HEAD_DIM)
    bm = bm.reshape(b, s, SSM_GROUPS, SSM_D_STATE)
    cm = cm.reshape(b, s, SSM_GROUPS, SSM_D_STATE)
    dt = jax.nn.softplus(dt_raw.astype(f32) + dt_bias.astype(f32))
    a = -jnp.exp(a_log.astype(f32))
    y = ssd_scan(xs, dt, a, bm, cm) + d_skip.astype(f32)[:, None] * xs.astype(f32)
    y = y.reshape(b, s, SSM_D_INNER) * jax.nn.silu(z.astype(f32))
    yg = y.reshape(b, s, SSM_GROUPS, SSM_D_INNER // SSM_GROUPS)
    yg = yg * lax.rsqrt(jnp.mean(yg * yg, axis=-1, keepdims=True) + NORM_EPS)
    return (yg.reshape(b, s, SSM_D_INNER) * norm_w.astype(f32)).astype(z.dtype)


def even_mixer(h, w_in, cmp_pos, cmp_w1, cmp_w2, conv_w, conv_b, dt_bias, a_log, d_skip, ssm_norm, w_out, cos, sin):
    b, s, _ = h.shape
    q, kc, vc, ksl, vsl, kwn, vwn, gate, z, xbc, dt_raw = _split(h @ w_in, EVEN_SIZES)
    kv = lambda t: t.reshape(b, s, NSA_KV_GROUPS, NSA_HEAD_DIM)
    q = apply_rope(q.reshape(b, s, NSA_HEADS, NSA_HEAD_DIM), cos, sin)
    kc = apply_rope(kv(kc), cos, sin)
    ksl = apply_rope(kv(ksl), cos, sin)
    kwn = apply_rope(kv(kwn), cos, sin)
    k_cmp = nsa_compress(kc, cmp_pos[0], cmp_w1[0], cmp_w2[0])
    v_cmp = nsa_compress(kv(vc), cmp_pos[1], cmp_w1[1], cmp_w2[1])
    gates = jax.nn.sigmoid(gate.astype(jnp.float32)).reshape(b, s, NSA_HEADS, 3)
    o_nsa = nsa_attention(q, k_cmp, v_cmp, ksl, kv(vsl), kwn, kv(vwn), gates)
    o_ssm = mamba2_mixer(z, xbc, dt_raw, conv_w, conv_b, dt_bias, a_log, d_skip, ssm_norm)
    return jnp.concatenate([o_nsa, o_ssm.astype(o_nsa.dtype)], axis=-1) @ w_out


def retention(q, k, v, log_gamma):
    f32 = jnp.float32
    b, s, h, dk = q.shape
    dv = v.shape[-1]
    L = RET_CHUNK
    nc = s // L
    idx = jnp.arange(L, dtype=f32)
    diff = idx[:, None] - idx[None, :]
    inner_decay = jnp.where(diff[None] >= 0, jnp.exp(jnp.maximum(diff, 0.0)[None] * log_gamma[:, None, None]), 0.0)
    cross_decay = jnp.exp((idx + 1.0)[None, :] * log_gamma[:, None])
    tail_decay = jnp.exp((L - 1.0 - idx)[None, :] * log_gamma[:, None])
    chunk_decay = jnp.exp(L * log_gamma)

    def step(state, inp):
        qc, kc, vc = inp
        att = jnp.einsum('bihd,bjhd->bhij', qc, kc) * inner_decay
        y = jnp.einsum('bhij,bjhe->bihe', att, vc)
        y = y + jnp.einsum('bihd,bhde->bihe', qc, state) * cross_decay.T[None, :, :, None]
        state = state * chunk_decay[None, :, None, None] + jnp.einsum('bjhd,bjhe,hj->bhde', kc, vc, tail_decay)
        return state, y

    chunks = lambda t: jnp.moveaxis(t.astype(f32).reshape(b, nc, L, h, t.shape[-1]), 1, 0)
    init = jnp.zeros((b, h, dk, dv), f32)
    _, ys = lax.scan(step, init, (chunks(q), chunks(k), chunks(v)))
    return jnp.moveaxis(ys, 0, 1).reshape(b, s, h, dv)


def odd_mixer(h, w_in, w_out, cos, sin):
    f32 = jnp.float32
    b, s, _ = h.shape
    q, k, v, gate = _split(h @ w_in, ODD_SIZES)
    q = apply_rope(q.reshape(b, s, RET_HEADS, RET_KEY_DIM), cos, sin)
    k = apply_rope(k.reshape(b, s, RET_HEADS, RET_KEY_DIM), cos, sin) * (RET_KEY_DIM ** -0.5)
    log_gamma = jnp.log1p(-jnp.exp2(-5.0 - jnp.arange(RET_HEADS, dtype=f32)))
    y = retention(q, k, v.reshape(b, s, RET_HEADS, RET_VALUE_DIM), log_gamma)
    mu = jnp.mean(y, axis=-1, keepdims=True)
    var = jnp.mean(jnp.square(y - mu), axis=-1, keepdims=True)
    y = ((y - mu) * lax.rsqrt(var + NORM_EPS)).reshape(b, s, RET_V_WIDTH)
    return (jax.nn.silu(gate.astype(f32)) * y).astype(h.dtype) @ w_out


def setup_inputs(seed: int = 0) -> dict:
    key = jax.random.key(seed)
    ks = jax.random.split(key, 20)
    f32 = jnp.float32

    def nrm(k, shape, fan_in):
        return jax.random.normal(k, shape, f32) * (fan_in ** -0.5)

    x = jax.random.normal(ks[0], (BATCH, SEQ, D_MODEL), f32)
    norm_g = 1.0 + 0.02 * jax.random.normal(ks[1], (DEPTH, 6, D_MODEL), f32)
    ffn_w_gate = nrm(ks[2], (DEPTH, 2, D_MODEL, D_FF), D_MODEL)
    ffn_w_up = nrm(ks[3], (DEPTH, 2, D_MODEL, D_FF), D_MODEL)
    ffn_w_down = nrm(ks[4], (DEPTH, 2, D_FF, D_MODEL), D_FF)
    ev_w_in = nrm(ks[5], (N_EVEN, D_MODEL, EVEN_IN), D_MODEL)
    ev_cmp_pos = 0.02 * jax.random.normal(ks[6], (N_EVEN, 2, NSA_CMP_LEN, NSA_HEAD_DIM), f32)
    ev_cmp_w1 = nrm(ks[7], (N_EVEN, 2, NSA_CMP_LEN * NSA_HEAD_DIM, NSA_CMP_HIDDEN), NSA_CMP_LEN * NSA_HEAD_DIM)
    ev_cmp_w2 = nrm(ks[8], (N_EVEN, 2, NSA_CMP_HIDDEN, NSA_HEAD_DIM), NSA_CMP_HIDDEN)
    ev_conv_w = nrm(ks[9], (N_EVEN, SSM_CONV, SSM_CONV_DIM), SSM_CONV)
    ev_conv_b = 0.02 * jax.random.normal(ks[10], (N_EVEN, SSM_CONV_DIM), f32)
    dt0 = jnp.exp(jax.random.uniform(ks[11], (N_EVEN, SSM_HEADS), f32) * (math.log(0.1) - math.log(0.001)) + math.log(0.001))
    ev_dt_bias = dt0 + jnp.log(-jnp.expm1(-dt0))
    ev_a_log = jnp.log(jax.random.uniform(ks[12], (N_EVEN, SSM_HEADS), f32, minval=1.0, maxval=16.0))
    ev_d_skip = 1.0 + 0.1 * jax.random.normal(ks[13], (N_EVEN, SSM_HEADS), f32)
    ev_ssm_norm = 1.0 + 0.02 * jax.random.normal(ks[14], (N_EVEN, SSM_D_INNER), f32)
    ev_w_out = nrm(ks[15], (N_EVEN, EVEN_MIX_OUT, D_MODEL), EVEN_MIX_OUT)
    od_w_in = nrm(ks[16], (N_ODD, D_MODEL, ODD_IN), D_MODEL)
    od_w_out = nrm(ks[17], (N_ODD, RET_V_WIDTH, D_MODEL), RET_V_WIDTH)
    return {"x": x, "norm_g": norm_g, "ffn_w_gate": ffn_w_gate, "ffn_w_up": ffn_w_up, "ffn_w_down": ffn_w_down,
            "ev_w_in": ev_w_in, "ev_cmp_pos": ev_cmp_pos, "ev_cmp_w1": ev_cmp_w1, "ev_cmp_w2": ev_cmp_w2,
            "ev_conv_w": ev_conv_w, "ev_conv_b": ev_conv_b, "ev_dt_bias": ev_dt_bias, "ev_a_log": ev_a_log,
            "ev_d_skip": ev_d_skip, "ev_ssm_norm": ev_ssm_norm, "ev_w_out": ev_w_out,
            "od_w_in": od_w_in, "od_w_out": od_w_out}


def reference(x, norm_g, ffn_w_gate, ffn_w_up, ffn_w_down, ev_w_in, ev_cmp_pos, ev_cmp_w1, ev_cmp_w2,
              ev_conv_w, ev_conv_b, ev_dt_bias, ev_a_log, ev_d_skip, ev_ssm_norm, ev_w_out, od_w_in, od_w_out):
    s = x.shape[1]
    cos_a, sin_a = rope_tables(s, NSA_HEAD_DIM)
    cos_r, sin_r = rope_tables(s, RET_KEY_DIM)
    for i in range(DEPTH):
        g = norm_g[i]
        x = x + 0.5 * rms_norm(swiglu(rms_norm(x, g[0]), ffn_w_gate[i, 0], ffn_w_up[i, 0], ffn_w_down[i, 0]), g[1])
        hm = rms_norm(x, g[2])
        if i % 2 == 0:
            e = i // 2
            mix = even_mixer(hm, ev_w_in[e], ev_cmp_pos[e], ev_cmp_w1[e], ev_cmp_w2[e], ev_conv_w[e], ev_conv_b[e],
                             ev_dt_bias[e], ev_a_log[e], ev_d_skip[e], ev_ssm_norm[e], ev_w_out[e], cos_a, sin_a)
        else:
            o = i // 2
            mix = odd_mixer(hm, od_w_in[o], od_w_out[o], cos_r, sin_r)
        x = x + rms_norm(mix, g[3])
        x = x + 0.5 * rms_norm(swiglu(rms_norm(x, g[4]), ffn_w_gate[i, 1], ffn_w_up[i, 1], ffn_w_down[i, 1]), g[5])
    return x
```

```python
from contextlib import ExitStack
import numpy as np
import concourse.bass as bass
import concourse.mybir as mybir
from concourse.bass_utils import run_bass_kernel_spmd

F32 = mybir.dt.float32
BF16 = mybir.dt.bfloat16
AF = mybir.ActivationFunctionType
ALU = mybir.AluOpType
AX = mybir.AxisListType

D = 2048
DFF = 5632
KD = D // 128
KF = DFF // 128
TT = 512
EPS = 1e-6

EPOCH = 30000
SAME_ENGINE_SYNC = ("act", "dve", "pool")


class Prog:
    ENGS = ("pe", "act", "dve", "pool", "sp")

    def __init__(self, nc, prefix="", bind=None, ext=None):
        self.nc = nc
        self.prefix = prefix
        self.bind = bind
        self.ext = ext
        self.ops = {e: [] for e in self.ENGS}
        self.res = {}
        self.dma_cnt = {}
        self.dma_rr = {}
        self.dkeys = {}
        self.stack = ExitStack()
        self.nm = 0

    def sb(self, name, shape, dt):
        return self.stack.enter_context(self.nc.sbuf_tensor(self.prefix + name, list(shape), dt))

    def ps(self, name, shape, dt=F32):
        return self.stack.enter_context(self.nc.psum_tensor(self.prefix + name, list(shape), dt))

    def dram(self, name, shape, dt, kind="Internal"):
        if self.bind is not None and name in self.bind:
            ap = self.bind[name]
            assert list(ap.shape) == list(shape), (name, ap.shape, shape)
            return ap
        if self.bind is not None and kind == "ExternalOutput":
            raise AssertionError(f"unbound output {name}")
        if self.ext is not None and kind == "ExternalInput":
            self.ext.append((self.prefix + name, name))
        return self.nc.dram_tensor(self.prefix + name, list(shape), dt, kind=kind).ap()

    def _deps(self, reads, writes):
        deps = []
        for r in reads:
            st = self.res.get(r)
            if st and st["w"] is not None:
                deps.append(st["w"])
        for w in writes:
            st = self.res.get(w)
            if st:
                if st["w"] is not None:
                    deps.append(st["w"])
                deps.extend(st["r"])
        return deps

    def _commit(self, tok, reads, writes):
        for r in reads:
            st = self.res.setdefault(r, {"w": None, "r": []})
            st["r"].append(tok)
        for w in writes:
            self.res[w] = {"w": tok, "r": []}

    def op(self, eng, fn, reads=(), writes=()):
        deps = self._deps(reads, writes)
        idx = len(self.ops[eng])
        self.ops[eng].append({"fn": fn, "deps": deps, "sig": False, "dma": None})
        self._commit(("e", eng, idx), reads, writes)

    NSLOT = 20

    def dma(self, eng, fn, stream, reads=(), writes=()):
        deps = self._deps(reads, writes)
        k = self.dma_rr.get(eng, 0)
        self.dma_rr[eng] = k + 1
        slot = (eng, k % self.NSLOT)
        n = self.dma_cnt.get(slot, 0) + 1
        self.dma_cnt[slot] = n
        if n > 1:
            deps.append(("d", slot, 16 * (n - 1)))
        self.ops[eng].append({"fn": fn, "deps": deps, "sig": False, "dma": slot})
        self._commit(("d", slot, 16 * n), reads, writes)

    def emit(self, final_waits=()):
        nc = self.nc
        for e in self.ENGS:
            for o in self.ops[e]:
                for d in o["deps"]:
                    if d[0] == "e":
                        if d[1] == e and e not in SAME_ENGINE_SYNC:
                            continue
                        self.ops[d[1]][d[2]]["sig"] = True
        sigval = {}
        nep = {}
        for e in self.ENGS:
            n = 0
            for i, o in enumerate(self.ops[e]):
                if o["sig"]:
                    sigval[(e, i)] = (n // EPOCH, n % EPOCH + 1)
                    n += 1
            nep[e] = (n + EPOCH - 1) // EPOCH
        sems = {}
        for e in self.ENGS:
            for k in range(nep[e]):
                sems[("e", e, k)] = nc.alloc_semaphore(name=f"{self.prefix}s_{e}_{k}")
        for s in self.dma_cnt:
            sems[("d", s)] = nc.alloc_semaphore(name=f"{self.prefix}d_{s[0]}_{s[1]}")
        self.sem_handles = list(sems.values())
        ops = self.ops
        dma_cnt = self.dma_cnt

        def run(e, eng):
            clock = {}
            for i, o in enumerate(ops[e]):
                need = {}
                for d in o["deps"]:
                    if d[0] == "e":
                        if d[1] == e and e not in SAME_ENGINE_SYNC:
                            continue
                        key = ("e", d[1])
                        val = sigval[(d[1], d[2])]
                    else:
                        key = ("d", d[1])
                        val = (0, d[2])
                    if clock.get(key, (-1, 0)) >= val:
                        continue
                    if need.get(key, (-1, 0)) < val:
                        need[key] = val
                for key, val in need.items():
                    clock[key] = val
                    if key[0] == "e":
                        eng.wait_ge(sems[("e", key[1], val[0])], val[1])
                    else:
                        eng.wait_ge(sems[("d", key[1])], val[1])
                ins = o["fn"](eng)
                if o["dma"] is not None:
                    ins.then_inc(sems[("d", o["dma"])], 16)
                elif o["sig"]:
                    ins.then_inc(sems[("e", e, sigval[(e, i)][0])], 1)
            if e == "sp":
                for s in dma_cnt:
                    if clock.get(("d", s), (-1, 0)) < (0, 16 * dma_cnt[s]):
                        eng.wait_ge(sems[("d", s)], 16 * dma_cnt[s])

        with nc.Block() as block:
            @block.tensor
            def _(eng):
                run("pe", eng)

            @block.scalar
            def _(eng):
                run("act", eng)

            @block.vector
            def _(eng):
                run("dve", eng)

            @block.gpsimd
            def _(eng):
                run("pool", eng)

            @block.sync
            def _(eng):
                run("sp", eng)
        self.stack.close()


class Conv:
    def __init__(self, P, width=1024, nbuf=3):
        self.P = P
        self.w = width
        self.nb = nbuf
        self.f = [P.sb(f"cvf{i}", [128, width], F32) for i in range(nbuf)]
        self.b = [P.sb(f"cvb{i}", [128, width], BF16) for i in range(nbuf)]
        self.i = 0

    def run(self, dst, src, tag):
        P = self.P
        n = src.shape[1]
        c0 = 0
        keys = P.dkeys.setdefault(tag, [])
        while c0 < n:
            w = min(self.w, n - c0)
            i = self.i % self.nb
            self.i += 1
            f, b = self.f[i], self.b[i]
            s_ap = src[:, c0:c0 + w]
            d_ap = dst[:, c0:c0 + w]
            P.dma("sp", lambda e, f=f, s_ap=s_ap, w=w: e.dma_start(out=f[:, :w], in_=s_ap),
                  "cvl", writes=[("cvf", i)])
            eng = ("dve", "pool", "act")[self.i % 3]
            if eng == "act":
                P.op("act", lambda e, f=f, b=b, w=w: e.copy(out=b[:, :w], in_=f[:, :w]),
                     reads=[("cvf", i)], writes=[("cvb", i)])
            else:
                P.op(eng, lambda e, f=f, b=b, w=w: e.tensor_copy(out=b[:, :w], in_=f[:, :w]),
                     reads=[("cvf", i)], writes=[("cvb", i)])
            key = ("dram", tag, len(keys))
            keys.append(key)
            P.dma("sp", lambda e, b=b, d_ap=d_ap, w=w: e.dma_start(out=d_ap, in_=b[:, :w]),
                  "cvs", reads=[("cvb", i)], writes=[key])
            c0 += w


def rms_rstd(P, S, src_chunks, src_res, nk, out_rstd, out_res, dim):
    for k in range(nk):
        j = S.sqi % 2
        S.sqi += 1
        sq = S.sq[j]
        src = src_chunks[k]
        P.op("act", lambda e, sq=sq, src=src: e.activation(out=sq[:], in_=src, func=AF.Square),
             reads=[src_res[k]], writes=[("sq", j)])
        P.op("pe", lambda e, sq=sq, k=k: e.matmul(S.ps_ss[:], lhsT=S.ones[:], rhs=sq[:],
                                                  start=(k == 0), stop=(k == nk - 1)),
             reads=[("sq", j)], writes=[("ps_ss",)])
    P.op("act", lambda e: e.activation(out=S.lnt[:], in_=S.ps_ss[:], func=AF.Ln,
                                       bias=S.epsb[:], scale=1.0 / dim),
         reads=[("ps_ss",)], writes=[("lnt",)])
    P.op("act", lambda e: e.activation(out=out_rstd[:], in_=S.lnt[:], func=AF.Exp, scale=-0.5),
         reads=[("lnt",)], writes=[out_res])


class TokState:
    pass


def build_tok(n_tiles, mix_kc, n_ffn, emit_hm, P=None):
    if P is None:
        P = Prog(bass.Bass("TRN2", target_bir_lowering=False))
    nc = P.nc
    NT = n_tiles * TT
    n_g = (1 if mix_kc else 0) + 2 * n_ffn + (1 if emit_hm else 0)
    xT = P.dram("xT", [D, NT], F32, "ExternalInput")
    g_all = P.dram("g_all", [128, n_g * KD], F32, "ExternalInput")
    xo = P.dram("xo", [D, NT], F32, "ExternalOutput")
    if emit_hm:
        hm_o = P.dram("hm", [D, NT], BF16, "ExternalOutput")
    if mix_kc:
        oT = P.dram("oT", [mix_kc * 128, NT], BF16, "ExternalInput")
        wo_f = P.dram("wout", [KD, 128, mix_kc * 128], F32, "ExternalInput")
        wo_b = P.dram("wout_b", [KD, 128, mix_kc * 128], BF16)
    wg_f, wu_f, wd_f, wg_b, wu_b, wd_b = [], [], [], [], [], []
    for i in range(n_ffn):
        wg_f.append(P.dram(f"wg{i}", [KF, 128, KD * 128], F32, "ExternalInput"))
        wu_f.append(P.dram(f"wu{i}", [KF, 128, KD * 128], F32, "ExternalInput"))
        wd_f.append(P.dram(f"wd{i}", [KD, 128, KF * 128], F32, "ExternalInput"))
        wg_b.append(P.dram(f"wg{i}_b", [KF, 128, KD * 128], BF16))
        wu_b.append(P.dram(f"wu{i}_b", [KF, 128, KD * 128], BF16))
        wd_b.append(P.dram(f"wd{i}_b", [KD, 128, KF * 128], BF16))

    S = TokState()
    S.sqi = 0
    S.ones = P.sb("ones", [128, 128], F32)
    S.epsb = P.sb("epsb", [128, 1], F32)
    S.g = P.sb("sb_g", [128, n_g * KD], F32)
    S.x = P.sb("sb_x", [128, KD, TT], F32)
    S.xn = P.sb("sb_xn", [128, KD, TT], BF16)
    S.y = P.sb("sb_y", [128, KD, TT], F32)
    S.h = P.sb("sb_h", [128, KF, TT], BF16)
    S.sq = [P.sb(f"sq{i}", [128, TT], F32) for i in range(2)]
    S.lnt = P.sb("lnt", [128, TT], F32)
    S.rstd = P.sb("rstd", [128, TT], F32)
    S.sg = [P.sb(f"sg{i}", [128, TT], F32) for i in range(2)]
    S.wgu = [P.sb(f"sb_wgu{i}", [128, 2, KD * 128], BF16) for i in range(2)]
    S.wd = [P.sb(f"sb_wd{i}", [128, KF * 128], BF16) for i in range(2)]
    S.ps_ss = P.ps("ps_ss", [128, TT])
    S.ps_g = [P.ps(f"ps_g{i}", [128, TT]) for i in range(2)]
    S.ps_u = [P.ps(f"ps_u{i}", [128, TT]) for i in range(2)]
    S.ps_y = [P.ps(f"ps_y{i}", [128, TT]) for i in range(2)]
    cv = Conv(P)

    P.op("dve", lambda e: e.memset(S.ones[:], 1.0), writes=[("ones",)])
    P.op("dve", lambda e: e.memset(S.epsb[:], EPS), writes=[("epsb",)])
    P.dma("pool", lambda e: e.dma_start(out=S.g[:], in_=g_all), "gl", writes=[("g",)])
    if mix_kc:
        for o in range(KD):
            cv.run(wo_b[o], wo_f[o], "wo")
    for i in range(n_ffn):
        for f in range(KF):
            cv.run(wg_b[i][f], wg_f[i][f], f"wg{i}")
            cv.run(wu_b[i][f], wu_f[i][f], f"wu{i}")
        for o in range(KD):
            cv.run(wd_b[i][o], wd_f[i][o], f"wd{i}")

    cnt = {"wgu": 0, "wd": 0, "psg": 0, "psy": 0}

    def proj_norm_res(src, src_res, nk, w_b, w_tag, gcol, coef):
        for o in range(KD):
            j = cnt["wd"] % 2
            cnt["wd"] += 1
            wt = S.wd[j]
            P.dma("sp", lambda e, wt=wt, o=o: e.dma_start(out=wt[:, :nk * 128], in_=w_b[o]),
                  f"wd{j}", reads=P.dkeys[w_tag], writes=[("wd", j)])
            pj = cnt["psy"] % 2
            cnt["psy"] += 1
            py = S.ps_y[pj]
            for k in range(nk):
                P.op("pe", lambda e, py=py, wt=wt, k=k: e.matmul(
                    py[:], lhsT=wt[:, k * 128:(k + 1) * 128], rhs=src[:, k, :],
                    start=(k == 0), stop=(k == nk - 1)),
                    reads=[("wd", j), src_res], writes=[("psy", pj)])
            P.op("act", lambda e, py=py, o=o: e.copy(out=S.y[:, o, :], in_=py[:]),
                 reads=[("psy", pj)], writes=[("y", o)])
        rms_rstd(P, S, [S.y[:, k, :] for k in range(KD)], [("y", k) for k in range(KD)], KD,
                 S.rstd, ("rstd",), D)
        for k in range(KD):
            P.op("dve", lambda e, k=k: e.scalar_tensor_tensor(
                out=S.y[:, k, :], in0=S.y[:, k, :], scalar=S.g[:, gcol * KD + k:gcol * KD + k + 1],
                in1=S.rstd[:], op0=ALU.mult, op1=ALU.mult),
                reads=[("y", k), ("rstd",), ("g",)], writes=[("y", k)])
            P.op("dve", lambda e, k=k: e.scalar_tensor_tensor(
                out=S.x[:, k, :], in0=S.y[:, k, :], scalar=float(coef),
                in1=S.x[:, k, :], op0=ALU.mult, op1=ALU.add),
                reads=[("y", k), ("x", k)], writes=[("x", k)])

    def norm_to(dst, dst_res, gcol):
        rms_rstd(P, S, [S.x[:, k, :] for k in range(KD)], [("x", k) for k in range(KD)], KD,
                 S.rstd, ("rstd",), D)
        for k in range(KD):
            P.op("dve", lambda e, k=k: e.scalar_tensor_tensor(
                out=dst[:, k, :], in0=S.x[:, k, :], scalar=S.g[:, gcol * KD + k:gcol * KD + k + 1],
                in1=S.rstd[:], op0=ALU.mult, op1=ALU.mult),
                reads=[("x", k), ("rstd",), ("g",)], writes=[dst_res])

    def ffn(i, gcol):
        norm_to(S.xn, ("xn",), gcol)
        for f in range(KF):
            j = cnt["wgu"] % 2
            cnt["wgu"] += 1
            wt = S.wgu[j]
            P.dma("sp", lambda e, wt=wt, f=f: e.dma_start(out=wt[:, 0, :], in_=wg_b[i][f]),
                  f"wgu{j}", reads=P.dkeys[f"wg{i}"], writes=[("wgu", j)])
            P.dma("sp", lambda e, wt=wt, f=f: e.dma_start(out=wt[:, 1, :], in_=wu_b[i][f]),
                  f"wgu{j}", reads=P.dkeys[f"wu{i}"], writes=[("wgu", j)])
            pj = cnt["psg"] % 2
            cnt["psg"] += 1
            pg, pu, sg = S.ps_g[pj], S.ps_u[pj], S.sg[pj]
            for k in range(KD):
                P.op("pe", lambda e, pg=pg, wt=wt, k=k: e.matmul(
                    pg[:], lhsT=wt[:, 0, k * 128:(k + 1) * 128], rhs=S.xn[:, k, :],
                    start=(k == 0), stop=(k == KD - 1)),
                    reads=[("wgu", j), ("xn",)], writes=[("psg", pj)])
            for k in range(KD):
                P.op("pe", lambda e, pu=pu, wt=wt, k=k: e.matmul(
                    pu[:], lhsT=wt[:, 1, k * 128:(k + 1) * 128], rhs=S.xn[:, k, :],
                    start=(k == 0), stop=(k == KD - 1)),
                    reads=[("wgu", j), ("xn",)], writes=[("psu", pj)])
            P.op("act", lambda e, pg=pg, sg=sg: e.activation(out=sg[:], in_=pg[:], func=AF.Silu),
                 reads=[("psg", pj)], writes=[("sg", pj)])
            P.op("dve", lambda e, pu=pu, sg=sg, f=f: e.tensor_tensor(
                out=S.h[:, f, :], in0=sg[:], in1=pu[:], op=ALU.mult),
                reads=[("sg", pj), ("psu", pj)], writes=[("h",)])
        proj_norm_res(S.h, ("h",), KF, wd_b[i], f"wd{i}", gcol + 1, 0.5)

    for t in range(n_tiles):
        ts = slice(t * TT, (t + 1) * TT)
        for k in range(KD):
            P.dma("pool", lambda e, k=k, ts=ts: e.dma_start(out=S.x[:, k, :], in_=xT[k * 128:(k + 1) * 128, ts]),
                  "xl", writes=[("x", k)])
        gc = 0
        if mix_kc:
            src = S.h
            for k in range(mix_kc):
                P.dma("pool", lambda e, k=k, ts=ts: e.dma_start(out=S.h[:, k, :], in_=oT[k * 128:(k + 1) * 128, ts]),
                      "ol", writes=[("h",)])
            proj_norm_res(S.h, ("h",), mix_kc, wo_b, "wo", gc, 1.0)
            gc += 1
        for i in range(n_ffn):
            ffn(i, gc)
            gc += 2
        if emit_hm:
            norm_to(S.xn, ("xn",), gc)
            for k in range(KD):
                P.dma("pool", lambda e, k=k, ts=ts: e.dma_start(out=hm_o[k * 128:(k + 1) * 128, ts], in_=S.xn[:, k, :]),
                      "hs", reads=[("xn",)], writes=[("dram", "hm")])
        for k in range(KD):
            P.dma("pool", lambda e, k=k, ts=ts: e.dma_start(out=xo[k * 128:(k + 1) * 128, ts], in_=S.x[:, k, :]),
                  "xs", reads=[("x", k)], writes=[("dram", "xo")])
    P.emit()
    return nc


def tile_w_in_out(W, kc_in, kc_out):
    return np.ascontiguousarray(
        W.reshape(kc_in, 128, kc_out, 128).transpose(2, 1, 0, 3).reshape(kc_out, 128, kc_in * 128))


def gain_cols(g):
    return np.ascontiguousarray(g.reshape(-1, 128).T)


RH = 4
RDK = 256
RDV = 512


def build_ret(S_len, P=None):
    if P is None:
        P = Prog(bass.Bass("TRN2", target_bir_lowering=False))
    nc = P.nc
    n_tiles = S_len // TT
    hmT = P.dram("hmT", [D, S_len], BF16, "ExternalInput")
    wq_f = P.dram("wq", [2 * RH, 128, D], F32, "ExternalInput")
    wk_f = P.dram("wk", [2 * RH, 128, D], F32, "ExternalInput")
    wv_f = P.dram("wv", [RH, 128, KD * RDV], F32, "ExternalInput")
    wg_f = P.dram("wgt", [RH, 128, KD * RDV], F32, "ExternalInput")
    wq_b = P.dram("wq_b", [2 * RH, 128, D], BF16)
    wk_b = P.dram("wk_b", [2 * RH, 128, D], BF16)
    wv_b = P.dram("wv_b", [RH, 128, KD * RDV], BF16)
    wg_b = P.dram("wg_b", [RH, 128, KD * RDV], BF16)
    cosq = P.dram("cosq", [128, S_len], F32, "ExternalInput")
    sinq = P.dram("sinq", [128, S_len], F32, "ExternalInput")
    cosk = P.dram("cosk", [128, S_len], F32, "ExternalInput")
    sink = P.dram("sink", [128, S_len], F32, "ExternalInput")
    c_inner = P.dram("c_inner", [128, RH * 128], F32, "ExternalInput")
    c_cross = P.dram("c_cross", [128, RH * 128], F32, "ExternalInput")
    c_misc = P.dram("c_misc", [128, RH * 2], F32, "ExternalInput")
    c_ident = P.dram("c_ident", [128, 128], BF16, "ExternalInput")
    oT = P.dram("oT", [RH * RDV, S_len], BF16, "ExternalOutput")

    hm = P.sb("r_hm", [128, KD, TT], BF16)
    wfm = [P.sb(f"r_wfm{i}", [128, D], BF16) for i in range(2)]
    wtm = [P.sb(f"r_wtm{i}", [128, KD * RDV], BF16) for i in range(2)]
    inner = P.sb("r_inner", [128, RH * 128], F32)
    cross = P.sb("r_cross", [128, RH * 128], F32)
    misc = P.sb("r_misc", [128, RH * 2], F32)
    ident = P.sb("r_ident", [128, 128], BF16)
    epsb = P.sb("r_epsb", [128, 1], F32)
    tab = P.sb("r_tab", [128, 4, TT], F32)
    raw = P.sb("r_raw", [128, 2, TT], F32)
    tmp = P.sb("r_tmp", [128, 2, TT], F32)
    qT = P.sb("r_qT", [128, RH, 2, TT], BF16)
    kT = P.sb("r_kT", [128, RH, 2, TT], BF16)
    v_sb = P.sb("r_v", [128, 4, RH, RDV], BF16)
    g_sb = P.sb("r_g", [128, 4, RH, RDV], BF16)
    ktl = P.sb("r_ktl", [128, RDK], BF16)
    attT = P.sb("r_attT", [128, 128], BF16)
    qs = P.sb("r_qs", [128, 2, 128], BF16)
    state = P.sb("r_state", [128, RH, 2, RDV], F32)
    state_b = P.sb("r_state_b", [128, RH, 2, RDV], BF16)
    stats = P.sb("r_stats", [128, 6], F32)
    mv = P.sb("r_mv", [128, 2], F32)
    rstd = P.sb("r_rstd", [128, 1], F32)
    yn = P.sb("r_yn", [128, RDV], F32)
    o_tm = P.sb("r_otm", [128, RDV], BF16)
    o_fm = P.sb("r_ofm", [128, RH * 4, TT], BF16)
    ps_a = [P.ps(f"r_psa{i}", [128, TT]) for i in range(2)]
    ps_y = P.ps("r_psy", [128, RDV])
    ps_st = [P.ps(f"r_psst{i}", [128, RDV]) for i in range(2)]
    ps_att = P.ps("r_psatt", [128, 128])
    ps_tr = [P.ps(f"r_pstr{i}", [128, 512], BF16) for i in range(2)]
    cv = Conv(P)

    P.op("dve", lambda e: e.memset(epsb[:], EPS), writes=[("epsb",)])
    P.op("dve", lambda e: e.memset(state[:], 0.0), writes=[("state", h, hf) for h in range(RH) for hf in range(2)])
    P.op("pool", lambda e: e.memset(state_b[:], 0.0), writes=[("state_b", h, hf) for h in range(RH) for hf in range(2)])
    P.dma("pool", lambda e: e.dma_start(out=inner[:], in_=c_inner), "cl", writes=[("inner",)])
    P.dma("pool", lambda e: e.dma_start(out=cross[:], in_=c_cross), "cl", writes=[("cross",)])
    P.dma("pool", lambda e: e.dma_start(out=misc[:], in_=c_misc), "cl", writes=[("misc",)])
    P.dma("pool", lambda e: e.dma_start(out=ident[:], in_=c_ident), "cl", writes=[("ident",)])
    for c in range(2 * RH):
        cv.run(wq_b[c], wq_f[c], "wq")
        cv.run(wk_b[c], wk_f[c], "wk")
    for h in range(RH):
        cv.run(wv_b[h], wv_f[h], "wv")
        cv.run(wg_b[h], wg_f[h], "wg")

    cnt = {"wfm": 0, "wtm": 0, "psa": 0, "pstr": 0, "psst": 0}

    for t in range(n_tiles):
        ts = slice(t * TT, (t + 1) * TT)
        for k in range(KD):
            P.dma("pool", lambda e, k=k, ts=ts: e.dma_start(out=hm[:, k, :], in_=hmT[k * 128:(k + 1) * 128, ts]),
                  "hl", writes=[("hm",)])
        for i, src in enumerate((cosq, sinq, cosk, sink)):
            P.dma("pool", lambda e, i=i, src=src, ts=ts: e.dma_start(out=tab[:, i, :], in_=src[:, ts]),
                  "tl", writes=[("tab",)])
        for which, w_b, dst, tag, ci, si in (("q", wq_b, qT, "wq", 0, 1), ("k", wk_b, kT, "wk", 2, 3)):
            for h in range(RH):
                for hf in range(2):
                    j = cnt["wfm"] % 2
                    cnt["wfm"] += 1
                    wt = wfm[j]
                    P.dma("sp", lambda e, wt=wt, w_b=w_b, c=2 * h + hf: e.dma_start(out=wt[:], in_=w_b[c]),
                          f"wfm{j}", reads=P.dkeys[tag], writes=[("wfm", j)])
                    pj = cnt["psa"] % 2
                    cnt["psa"] += 1
                    pa = ps_a[pj]
                    for k in range(KD):
                        P.op("pe", lambda e, pa=pa, wt=wt, k=k: e.matmul(
                            pa[:], lhsT=wt[:, k * 128:(k + 1) * 128], rhs=hm[:, k, :],
                            start=(k == 0), stop=(k == KD - 1)),
                            reads=[("wfm", j), ("hm",)], writes=[("psa", pj)])
                    P.op("act", lambda e, pa=pa, hf=hf: e.copy(out=raw[:, hf, :], in_=pa[:]),
                         reads=[("psa", pj)], writes=[("raw", hf)])
                P.op("dve", lambda e, ci=ci: e.tensor_tensor(out=tmp[:, 0, :], in0=raw[:, 0, :], in1=tab[:, ci, :], op=ALU.mult),
                     reads=[("raw", 0), ("tab",)], writes=[("tmp", 0)])
                P.op("dve", lambda e, si=si: e.tensor_tensor(out=tmp[:, 1, :], in0=raw[:, 1, :], in1=tab[:, si, :], op=ALU.mult),
                     reads=[("raw", 1), ("tab",)], writes=[("tmp", 1)])
                P.op("dve", lambda e, dst=dst, h=h: e.tensor_tensor(out=dst[:, h, 0, :], in0=tmp[:, 0, :], in1=tmp[:, 1, :], op=ALU.subtract),
                     reads=[("tmp", 0), ("tmp", 1)], writes=[(which, h)])
                P.op("dve", lambda e, ci=ci: e.tensor_tensor(out=tmp[:, 0, :], in0=raw[:, 1, :], in1=tab[:, ci, :], op=ALU.mult),
                     reads=[("raw", 1), ("tab",), (which, h)], writes=[("tmp", 0)])
                P.op("dve", lambda e, si=si: e.tensor_tensor(out=tmp[:, 1, :], in0=raw[:, 0, :], in1=tab[:, si, :], op=ALU.mult),
                     reads=[("raw", 0), ("tab",), (which, h)], writes=[("tmp", 1)])
                P.op("dve", lambda e, dst=dst, h=h: e.tensor_tensor(out=dst[:, h, 1, :], in0=tmp[:, 0, :], in1=tmp[:, 1, :], op=ALU.add),
                     reads=[("tmp", 0), ("tmp", 1)], writes=[(which, h)])
        for which, w_b, tag in (("v", wv_b, "wv"), ("g", wg_b, "wg")):
            for h in range(RH):
                j = cnt["wtm"] % 2
                cnt["wtm"] += 1
                wt = wtm[j]
                P.dma("sp", lambda e, wt=wt, w_b=w_b, h=h: e.dma_start(out=wt[:], in_=w_b[h]),
                      f"wtm{j}", reads=P.dkeys[tag], writes=[("wtm", j)])
                for sub in range(4):
                    pj = cnt["psa"] % 2
                    cnt["psa"] += 1
                    pa = ps_a[pj]
                    for k in range(KD):
                        P.op("pe", lambda e, pa=pa, wt=wt, k=k, sub=sub: e.matmul(
                            pa[:], lhsT=hm[:, k, sub * 128:(sub + 1) * 128], rhs=wt[:, k * RDV:(k + 1) * RDV],
                            start=(k == 0), stop=(k == KD - 1)),
                            reads=[("wtm", j), ("hm",)], writes=[("psa", pj)])
                    if which == "v":
                        P.op("act", lambda e, pa=pa, sub=sub, h=h: e.copy(out=v_sb[:, sub, h, :], in_=pa[:]),
                             reads=[("psa", pj)], writes=[("v", sub, h)])
                    else:
                        P.op("act", lambda e, pa=pa, sub=sub, h=h: e.activation(out=g_sb[:, sub, h, :], in_=pa[:], func=AF.Silu),
                             reads=[("psa", pj)], writes=[("g", sub, h)])
        for sub in range(4):
            cs = slice(sub * 128, (sub + 1) * 128)
            for h in range(RH):
                pj = cnt["pstr"] % 2
                cnt["pstr"] += 1
                ptr = ps_tr[pj]
                for hf in range(2):
                    P.op("pe", lambda e, ptr=ptr, h=h, hf=hf, cs=cs: e.transpose(
                        ptr[:, hf * 128:(hf + 1) * 128], kT[:, h, hf, cs], ident[:]),
                        reads=[("k", h), ("ident",)], writes=[("pstr", pj)])
                P.op("dve", lambda e, ptr=ptr, h=h: e.tensor_scalar(
                    out=ktl[:], in0=ptr[:, 0:RDK], scalar1=misc[:, 2 * h:2 * h + 1], scalar2=None, op0=ALU.mult),
                    reads=[("pstr", pj), ("misc",)], writes=[("ktl",)])
                for hf in range(2):
                    P.op("pe", lambda e, h=h, hf=hf, cs=cs: e.matmul(
                        ps_att[:], lhsT=kT[:, h, hf, cs], rhs=qT[:, h, hf, cs], start=(hf == 0), stop=(hf == 1)),
                        reads=[("k", h), ("q", h)], writes=[("psatt",)])
                P.op("dve", lambda e, h=h: e.tensor_tensor(
                    out=attT[:], in0=ps_att[:], in1=inner[:, h * 128:(h + 1) * 128], op=ALU.mult),
                    reads=[("psatt",), ("inner",)], writes=[("attT",)])
                for hf in range(2):
                    P.op("pool", lambda e, h=h, hf=hf, cs=cs: e.tensor_tensor(
                        out=qs[:, hf, :], in0=qT[:, h, hf, cs], in1=cross[:, h * 128:(h + 1) * 128], op=ALU.mult),
                        reads=[("q", h), ("cross",)], writes=[("qs", hf)])
                P.op("pe", lambda e, sub=sub, h=h: e.matmul(
                    ps_y[:], lhsT=attT[:], rhs=v_sb[:, sub, h, :], start=True, stop=False),
                    reads=[("attT",), ("v", sub, h)], writes=[("psy",)])
                for hf in range(2):
                    P.op("pe", lambda e, h=h, hf=hf: e.matmul(
                        ps_y[:], lhsT=qs[:, hf, :], rhs=state_b[:, h, hf, :], start=False, stop=(hf == 1)),
                        reads=[("qs", hf), ("state_b", h, hf)], writes=[("psy",)])
                for hf in range(2):
                    sj = cnt["psst"] % 2
                    cnt["psst"] += 1
                    pst = ps_st[sj]
                    P.op("pe", lambda e, pst=pst, sub=sub, h=h, hf=hf: e.matmul(
                        pst[:], lhsT=ktl[:, hf * 128:(hf + 1) * 128], rhs=v_sb[:, sub, h, :], start=True, stop=True),
                        reads=[("ktl",), ("v", sub, h)], writes=[("psst", sj)])
                    P.op("dve", lambda e, pst=pst, h=h, hf=hf: e.scalar_tensor_tensor(
                        out=state[:, h, hf, :], in0=state[:, h, hf, :], scalar=misc[:, 2 * h + 1:2 * h + 2],
                        in1=pst[:], op0=ALU.mult, op1=ALU.add),
                        reads=[("psst", sj), ("state", h, hf), ("misc",)], writes=[("state", h, hf)])
                    P.op("act", lambda e, h=h, hf=hf: e.copy(out=state_b[:, h, hf, :], in_=state[:, h, hf, :]),
                         reads=[("state", h, hf)], writes=[("state_b", h, hf)])
                P.op("dve", lambda e: e.bn_stats(out=stats[:], in_=ps_y[:]),
                     reads=[("psy",)], writes=[("stats",)])
                P.op("dve", lambda e: e.bn_aggr(out=mv[:], in_=stats[:]),
                     reads=[("stats",)], writes=[("mv",)])
                P.op("act", lambda e: e.activation(out=rstd[:], in_=mv[:, 1:2], func=AF.Ln, bias=epsb[:], scale=1.0),
                     reads=[("mv",), ("epsb",)], writes=[("rstd",)])
                P.op("act", lambda e: e.activation(out=rstd[:], in_=rstd[:], func=AF.Exp, scale=-0.5),
                     reads=[("rstd",)], writes=[("rstd",)])
                P.op("dve", lambda e: e.tensor_scalar(
                    out=yn[:], in0=ps_y[:], scalar1=mv[:, 0:1], scalar2=rstd[:, 0:1], op0=ALU.subtract, op1=ALU.mult),
                    reads=[("psy",), ("mv",), ("rstd",)], writes=[("yn",)])
                P.op("pool", lambda e, sub=sub, h=h: e.tensor_tensor(
                    out=o_tm[:], in0=yn[:], in1=g_sb[:, sub, h, :], op=ALU.mult),
                    reads=[("yn",), ("g", sub, h)], writes=[("otm",)])
                pj = cnt["pstr"] % 2
                cnt["pstr"] += 1
                ptr = ps_tr[pj]
                for c in range(4):
                    P.op("pe", lambda e, ptr=ptr, c=c: e.transpose(
                        ptr[:, c * 128:(c + 1) * 128], o_tm[:, c * 128:(c + 1) * 128], ident[:]),
                        reads=[("otm",), ("ident",)], writes=[("pstr", pj)])
                for c in range(4):
                    P.op("act", lambda e, ptr=ptr, c=c, h=h, cs=cs: e.copy(
                        out=o_fm[:, h * 4 + c, cs], in_=ptr[:, c * 128:(c + 1) * 128]),
                        reads=[("pstr", pj)], writes=[("ofm",)])
        for c in range(RH * 4):
            P.dma("pool", lambda e, c=c, ts=ts: e.dma_start(out=oT[c * 128:(c + 1) * 128, ts], in_=o_fm[:, c, :]),
                  "os", reads=[("ofm",)], writes=[("dram", "oT")])
    P.emit()
    return nc


def ret_consts(h0):
    L = 128
    idx = np.arange(L, dtype=np.float32)
    lg = np.log1p(-np.exp2(-5.0 - np.arange(8, dtype=np.float32))).astype(np.float32)
    inner = np.zeros((128, RH * 128), np.float32)
    cross = np.zeros((128, RH * 128), np.float32)
    misc = np.zeros((128, RH * 2), np.float32)
    for h in range(RH):
        g = lg[h0 + h]
        diff = idx[:, None] - idx[None, :]
        dec = np.where(diff >= 0, np.exp(np.maximum(diff, 0.0) * g), 0.0).astype(np.float32)
        inner[:, h * 128:(h + 1) * 128] = dec.T
        cross[:, h * 128:(h + 1) * 128] = np.exp((idx + 1.0) * g)[None, :]
        misc[:, 2 * h] = np.exp((L - 1.0 - idx) * g)
        misc[:, 2 * h + 1] = np.exp(L * g)
    return inner, cross, misc


def rope_tables_T(S_len, dim, scale):
    inv = (1.0 / (10000.0 ** (np.arange(0, dim, 2, dtype=np.float32) / np.float32(dim)))).astype(np.float32)
    ang = np.arange(S_len, dtype=np.float32)[:, None] * inv[None, :]
    return (np.ascontiguousarray(np.cos(ang).T.astype(np.float32)) * np.float32(scale),
            np.ascontiguousarray(np.sin(ang).T.astype(np.float32)) * np.float32(scale))


def tile_w_fm(W):
    n = W.shape[1] // 128
    return tile_w_in_out(W, KD, n)


def tile_w_tm(W, ncol):
    g = W.shape[1] // ncol
    return np.ascontiguousarray(W.reshape(KD, 128, g, ncol).transpose(2, 1, 0, 3).reshape(g, 128, KD * ncol))


def ret_inputs(hm_bf_T, od_w_in, h0, S_len):
    import ml_dtypes
    W = od_w_in
    q0, k0, v0, g0 = 0, 2048, 4096, 8192
    wq = W[:, q0 + h0 * RDK: q0 + (h0 + RH) * RDK]
    wk = W[:, k0 + h0 * RDK: k0 + (h0 + RH) * RDK]
    wv = W[:, v0 + h0 * RDV: v0 + (h0 + RH) * RDV]
    wg = W[:, g0 + h0 * RDV: g0 + (h0 + RH) * RDV]
    cq, sq = rope_tables_T(S_len, RDK, 1.0)
    ck, sk = rope_tables_T(S_len, RDK, RDK ** -0.5)
    inner, cross, misc = ret_consts(h0)
    return {"hmT": hm_bf_T, "wq": tile_w_fm(wq), "wk": tile_w_fm(wk), "wv": tile_w_tm(wv, RDV),
            "wgt": tile_w_tm(wg, RDV), "cosq": cq, "sinq": sq, "cosk": ck, "sink": sk,
            "c_inner": inner, "c_cross": cross, "c_misc": misc,
            "c_ident": np.eye(128, dtype=np.float32).astype(ml_dtypes.bfloat16)}


SH = 8
NEG = -30000.0


def build_ssd(S_len, dbg=99, P=None):
    if P is None:
        P = Prog(bass.Bass("TRN2", target_bir_lowering=False))
    nc = P.nc
    n_tiles = S_len // TT
    hmT = P.dram("hmT", [D, S_len], BF16, "ExternalInput")
    wfm_f = P.dram("wfm", [8, 128, D], F32, "ExternalInput")
    wz_f = P.dram("wz", [1, 128, KD * 512], F32, "ExternalInput")
    wdt_f = P.dram("wdt", [1, 128, KD * 8], F32, "ExternalInput")
    wfm_b = P.dram("wfm_b", [8, 128, D], BF16)
    wz_b = P.dram("wz_b", [1, 128, KD * 512], BF16)
    wdt_b = P.dram("wdt_b", [1, 128, KD * 8], BF16)
    c_conv = P.dram("c_conv", [128, 8 * 5], F32, "ExternalInput")
    c_hp = P.dram("c_hp", [128, 3 * 8], F32, "ExternalInput")
    c_nw = P.dram("c_nw", [128, 512], F32, "ExternalInput")
    c_U = P.dram("c_U", [128, 128], F32, "ExternalInput")
    c_nm = P.dram("c_nm", [128, 128], F32, "ExternalInput")
    c_idf = P.dram("c_idf", [128, 128], F32, "ExternalInput")
    c_ident = P.dram("c_ident", [128, 128], BF16, "ExternalInput")
    oT = P.dram("oT", [512, S_len], BF16, "ExternalOutput")

    hm = P.sb("s_hm", [128, KD, TT], BF16)
    wfm = [P.sb(f"s_wfm{i}", [128, D], BF16) for i in range(2)]
    wz = P.sb("s_wz", [128, KD * 512], BF16)
    wdt = P.sb("s_wdt", [128, KD * 8], BF16)
    conv = P.sb("s_conv", [128, 40], F32)
    hp = P.sb("s_hp", [128, 24], F32)
    nw = P.sb("s_nw", [128, 512], F32)
    U = P.sb("s_U", [128, 128], F32)
    nm = P.sb("s_nm", [128, 128], F32)
    idf = P.sb("s_idf", [128, 128], F32)
    ident = P.sb("s_ident", [128, 128], BF16)
    ones = P.sb("s_ones", [128, 128], F32)
    epsb = P.sb("s_epsb", [128, 1], F32)
    oneb = P.sb("s_oneb", [128, 1], F32)
    aneg = P.sb("s_aneg", [128, 8], F32)
    xc = P.sb("s_xc", [128, 8, TT + 3], F32)
    acc = P.sb("s_acc", [128, TT], F32)
    xact = P.sb("s_xact", [128, 8, TT], BF16)
    zs = P.sb("s_zs", [128, 4, 512], F32)
    dt = P.sb("s_dt", [128, 4, 8], F32)
    la = P.sb("s_la", [128, 4, 8], F32)
    cum = P.sb("s_cum", [128, 8], F32)
    ncum = P.sb("s_ncum", [128, 8], F32)
    ecum = P.sb("s_ecum", [128, 8], F32)
    cbT = P.sb("s_cbT", [128, 2, 128], F32)
    Btm = P.sb("s_Btm", [128, 2, 128], BF16)
    xs_tm = P.sb("s_xstm", [128, 512], BF16)
    LAb = [P.sb(f"s_LAb{i}", [128, 128], F32) for i in range(2)]
    decT = [P.sb(f"s_decT{i}", [128, 128], F32) for i in range(2)]
    MT = [P.sb(f"s_MT{i}", [128, 128], BF16) for i in range(2)]
    xd = [P.sb(f"s_xd{i}", [128, 64], BF16) for i in range(2)]
    wxd = [P.sb(f"s_wxd{i}", [128, 64], BF16) for i in range(2)]
    ecl = [P.sb(f"s_ecl{i}", [128, 1], F32) for i in range(2)]
    ysb = [P.sb(f"s_ysb{i}", [128, 64], F32) for i in range(2)]
    state = P.sb("s_state", [128, SH, 64], F32)
    state_b = P.sb("s_state_b", [128, SH, 64], BF16)
    y_all = P.sb("s_yall", [128, 512], F32)
    sqj = P.sb("s_sqj", [128, 256], F32)
    ss = P.sb("s_ss", [128, 2], F32)
    o_tm = P.sb("s_otm", [128, 512], BF16)
    o_fm = P.sb("s_ofm", [128, 4, TT], BF16)
    ps_a = [P.ps(f"s_psa{i}", [128, TT]) for i in range(2)]
    ps_seg = [P.ps(f"s_psseg{i}", [128, 128]) for i in range(2)]
    ps_cb = P.ps("s_pscb", [128, 128])
    ps_y = [P.ps(f"s_psy{i}", [128, 192]) for i in range(2)]
    ps_tr = P.ps("s_pstr", [128, 512], BF16)
    cv = Conv(P)

    P.op("dve", lambda e: e.memset(epsb[:], EPS), writes=[("epsb",)])
    P.op("dve", lambda e: e.memset(oneb[:], 1.0), writes=[("oneb",)])
    P.op("dve", lambda e: e.memset(ones[:], 1.0), writes=[("ones",)])
    P.op("dve", lambda e: e.memset(state[:], 0.0), writes=[("state", h) for h in range(SH)])
    P.op("pool", lambda e: e.memset(state_b[:], 0.0), writes=[("state_b", h) for h in range(SH)])
    P.op("pool", lambda e: e.memset(xc[:], 0.0), writes=[("xc", c) for c in range(8)])
    for dst, src, tag in ((conv, c_conv, "conv"), (hp, c_hp, "hp"), (nw, c_nw, "nw"), (U, c_U, "U"),
                          (nm, c_nm, "nm"), (idf, c_idf, "idf"), (ident, c_ident, "ident")):
        P.dma("pool", lambda e, dst=dst, src=src: e.dma_start(out=dst[:], in_=src), "cl", writes=[(tag,)])
    P.op("act", lambda e: e.activation(out=aneg[:], in_=hp[:, 8:16], func=AF.Exp), reads=[("hp",)], writes=[("aneg",)])
    P.op("dve", lambda e: e.tensor_scalar(out=aneg[:], in0=aneg[:], scalar1=-1.0, scalar2=None, op0=ALU.mult),
         reads=[("aneg",)], writes=[("aneg",)])
    for c in range(8):
        cv.run(wfm_b[c], wfm_f[c], "wfm")
    cv.run(wz_b[0], wz_f[0], "wz")
    cv.run(wdt_b[0], wdt_f[0], "wdt")
    P.dma("sp", lambda e: e.dma_start(out=wz[:], in_=wz_b[0]), "wl", reads=P.dkeys["wz"], writes=[("wz",)])
    P.dma("sp", lambda e: e.dma_start(out=wdt[:], in_=wdt_b[0]), "wl", reads=P.dkeys["wdt"], writes=[("wdt",)])

    cnt = {"wfm": 0, "psa": 0, "hh": 0}

    for t in range(n_tiles):
        ts = slice(t * TT, (t + 1) * TT)
        for k in range(KD):
            P.dma("pool", lambda e, k=k, ts=ts: e.dma_start(out=hm[:, k, :], in_=hmT[k * 128:(k + 1) * 128, ts]),
                  "hl", writes=[("hm",)])
        for c in range(8):
            j = cnt["wfm"] % 2
            cnt["wfm"] += 1
            wt = wfm[j]
            P.dma("sp", lambda e, wt=wt, c=c: e.dma_start(out=wt[:], in_=wfm_b[c]),
                  "wl", reads=P.dkeys["wfm"], writes=[("wfm", j)])
            pj = cnt["psa"] % 2
            cnt["psa"] += 1
            pa = ps_a[pj]
            for k in range(KD):
                P.op("pe", lambda e, pa=pa, wt=wt, k=k: e.matmul(
                    pa[:], lhsT=wt[:, k * 128:(k + 1) * 128], rhs=hm[:, k, :], start=(k == 0), stop=(k == KD - 1)),
                    reads=[("wfm", j), ("hm",)], writes=[("psa", pj)])
            P.op("act", lambda e, pa=pa, c=c: e.copy(out=xc[:, c, 3:], in_=pa[:]),
                 reads=[("psa", pj)], writes=[("xc", c)])
            P.op("dve", lambda e, c=c: e.tensor_scalar(
                out=acc[:], in0=xc[:, c, 3:TT + 3], scalar1=conv[:, 5 * c + 3:5 * c + 4], scalar2=conv[:, 5 * c + 4:5 * c + 5],
                op0=ALU.mult, op1=ALU.add), reads=[("xc", c), ("conv",)], writes=[("acc",)])
            for jj in range(3):
                P.op("dve", lambda e, c=c, jj=jj: e.scalar_tensor_tensor(
                    out=acc[:], in0=xc[:, c, jj:TT + jj], scalar=conv[:, 5 * c + jj:5 * c + jj + 1], in1=acc[:],
                    op0=ALU.mult, op1=ALU.add), reads=[("xc", c), ("conv",), ("acc",)], writes=[("acc",)])
            P.op("act", lambda e, c=c: e.activation(out=xact[:, c, :], in_=acc[:], func=AF.Silu),
                 reads=[("acc",)], writes=[("xact", c)])
            P.op("pool", lambda e, c=c: e.tensor_copy(out=xc[:, c, 0:3], in_=xc[:, c, TT:TT + 3]),
                 reads=[("xc", c)], writes=[("xc", c)])
        for sub in range(4 if dbg >= 2 else 0):
            ssl = slice(sub * 128, (sub + 1) * 128)
            pj = cnt["psa"] % 2
            cnt["psa"] += 1
            pa = ps_a[pj]
            for k in range(KD):
                P.op("pe", lambda e, pa=pa, k=k, ssl=ssl: e.matmul(
                    pa[:], lhsT=hm[:, k, ssl], rhs=wz[:, k * 512:(k + 1) * 512], start=(k == 0), stop=(k == KD - 1)),
                    reads=[("wz",), ("hm",)], writes=[("psa", pj)])
            P.op("act", lambda e, pa=pa, sub=sub: e.activation(out=zs[:, sub, :], in_=pa[:], func=AF.Silu),
                 reads=[("psa", pj)], writes=[("zs", sub)])
            pj = cnt["psa"] % 2
            cnt["psa"] += 1
            pa = ps_a[pj]
            for k in range(KD):
                P.op("pe", lambda e, pa=pa, k=k, ssl=ssl: e.matmul(
                    pa[:, 0:8], lhsT=hm[:, k, ssl], rhs=wdt[:, k * 8:(k + 1) * 8], start=(k == 0), stop=(k == KD - 1)),
                    reads=[("wdt",), ("hm",)], writes=[("psa", pj)])
            P.op("dve", lambda e, pa=pa, sub=sub: e.tensor_tensor(out=dt[:, sub, :], in0=pa[:, 0:8], in1=hp[:, 0:8], op=ALU.add),
                 reads=[("psa", pj), ("hp",)], writes=[("dt", sub)])
            P.op("act", lambda e, sub=sub: e.activation(out=dt[:, sub, :], in_=dt[:, sub, :], func=AF.Exp),
                 reads=[("dt", sub)], writes=[("dt", sub)])
            P.op("act", lambda e, sub=sub: e.activation(out=dt[:, sub, :], in_=dt[:, sub, :], func=AF.Ln, bias=oneb[:], scale=1.0),
                 reads=[("dt", sub), ("oneb",)], writes=[("dt", sub)])
            P.op("dve", lambda e, sub=sub: e.tensor_tensor(out=la[:, sub, :], in0=dt[:, sub, :], in1=aneg[:], op=ALU.mult),
                 reads=[("dt", sub), ("aneg",)], writes=[("la", sub)])
        for sub in range(4 if dbg >= 3 else 0):
            cs = slice(sub * 128, (sub + 1) * 128)
            P.op("pe", lambda e, sub=sub: e.matmul(ps_cb[:, 0:8], lhsT=U[:], rhs=la[:, sub, :], start=True, stop=True),
                 reads=[("U",), ("la", sub)], writes=[("pscb",)])
            P.op("dve", lambda e: e.tensor_copy(out=cum[:], in_=ps_cb[:, 0:8]), reads=[("pscb",)], writes=[("cum",)])
            P.op("dve", lambda e: e.tensor_scalar(out=ncum[:], in0=cum[:], scalar1=-1.0, scalar2=None, op0=ALU.mult),
                 reads=[("cum",)], writes=[("ncum",)])
            P.op("act", lambda e: e.activation(out=ecum[:], in_=cum[:], func=AF.Exp), reads=[("cum",)], writes=[("ecum",)])
            if dbg < 3.2:
                continue
            for g in range(2):
                P.op("pe", lambda e, g=g, cs=cs: e.matmul(
                    ps_cb[:], lhsT=xact[:, 4 + g, cs], rhs=xact[:, 6 + g, cs], start=True, stop=True),
                    reads=[("xact", 4 + g), ("xact", 6 + g), ("cum",), ("ncum",), ("ecum",)], writes=[("pscb",)])
                P.op("act", lambda e, g=g: e.copy(out=cbT[:, g, :], in_=ps_cb[:]), reads=[("pscb",)], writes=[("cbT", g)])
                P.op("pe", lambda e, g=g, cs=cs: e.transpose(ps_tr[:, g * 128:(g + 1) * 128], xact[:, 4 + g, cs], ident[:]),
                     reads=[("xact", 4 + g), ("ident",)], writes=[("pstr",)])
            P.op("act", lambda e: e.copy(out=Btm[:].rearrange("p g n -> p (g n)"), in_=ps_tr[:, 0:256]),
                 reads=[("pstr",)], writes=[("Btm",)])
            if dbg < 3.4:
                continue
            for c in range(4):
                P.op("pe", lambda e, c=c, cs=cs: e.transpose(ps_tr[:, c * 128:(c + 1) * 128], xact[:, c, cs], ident[:]),
                     reads=[("xact", c), ("ident",), ("Btm",)], writes=[("pstr",)])
            P.op("act", lambda e: e.copy(out=xs_tm[:], in_=ps_tr[:]), reads=[("pstr",)], writes=[("xstm",)])
            for h in range(SH if dbg >= 4 else 0):
                g = h // 4
                i2 = cnt["hh"] % 2
                cnt["hh"] += 1
                hs = slice(h * 64, (h + 1) * 64)
                P.op("dve", lambda e, i2=i2, sub=sub, h=h: e.tensor_scalar(
                    out=LAb[i2][:], in0=ones[:], scalar1=la[:, sub, h:h + 1], scalar2=None, op0=ALU.mult),
                    reads=[("ones",), ("la", sub)], writes=[("LAb", i2)])
                P.op("pe", lambda e, i2=i2: e.matmul(ps_seg[i2][:], lhsT=LAb[i2][:], rhs=U[:], start=True, stop=True),
                     reads=[("LAb", i2), ("U",)], writes=[("psseg", i2)])
                P.op("act", lambda e, i2=i2: e.activation(out=ecl[i2][:], in_=ps_seg[i2][:, 127:128], func=AF.Exp),
                     writes=[("ecl", i2), ("psseg", i2)])
                P.op("dve", lambda e, i2=i2: e.tensor_tensor(out=decT[i2][:], in0=ps_seg[i2][:], in1=nm[:], op=ALU.add),
                     reads=[("psseg", i2), ("nm",)], writes=[("decT", i2)])
                P.op("act", lambda e, i2=i2, h=h: e.activation(
                    out=decT[i2][:], in_=decT[i2][:], func=AF.Exp, bias=ncum[:, h:h + 1], scale=1.0),
                    reads=[("decT", i2), ("ncum",)], writes=[("decT", i2)])
                P.op("dve", lambda e, i2=i2, g=g: e.tensor_tensor(out=MT[i2][:], in0=decT[i2][:], in1=cbT[:, g, :], op=ALU.mult),
                     reads=[("decT", i2), ("cbT", g)], writes=[("MT", i2)])
                if dbg < 4.2:
                    continue
                P.op("pool", lambda e, i2=i2, sub=sub, h=h, hs=hs: e.tensor_scalar(
                    out=xd[i2][:], in0=xs_tm[:, hs], scalar1=dt[:, sub, h:h + 1], scalar2=None, op0=ALU.mult),
                    reads=[("xstm",), ("dt", sub)], writes=[("xd", i2)])
                P.op("dve", lambda e, i2=i2: e.tensor_scalar(
                    out=wxd[i2][:], in0=xd[i2][:], scalar1=decT[i2][:, 127:128], scalar2=None, op0=ALU.mult),
                    reads=[("xd", i2), ("decT", i2)], writes=[("wxd", i2)])
                if dbg < 4.3:
                    continue
                py = ps_y[i2]
                P.op("pe", lambda e, py=py, i2=i2: e.matmul(py[:, 0:64], lhsT=MT[i2][:], rhs=xd[i2][:], start=True, stop=True),
                     reads=[("MT", i2), ("xd", i2)], writes=[("psy", i2)])
                P.op("pe", lambda e, py=py, g=g, cs=cs, h=h: e.matmul(
                    py[:, 64:128], lhsT=xact[:, 6 + g, cs], rhs=state_b[:, h, :], start=True, stop=True),
                    reads=[("xact", 6 + g), ("state_b", h)], writes=[("psy", i2)])
                P.op("pe", lambda e, py=py, g=g, i2=i2: e.matmul(
                    py[:, 128:192], lhsT=Btm[:, g, :], rhs=wxd[i2][:], start=True, stop=True),
                    reads=[("Btm",), ("wxd", i2)], writes=[("psy", i2)])
                if dbg < 4.4:
                    continue
                P.op("act", lambda e, py=py, i2=i2: e.copy(out=ysb[i2][:], in_=py[:, 0:64]),
                     writes=[("ysb", i2), ("psy", i2)])
                P.op("dve", lambda e, py=py, i2=i2, h=h, hs=hs: e.scalar_tensor_tensor(
                    out=y_all[:, hs], in0=py[:, 64:128], scalar=ecum[:, h:h + 1], in1=ysb[i2][:], op0=ALU.mult, op1=ALU.add),
                    reads=[("psy", i2), ("ecum",), ("ysb", i2)], writes=[("yall", h)])
                P.op("dve", lambda e, h=h, hs=hs: e.scalar_tensor_tensor(
                    out=y_all[:, hs], in0=xs_tm[:, hs], scalar=hp[:, 16 + h:17 + h], in1=y_all[:, hs], op0=ALU.mult, op1=ALU.add),
                    reads=[("xstm",), ("hp",), ("yall", h)], writes=[("yall", h)])
                P.op("dve", lambda e, py=py, i2=i2, h=h: e.scalar_tensor_tensor(
                    out=state[:, h, :], in0=state[:, h, :], scalar=ecl[i2][:, 0:1], in1=py[:, 128:192], op0=ALU.mult, op1=ALU.add),
                    reads=[("psy", i2), ("ecl", i2), ("state", h)], writes=[("state", h)])
                P.op("act", lambda e, h=h: e.copy(out=state_b[:, h, :], in_=state[:, h, :]),
                     reads=[("state", h)], writes=[("state_b", h)])
            if dbg < 5:
                continue
            yres = [("yall", h) for h in range(SH)]
            P.op("dve", lambda e, sub=sub: e.tensor_tensor(out=y_all[:], in0=y_all[:], in1=zs[:, sub, :], op=ALU.mult),
                 reads=yres + [("zs", sub)], writes=yres)
            for g in range(2):
                P.op("act", lambda e, g=g: e.activation(out=sqj[:], in_=y_all[:, g * 256:(g + 1) * 256], func=AF.Square,
                                                        accum_out=ss[:, g:g + 1]),
                     reads=yres, writes=[("sqj",), ("ss", g)])
            P.op("act", lambda e: e.activation(out=ss[:], in_=ss[:], func=AF.Ln, bias=epsb[:], scale=1.0 / 256),
                 reads=[("ss", 0), ("ss", 1), ("epsb",)], writes=[("ss", 0), ("ss", 1)])
            P.op("act", lambda e: e.activation(out=ss[:], in_=ss[:], func=AF.Exp, scale=-0.5),
                 reads=[("ss", 0), ("ss", 1)], writes=[("ss", 0), ("ss", 1)])
            for g in range(2):
                gs = slice(g * 256, (g + 1) * 256)
                P.op("dve", lambda e, g=g, gs=gs: e.scalar_tensor_tensor(
                    out=o_tm[:, gs], in0=y_all[:, gs], scalar=ss[:, g:g + 1], in1=nw[:, gs], op0=ALU.mult, op1=ALU.mult),
                    reads=yres + [("ss", g), ("nw",)], writes=[("otm", g)])
            for c in range(4):
                P.op("pe", lambda e, c=c: e.transpose(ps_tr[:, c * 128:(c + 1) * 128], o_tm[:, c * 128:(c + 1) * 128], ident[:]),
                     reads=[("otm", 0), ("otm", 1), ("ident",), ("xstm",)], writes=[("pstr",)])
            P.op("act", lambda e, cs=cs: e.copy(out=o_fm[:, :, cs], in_=ps_tr[:].rearrange("p (c t) -> p c t", c=4)),
                 reads=[("pstr",)], writes=[("ofm",)])
        for c in range(4):
            P.dma("pool", lambda e, c=c, ts=ts: e.dma_start(out=oT[c * 128:(c + 1) * 128, ts], in_=o_fm[:, c, :]),
                  "os", reads=[("ofm",)], writes=[("dram", "oT")])
    P.emit()
    return nc


EV_OFF = {}
_o = 0
for _n, _s in zip(("q", "kc", "vc", "ksl", "vsl", "kwn", "vwn", "gate", "z", "xbc", "dt"),
                  (1024, 256, 256, 256, 256, 256, 256, 48, 1024, 2048, 16)):
    EV_OFF[_n] = _o
    _o += _s


def ssd_inputs(hm_bf_T, w_in, conv_w, conv_b, dt_bias, a_log, d_skip, ssm_norm, hh):
    import ml_dtypes
    xo = EV_OFF["xbc"]
    xs_cols = np.arange(xo + hh * 512, xo + (hh + 1) * 512)
    b_cols = np.arange(xo + 1024 + hh * 256, xo + 1024 + (hh + 1) * 256)
    c_cols = np.arange(xo + 1536 + hh * 256, xo + 1536 + (hh + 1) * 256)
    cols = np.concatenate([xs_cols, b_cols, c_cols])
    wfm = tile_w_fm(w_in[:, cols])
    wz = tile_w_tm(w_in[:, EV_OFF["z"] + hh * 512: EV_OFF["z"] + (hh + 1) * 512], 512)
    wdt = tile_w_tm(w_in[:, EV_OFF["dt"] + hh * 8: EV_OFF["dt"] + (hh + 1) * 8], 8)
    cc = cols - xo
    cw = conv_w[:, cc]
    cb = conv_b[cc]
    c_conv = np.zeros((128, 40), np.float32)
    for c in range(8):
        c_conv[:, 5 * c:5 * c + 4] = cw[:, c * 128:(c + 1) * 128].T
        c_conv[:, 5 * c + 4] = cb[c * 128:(c + 1) * 128]
    hsl = slice(hh * 8, (hh + 1) * 8)
    c_hp = np.concatenate([np.tile(dt_bias[hsl][None], (128, 1)), np.tile(a_log[hsl][None], (128, 1)),
                           np.tile(d_skip[hsl][None], (128, 1))], axis=1).astype(np.float32)
    c_nw = np.tile(ssm_norm[hh * 512:(hh + 1) * 512][None], (128, 1)).astype(np.float32)
    r = np.arange(128)
    U = (r[:, None] <= r[None, :]).astype(np.float32)
    nm = np.where(r[:, None] <= r[None, :], 0.0, NEG).astype(np.float32)
    return {"hmT": hm_bf_T, "wfm": wfm, "wz": wz, "wdt": wdt, "c_conv": c_conv, "c_hp": c_hp, "c_nw": c_nw,
            "c_U": U, "c_nm": nm, "c_idf": np.eye(128, dtype=np.float32),
            "c_ident": np.eye(128, dtype=np.float32).astype(ml_dtypes.bfloat16)}


NFM = 19
NTM = 280


def build_nsa(S_len, P=None):
    if P is None:
        P = Prog(bass.Bass("TRN2", target_bir_lowering=False))
    nc = P.nc
    n_tiles = S_len // TT
    NB = S_len // 16
    NBLK = S_len // 64
    NKT = S_len // 128
    NBC = (NB + 127) // 128
    hmT = P.dram("hmT", [D, S_len], BF16, "ExternalInput")
    wfm_f = P.dram("wfm", [NFM, 128, D], F32, "ExternalInput")
    wtm_f = P.dram("wtm", [1, 128, KD * NTM], F32, "ExternalInput")
    w1_f = P.dram("w1", [2, 128, 32 * 256], F32, "ExternalInput")
    w2k_f = P.dram("w2k", [128, 2 * 128], F32, "ExternalInput")
    w2v_f = P.dram("w2v", [128, 2 * 64], F32, "ExternalInput")
    posT_f = P.dram("posT", [128, 2 * 32], F32, "ExternalInput")
    wfm_b = P.dram("wfm_b", [NFM, 128, D], BF16)
    wtm_b = P.dram("wtm_b", [1, 128, KD * NTM], BF16)
    w1_b = P.dram("w1_b", [2, 128, 32 * 256], BF16)
    t_cq = P.dram("t_cq", [128, S_len], F32, "ExternalInput")
    t_sq = P.dram("t_sq", [128, S_len], F32, "ExternalInput")
    t_ck = P.dram("t_ck", [128, S_len], F32, "ExternalInput")
    t_sk = P.dram("t_sk", [128, S_len], F32, "ExternalInput")
    c_ident = P.dram("c_ident", [128, 128], BF16, "ExternalInput")
    c_E = P.dram("c_E", [128, S_len], BF16, "ExternalInput")
    c_Wb = P.dram("c_Wb", [128, 8 * 512], BF16, "ExternalInput")
    c_Cw = P.dram("c_Cw", [128, 1024], BF16, "ExternalInput")
    c_AB = P.dram("c_AB", [128, 512], F32, "ExternalInput")
    oT = P.dram("oT", [512, S_len], BF16, "ExternalOutput")
    QT = P.dram("QT", [4, 128, S_len], BF16)
    KslT = P.dram("KslT", [2, 128, S_len], BF16)
    KwnT = P.dram("KwnT", [2, 128, S_len], BF16)
    KcT = P.dram("KcT", [128, S_len], BF16)
    VcT = P.dram("VcT", [128, S_len], BF16)
    Vsl = P.dram("Vsl", [2, S_len, 64], BF16)
    Vwn = P.dram("Vwn", [2, S_len, 64], BF16)
    Gate = P.dram("Gate", [S_len, 24], F32)

    ident = P.sb("n_ident", [128, 128], BF16)
    zer = P.sb("n_zer", [128, 260], BF16)
    ps_st = [P.ps(f"n_psst{i}", [128, 512]) for i in range(2)]
    ps_acc = [P.ps(f"n_psacc{i}", [128, 4, 65]) for i in range(2)]
    ps_c = P.ps("n_psc", [128, 512])
    ps_tr = P.ps("n_pstr", [128, 512], BF16)
    ps_oc = P.ps("n_psoc", [128, 64])
    P.dma("pool", lambda e: e.dma_start(out=ident[:], in_=c_ident), "cl", writes=[("ident",)])
    P.op("dve", lambda e: e.memset(zer[:], 0.0), writes=[("zer",)])

    hm = P.sb("n_hm", [128, KD, TT], BF16)
    wfm = [P.sb(f"n_wfm{i}", [128, D], BF16) for i in range(2)]
    wtm = P.sb("n_wtm", [128, KD * NTM], BF16)
    tab = P.sb("n_tab", [128, 4, TT], F32)
    raw = P.sb("n_raw", [128, TT], F32)
    tmp = P.sb("n_tmp", [128, TT], F32)
    fmo = [P.sb(f"n_fmo{i}", [128, TT], BF16) for i in range(2)]
    vg = [P.sb(f"n_vg{i}", [128, 256], BF16) for i in range(2)]
    gt = [P.sb(f"n_gt{i}", [128, 24], F32) for i in range(2)]
    cv = Conv(P)
    for c in range(NFM):
        cv.run(wfm_b[c], wfm_f[c], ("wfm", c))
    cv.run(wtm_b[0], wtm_f[0], "wtm")
    for kv in range(2):
        cv.run(w1_b[kv], w1_f[kv], ("w1", kv))
    P.dma("sp", lambda e: e.dma_start(out=wtm[:], in_=wtm_b[0]), "wl", reads=P.dkeys["wtm"], writes=[("wtm",)])
    cnt = {"wfm": 0, "pst": 0, "fmo": 0, "vg": 0}

    def fm_chunk(c):
        j = cnt["wfm"] % 2
        cnt["wfm"] += 1
        wt = wfm[j]
        P.dma("sp", lambda e, wt=wt, c=c: e.dma_start(out=wt[:], in_=wfm_b[c]),
              "wl", reads=P.dkeys[("wfm", c)], writes=[("wfm", j)])
        pj = cnt["pst"] % 2
        cnt["pst"] += 1
        pa = ps_st[pj]
        for k in range(KD):
            P.op("pe", lambda e, pa=pa, wt=wt, k=k: e.matmul(
                pa[:], lhsT=wt[:, k * 128:(k + 1) * 128], rhs=hm[:, k, :], start=(k == 0), stop=(k == KD - 1)),
                reads=[("wfm", j), ("hm",)], writes=[("psst", pj)])
        return pa, pj

    def roped(c_main, c_swap, ci, si, dst_ap, dst_key, ts):
        pa, pj = fm_chunk(c_main)
        P.op("dve", lambda e, pa=pa, ci=ci: e.tensor_tensor(out=raw[:], in0=pa[:], in1=tab[:, ci, :], op=ALU.mult),
             reads=[("psst", pj), ("tab",)], writes=[("raw",)])
        pb, pk = fm_chunk(c_swap)
        P.op("dve", lambda e, pb=pb, si=si: e.tensor_tensor(out=tmp[:], in0=pb[:], in1=tab[:, si, :], op=ALU.mult),
             reads=[("psst", pk), ("tab",)], writes=[("tmp",)])
        fj = cnt["fmo"] % 2
        cnt["fmo"] += 1
        fo = fmo[fj]
        P.op("dve", lambda e, fo=fo: e.tensor_tensor(out=fo[:], in0=raw[:], in1=tmp[:], op=ALU.add),
             reads=[("raw",), ("tmp",)], writes=[("fmo", fj)])
        P.dma("pool", lambda e, fo=fo, dst_ap=dst_ap: e.dma_start(out=dst_ap, in_=fo[:]),
              "sc", reads=[("fmo", fj)], writes=[("dram", dst_key)])

    for t in range(n_tiles):
        ts = slice(t * TT, (t + 1) * TT)
        for k in range(KD):
            P.dma("pool", lambda e, k=k, ts=ts: e.dma_start(out=hm[:, k, :], in_=hmT[k * 128:(k + 1) * 128, ts]),
                  "hl", writes=[("hm",)])
        for i, src in enumerate((t_cq, t_sq, t_ck, t_sk)):
            P.dma("pool", lambda e, i=i, src=src, ts=ts: e.dma_start(out=tab[:, i, :], in_=src[:, ts]),
                  "tl", writes=[("tab",)])
        for j in range(4):
            roped(j, 4 + j, 0, 1, QT[j][:, ts], ("QT", j, t), ts)
        for g in range(2):
            roped(8 + g, 10 + g, 2, 3, KslT[g][:, ts], ("KslT", g, t), ts)
            roped(12 + g, 14 + g, 2, 3, KwnT[g][:, ts], ("KwnT", g, t), ts)
        roped(16, 17, 2, 3, KcT[:, ts], ("KcT", t), ts)
        pa, pj = fm_chunk(18)
        fj = cnt["fmo"] % 2
        cnt["fmo"] += 1
        fo = fmo[fj]
        P.op("act", lambda e, pa=pa, fo=fo: e.copy(out=fo[:], in_=pa[:]), reads=[("psst", pj)], writes=[("fmo", fj)])
        P.dma("pool", lambda e, fo=fo, ts=ts: e.dma_start(out=VcT[:, ts], in_=fo[:]),
              "sc", reads=[("fmo", fj)], writes=[("dram", ("VcT", t))])
        for sub in range(4):
            ssl = slice(sub * 128, (sub + 1) * 128)
            r0 = t * TT + sub * 128
            pj = cnt["pst"] % 2
            cnt["pst"] += 1
            pa = ps_st[pj]
            for k in range(KD):
                P.op("pe", lambda e, pa=pa, k=k, ssl=ssl: e.matmul(
                    pa[:, 0:NTM], lhsT=hm[:, k, ssl], rhs=wtm[:, k * NTM:(k + 1) * NTM], start=(k == 0), stop=(k == KD - 1)),
                    reads=[("wtm",), ("hm",)], writes=[("psst", pj)])
            vj = cnt["vg"] % 2
            cnt["vg"] += 1
            P.op("act", lambda e, pa=pa, vj=vj: e.copy(out=vg[vj][:], in_=pa[:, 0:256]),
                 reads=[("psst", pj)], writes=[("vg", vj)])
            P.op("act", lambda e, pa=pa, vj=vj: e.activation(out=gt[vj][:], in_=pa[:, 256:280], func=AF.Sigmoid),
                 reads=[("psst", pj)], writes=[("gt", vj)])
            for g in range(2):
                P.dma("pool", lambda e, vj=vj, g=g, r0=r0: e.dma_start(out=Vsl[g][r0:r0 + 128, :], in_=vg[vj][:, g * 64:(g + 1) * 64]),
                      "sc", reads=[("vg", vj)], writes=[("dram", ("Vsl", g, t, sub))])
                P.dma("pool", lambda e, vj=vj, g=g, r0=r0: e.dma_start(out=Vwn[g][r0:r0 + 128, :], in_=vg[vj][:, 128 + g * 64:128 + (g + 1) * 64]),
                      "sc", reads=[("vg", vj)], writes=[("dram", ("Vwn", g, t, sub))])
            P.dma("pool", lambda e, vj=vj, r0=r0: e.dma_start(out=Gate[r0:r0 + 128, :], in_=gt[vj][:]),
                  "sc", reads=[("gt", vj)], writes=[("dram", ("Gate", t, sub))])

    kcv = P.sb("n_kcv", [128, 2, S_len + 16], BF16)
    w1 = P.sb("n_w1", [128, 32 * 256], BF16)
    w2k = P.sb("n_w2k", [128, 256], BF16)
    w2v = P.sb("n_w2v", [128, 128], BF16)
    w2f = P.sb("n_w2f", [128, 256], F32)
    posT = P.sb("n_posT", [128, 64], BF16)
    posf = P.sb("n_posf", [128, 64], F32)
    c1 = P.sb("n_c1", [128, 2], F32)
    hid = P.sb("n_hid", [128, 2, NB], BF16)
    KcmpT = P.sb("n_KcmpT", [128, 2, NB], BF16)
    Vcmp = P.sb("n_Vcmp", [128, 2, NBC, 64], BF16)
    P.op("dve", lambda e: e.memset(kcv[:, :, S_len:], 0.0), writes=[("kcv_pad",)])
    P.dma("pool", lambda e: e.dma_start(out=kcv[:, 0, 0:S_len], in_=KcT), "kl",
          reads=[("dram", ("KcT", t)) for t in range(n_tiles)], writes=[("kcv", 0)])
    P.dma("pool", lambda e: e.dma_start(out=kcv[:, 1, 0:S_len], in_=VcT), "kl",
          reads=[("dram", ("VcT", t)) for t in range(n_tiles)], writes=[("kcv", 1)])
    P.dma("pool", lambda e: e.dma_start(out=posf[:], in_=posT_f), "cl", writes=[("posf",)])
    P.op("dve", lambda e: e.tensor_copy(out=posT[:], in_=posf[:]), reads=[("posf",)], writes=[("posT",)])
    P.dma("pool", lambda e: e.dma_start(out=w2f[:], in_=w2k_f), "cl", writes=[("w2f",)])
    P.op("dve", lambda e: e.tensor_copy(out=w2k[:], in_=w2f[:]), reads=[("w2f",)], writes=[("w2k",)])
    P.dma("pool", lambda e: e.dma_start(out=w2f[:, 0:128], in_=w2v_f), "cl", reads=[("w2k",)], writes=[("w2f",)])
    P.op("dve", lambda e: e.tensor_copy(out=w2v[:], in_=w2f[:, 0:128]), reads=[("w2f",)], writes=[("w2v",)])
    for kv in range(2):
        P.dma("sp", lambda e, kv=kv: e.dma_start(out=w1[:], in_=w1_b[kv]), "wl",
              reads=P.dkeys[("w1", kv)], writes=[("w1",)])
        for hc in range(2):
            for l in range(32):
                P.op("pe", lambda e, hc=hc, l=l, kv=kv: e.matmul(
                    ps_oc[:, hc:hc + 1], lhsT=w1[0:64, l * 256 + hc * 128:l * 256 + (hc + 1) * 128],
                    rhs=posT[0:64, kv * 32 + l:kv * 32 + l + 1], start=(l == 0), stop=(l == 31)),
                    reads=[("w1",), ("posT",)], writes=[("psoc",)])
        P.op("dve", lambda e: e.tensor_copy(out=c1[:], in_=ps_oc[:, 0:2]), reads=[("psoc",)], writes=[("c1",)])
        for g in range(2):
            gp = slice(g * 64, (g + 1) * 64)
            for hc in range(2):
                for l in range(32):
                    P.op("pe", lambda e, hc=hc, l=l, kv=kv, gp=gp: e.matmul(
                        ps_c[:, 0:NB], lhsT=w1[gp, l * 256 + hc * 128:l * 256 + (hc + 1) * 128],
                        rhs=kcv[gp, kv, l:l + 16 * (NB - 1) + 1:16], start=(l == 0), stop=(l == 31)),
                        reads=[("w1",), ("kcv", kv), ("kcv_pad",)], writes=[("psc",)])
                P.op("act", lambda e, hc=hc: e.activation(out=hid[:, hc, :], in_=ps_c[:, 0:NB], func=AF.Silu,
                                                          bias=c1[:, hc:hc + 1], scale=1.0),
                     reads=[("psc",), ("c1",)], writes=[("hid", hc)])
            if kv == 0:
                for hc in range(2):
                    P.op("pe", lambda e, hc=hc: e.matmul(ps_c[:, 0:NB], lhsT=w2k[:, hc * 128:(hc + 1) * 128], rhs=hid[:, hc, :],
                                                         start=(hc == 0), stop=(hc == 1)),
                         reads=[("w2k",), ("hid", 0), ("hid", 1)], writes=[("psc",)])
                P.op("act", lambda e, g=g: e.copy(out=KcmpT[:, g, :], in_=ps_c[:, 0:NB]), reads=[("psc",)], writes=[("KcmpT", g)])
            else:
                for ncn in range(NBC):
                    nn = min(128, NB - ncn * 128)
                    for hc in range(2):
                        P.op("pe", lambda e, hc=hc, ncn=ncn, nn=nn: e.matmul(
                            ps_oc[0:nn, :], lhsT=hid[:, hc, ncn * 128:ncn * 128 + nn], rhs=w2v[:, hc * 64:(hc + 1) * 64],
                            start=(hc == 0), stop=(hc == 1)),
                            reads=[("w2v",), ("hid", 0), ("hid", 1)], writes=[("psoc",)])
                    P.op("act", lambda e, g=g, ncn=ncn, nn=nn: e.copy(out=Vcmp[0:nn, g, ncn, :], in_=ps_oc[0:nn, :]),
                         reads=[("psoc",)], writes=[("Vcmp", g)])

    E = hm[:].rearrange("p k t -> p (k t)")[:, 0:S_len]
    Wb = P.sb("n_Wb", [128, 8, 512], BF16)
    Cw = P.sb("n_Cw", [128, 1024], BF16)
    AB = P.sb("n_AB", [128, 512], F32)
    P.dma("pool", lambda e: e.dma_start(out=E, in_=c_E), "cl", writes=[("E",), ("hm",)])
    for dst, src, tag in ((Wb, c_Wb, "Wb"), (Cw, c_Cw, "Cw"), (AB, c_AB, "AB")):
        P.dma("pool", lambda e, dst=dst, src=src: e.dma_start(
            out=dst[:] if len(dst.shape) == 2 else dst[:].rearrange("p a b -> p (a b)"), in_=src), "cl", writes=[(tag,)])
    Ks = kcv[:, 0, 0:S_len]
    Kw = kcv[:, 1, 0:S_len]
    Vs1 = P.sb("n_Vs1", [128, NKT, 65], BF16)
    Vw1 = P.sb("n_Vw1", [128, NKT, 65], BF16)
    Qt = P.sb("n_Qt", [128, 2, TT], BF16)
    gtile = P.sb("n_gtile", [128, 4, 24], F32)
    pc = P.sb("n_pc", [128, NB], F32)
    pcb = P.sb("n_pcb", [128, NBC * 128], BF16)
    pcT = P.sb("n_pcT", [128, NBC * 128], BF16)
    imp = P.sb("n_imp", [128, NB], F32)
    rs = P.sb("n_rs", [128, 2], F32)
    vals = P.sb("n_vals", [128, NBLK], F32)
    vtmp = P.sb("n_vtmp", [128, NBLK], F32)
    m8 = P.sb("n_m8", [128, 16], F32)
    sel = P.sb("n_sel", [128, NBLK], F32)
    negb = P.sb("n_negb", [128, 128], BF16)
    negbT = P.sb("n_negbT", [128, TT], BF16)
    PT = [P.sb(f"n_PT{i}", [128, TT], BF16) for i in range(2)]
    oacc = P.sb("n_oacc", [128, 4, 4, 64], F32)
    oaccb = P.sb("n_oaccb", [128, 4, 256], BF16)
    w4 = P.sb("n_w4", [128, 4], F32)
    o_fm = P.sb("n_ofm", [128, 2, TT], BF16)
    P.op("dve", lambda e: e.memset(Vs1[:], 1.0), writes=[("Vs1",)])
    P.op("dve", lambda e: e.memset(Vw1[:], 1.0), writes=[("Vw1",)])
    P.op("dve", lambda e: e.memset(negb[:], 0.0), writes=[("negb",)])
    cnt.update({"acc": 0, "PT": 0})

    for g in range(2):
        P.dma("pool", lambda e, g=g: e.dma_start(out=Ks, in_=KslT[g]), "kl",
              reads=[("dram", ("KslT", g, t)) for t in range(n_tiles)], writes=[("Ks",), ("kcv", 0), ("kcv_pad",)])
        P.dma("pool", lambda e, g=g: e.dma_start(out=Kw, in_=KwnT[g]), "kl",
              reads=[("dram", ("KwnT", g, t)) for t in range(n_tiles)], writes=[("Kw",), ("kcv", 1), ("kcv_pad",)])
        vdeps = lambda nm: [("dram", (nm, g, t, s)) for t in range(n_tiles) for s in range(4)]
        P.dma("pool", lambda e, g=g: e.dma_start(out=Vs1[:, :, 0:64], in_=Vsl[g].rearrange("(k p) d -> p k d", p=128)), "kl",
              reads=vdeps("Vsl"), writes=[("Vs1",)])
        P.dma("pool", lambda e, g=g: e.dma_start(out=Vw1[:, :, 0:64], in_=Vwn[g].rearrange("(k p) d -> p k d", p=128)), "kl",
              reads=vdeps("Vwn"), writes=[("Vw1",)])
        for qt in range(n_tiles):
            T0 = qt * TT
            ts = slice(T0, T0 + TT)
            for j in range(2):
                P.dma("pool", lambda e, j=j, g=g, ts=ts: e.dma_start(out=Qt[:, j, :], in_=QT[2 * g + j][:, ts]), "ql",
                      reads=[("dram", ("QT", 2 * g + j, qt))], writes=[("Qt",)])
            P.dma("pool", lambda e, ts=ts: e.dma_start(out=gtile[:], in_=Gate[ts, :].rearrange("(s p) c -> p s c", p=128)), "ql",
                  reads=[("dram", ("Gate", qt, s)) for s in range(4)], writes=[("gtile",)])
            NBv = min(NB, 32 * (qt + 1))
            nch = (NBv + 127) // 128
            for sub in range(4):
                m = 4 * qt + sub
                qs = slice(sub * 128, (sub + 1) * 128)
                P.op("pool", lambda e: e.memset(imp[:], 0.0), writes=[("imp",)])
                for hl in range(4):
                    j, half = hl // 2, hl % 2
                    hp_ = slice(half * 64, (half + 1) * 64)
                    gcol = 3 * (4 * g + hl)
                    P.op("pe", lambda e, j=j, hp_=hp_, qs=qs, g=g, NBv=NBv: e.matmul(
                        ps_c[:, 0:NBv], lhsT=Qt[hp_, j, qs], rhs=KcmpT[hp_, g, 0:NBv], start=True, stop=False),
                        reads=[("Qt",), ("KcmpT", g)], writes=[("psc",)])
                    P.op("pe", lambda e, m=m, NBv=NBv: e.matmul(
                        ps_c[:, 0:NBv], lhsT=ident[:], rhs=Cw[:, 512 - 8 * m:512 - 8 * m + NBv], start=False, stop=True),
                        reads=[("ident",), ("Cw",)], writes=[("psc",)])
                    P.op("act", lambda e, NBv=NBv: e.activation(out=pc[:, 0:NBv], in_=ps_c[:, 0:NBv], func=AF.Exp,
                                                                accum_out=rs[:, 0:1]),
                         reads=[("psc",)], writes=[("pc",), ("rs",)])
                    P.op("dve", lambda e: e.tensor_scalar(out=rs[:, 0:1], in0=rs[:, 0:1], scalar1=1e-30, scalar2=None, op0=ALU.max),
                         reads=[("rs",)], writes=[("rs",)])
                    P.op("dve", lambda e: e.reciprocal(out=rs[:, 1:2], in_=rs[:, 0:1]), reads=[("rs",)], writes=[("rs",)])
                    P.op("dve", lambda e, NBv=NBv: e.tensor_scalar(out=pc[:, 0:NBv], in0=pc[:, 0:NBv], scalar1=rs[:, 1:2],
                                                                   scalar2=None, op0=ALU.mult),
                         reads=[("pc",), ("rs",)], writes=[("pc",)])
                    P.op("pool", lambda e, NBv=NBv: e.tensor_tensor(out=imp[:, 0:NBv], in0=imp[:, 0:NBv], in1=pc[:, 0:NBv], op=ALU.add),
                         reads=[("pc",), ("imp",)], writes=[("imp",)])
                    P.op("act", lambda e, NBv=NBv: e.copy(out=pcb[:, 0:NBv], in_=pc[:, 0:NBv]), reads=[("pc",)], writes=[("pcb",)])
                    for c in range(nch):
                        nn = min(128, NBv - c * 128)
                        P.op("pe", lambda e, c=c, nn=nn: e.transpose(ps_tr[0:nn, c * 128:(c + 1) * 128], pcb[:, c * 128:c * 128 + nn], ident[:]),
                             reads=[("pcb",), ("ident",)], writes=[("pstr",)])
                    for c in range(nch):
                        nn = min(128, NBv - c * 128)
                        P.op("act", lambda e, c=c, nn=nn: e.copy(out=pcT[0:nn, c * 128:(c + 1) * 128], in_=ps_tr[0:nn, c * 128:(c + 1) * 128]),
                             reads=[("pstr",)], writes=[("pcT",)])
                    for c in range(nch):
                        nn = min(128, NBv - c * 128)
                        P.op("pe", lambda e, c=c, nn=nn, g=g: e.matmul(
                            ps_oc[:], lhsT=pcT[0:nn, c * 128:(c + 1) * 128], rhs=Vcmp[0:nn, g, c, :], start=(c == 0), stop=(c == nch - 1)),
                            reads=[("pcT",), ("Vcmp", g)], writes=[("psoc",)])
                    P.op("dve", lambda e, sub=sub, hl=hl, gcol=gcol: e.tensor_scalar(
                        out=oacc[:, sub, hl, :], in0=ps_oc[:], scalar1=gtile[:, sub, gcol:gcol + 1], scalar2=None, op0=ALU.mult),
                        reads=[("psoc",), ("gtile",)], writes=[("oacc", sub, hl)])
                P.op("dve", lambda e: e.tensor_reduce(out=vals[:], in_=imp[:].rearrange("p (j f) -> p j f", f=4), axis=AX.X, op=ALU.add),
                     reads=[("imp",)], writes=[("vals",)])
                a0 = 128 - 2 * m
                P.op("dve", lambda e, a0=a0: e.tensor_tensor(out=vals[:], in0=vals[:], in1=AB[:, a0:a0 + NBLK], op=ALU.mult),
                     reads=[("vals",), ("AB",)], writes=[("vals",)])
                P.op("dve", lambda e, a0=a0: e.tensor_tensor(out=vals[:], in0=vals[:], in1=AB[:, 256 + a0:256 + a0 + NBLK], op=ALU.add),
                     reads=[("vals",), ("AB",)], writes=[("vals",)])
                P.op("dve", lambda e: e.memset(vals[:, 0:1], 1e4), reads=[("vals",)], writes=[("vals",)])
                P.op("dve", lambda e: e.max(out=m8[:, 0:8], in_=vals[:]), reads=[("vals",)], writes=[("m8",)])
                P.op("dve", lambda e: e.match_replace(out=vtmp[:], in_to_replace=m8[:, 0:8], in_values=vals[:], imm_value=-2.0),
                     reads=[("vals",), ("m8",)], writes=[("vtmp",)])
                P.op("dve", lambda e: e.max(out=m8[:, 8:16], in_=vtmp[:]), reads=[("vtmp",)], writes=[("m8",)])
                P.op("dve", lambda e: e.tensor_scalar(out=sel[:], in0=vals[:], scalar1=m8[:, 15:16], scalar2=None, op0=ALU.is_ge),
                     reads=[("vals",), ("m8",)], writes=[("sel",)])
                P.op("dve", lambda e: e.tensor_scalar(out=vtmp[:], in0=vals[:], scalar1=0.0, scalar2=None, op0=ALU.is_ge),
                     reads=[("vals",), ("sel",)], writes=[("vtmp",)])
                P.op("dve", lambda e: e.tensor_tensor(out=sel[:], in0=sel[:], in1=vtmp[:], op=ALU.mult),
                     reads=[("sel",), ("vtmp",)], writes=[("sel",)])
                P.op("dve", lambda e: e.tensor_scalar(out=negb[:, 0:NBLK], in0=sel[:], scalar1=-1.0, scalar2=-NEG, op0=ALU.add, op1=ALU.mult),
                     reads=[("sel",)], writes=[("negb",)])
                P.op("pe", lambda e: e.transpose(ps_tr[:, 0:128], negb[:], ident[:]), reads=[("negb",), ("ident",)], writes=[("pstr",)])
                P.op("act", lambda e, qs=qs: e.copy(out=negbT[:, qs], in_=ps_tr[:, 0:128]), reads=[("pstr",)], writes=[("negbT",)])
            for hl in range(4):
                j, half = hl // 2, hl % 2
                hp_ = slice(half * 64, (half + 1) * 64)
                for br in range(2):
                    gcol = 3 * (4 * g + hl) + 1 + br
                    Kt, V1 = (Ks, Vs1) if br == 0 else (Kw, Vw1)
                    kres, vres = (("Ks",), ("Vs1",)) if br == 0 else (("Kw",), ("Vw1",))
                    if br == 0:
                        units = [(kt, kt - 4 * qt + 4) for kt in range(4 * qt + 4)]
                    else:
                        units = [(4 * qt - 4 + o, o) for o in range(8) if 4 * qt - 4 + o >= 0]
                    aj = cnt["acc"] % 2
                    cnt["acc"] += 1
                    acc = ps_acc[aj]
                    P.op("pe", lambda e, acc=acc: e.matmul(acc[:].rearrange("p a b -> p (a b)"), lhsT=zer[:, 0:128], rhs=zer[:],
                                                           start=True, stop=False),
                         reads=[("zer",)], writes=[("psacc", aj)])
                    for ui, (kt, o) in enumerate(units):
                        pj = cnt["pst"] % 2
                        cnt["pst"] += 1
                        pst = ps_st[pj]
                        ks = slice(kt * 128, (kt + 1) * 128)
                        need_wb = (o >= 4) if br == 0 else True
                        P.op("pe", lambda e, pst=pst, Kt=Kt, hp_=hp_, ks=ks, j=j, br=br, need_wb=need_wb: e.matmul(
                            pst[:], lhsT=Kt[hp_, ks], rhs=Qt[hp_, j, :], start=True, stop=(br == 1 and not need_wb)),
                            reads=[kres, ("Qt",)], writes=[("psst", pj)])
                        if br == 0:
                            P.op("pe", lambda e, pst=pst, ks=ks, need_wb=need_wb: e.matmul(
                                pst[:], lhsT=E[0:NBLK, ks], rhs=negbT[0:NBLK, :], start=False, stop=not need_wb),
                                reads=[("E",), ("negbT",)], writes=[("psst", pj)])
                        if need_wb:
                            P.op("pe", lambda e, pst=pst, o=o: e.matmul(pst[:], lhsT=ident[:], rhs=Wb[:, o, :], start=False, stop=True),
                                 reads=[("ident",), ("Wb",)], writes=[("psst", pj)])
                        tj = cnt["PT"] % 2
                        cnt["PT"] += 1
                        pt = PT[tj]
                        P.op("act", lambda e, pt=pt, pst=pst: e.activation(out=pt[:], in_=pst[:], func=AF.Exp),
                             reads=[("psst", pj)], writes=[("PT", tj)])
                        subs = [s for s in range(4) if (o - 4 <= s <= o if br == 1 else s >= o - 4)]
                        last_u = ui == len(units) - 1
                        for s in subs:
                            P.op("pe", lambda e, acc=acc, pt=pt, V1=V1, kt=kt, s=s, last=(last_u and s == subs[-1]): e.matmul(
                                acc[:, s, :], lhsT=pt[:, s * 128:(s + 1) * 128], rhs=V1[:, kt, :], start=False, stop=last),
                                reads=[("PT", tj), vres], writes=[("psacc", aj)])
                    P.op("dve", lambda e, acc=acc: e.tensor_scalar(out=w4[:], in0=acc[:, :, 64], scalar1=1e-30, scalar2=None, op0=ALU.max),
                         reads=[("psacc", aj)], writes=[("w4",)])
                    P.op("dve", lambda e: e.reciprocal(out=w4[:], in_=w4[:]), reads=[("w4",)], writes=[("w4",)])
                    P.op("dve", lambda e, gcol=gcol: e.tensor_tensor(out=w4[:], in0=w4[:], in1=gtile[:, :, gcol], op=ALU.mult),
                         reads=[("w4",), ("gtile",)], writes=[("w4",)])
                    for s in range(4):
                        P.op("dve", lambda e, acc=acc, s=s, hl=hl: e.scalar_tensor_tensor(
                            out=oacc[:, s, hl, :], in0=acc[:, s, 0:64], scalar=w4[:, s:s + 1], in1=oacc[:, s, hl, :],
                            op0=ALU.mult, op1=ALU.add),
                            reads=[("psacc", aj), ("w4",), ("oacc", s, hl)], writes=[("oacc", s, hl)])
            ores = [("oacc", s, hl) for s in range(4) for hl in range(4)]
            P.op("act", lambda e: e.copy(out=oaccb[:], in_=oacc[:].rearrange("p s h d -> p s (h d)")), reads=ores, writes=[("oaccb",)])
            for jj in range(2):
                for s in range(4):
                    P.op("pe", lambda e, jj=jj, s=s: e.transpose(ps_tr[:, s * 128:(s + 1) * 128], oaccb[:, s, jj * 128:(jj + 1) * 128], ident[:]),
                         reads=[("oaccb",), ("ident",)], writes=[("pstr",)])
                P.op("act", lambda e, jj=jj: e.copy(out=o_fm[:, jj, :], in_=ps_tr[:]), reads=[("pstr",)], writes=[("ofm", jj)])
                P.dma("pool", lambda e, jj=jj, g=g, ts=ts: e.dma_start(out=oT[(2 * g + jj) * 128:(2 * g + jj + 1) * 128, ts], in_=o_fm[:, jj, :]),
                      "os", reads=[("ofm", jj)], writes=[("dram", "oT")])
    P.emit()
    return nc


def nsa_inputs(hm_bf_T, w_in, cmp_pos, cmp_w1, cmp_w2, hh, S_len):
    import ml_dtypes
    bf = ml_dtypes.bfloat16
    sw = np.concatenate([np.arange(32, 64), np.arange(0, 32)])

    def head_cols(base, h):
        return base + h * 64 + np.arange(64)

    chunks = []
    qh = [8 * hh + i for i in range(8)]
    for j in range(4):
        chunks.append(np.concatenate([head_cols(EV_OFF["q"], qh[2 * j]), head_cols(EV_OFF["q"], qh[2 * j + 1])]))
    for j in range(4):
        chunks.append(np.concatenate([head_cols(EV_OFF["q"], qh[2 * j])[sw], head_cols(EV_OFF["q"], qh[2 * j + 1])[sw]]))
    gg = [2 * hh, 2 * hh + 1]
    for nm in ("ksl", "kwn"):
        for g in gg:
            c = head_cols(EV_OFF[nm], g)
            chunks.append(np.concatenate([c, c]))
        for g in gg:
            c = head_cols(EV_OFF[nm], g)[sw]
            chunks.append(np.concatenate([c, c]))
    chunks.append(np.concatenate([head_cols(EV_OFF["kc"], gg[0]), head_cols(EV_OFF["kc"], gg[1])]))
    chunks.append(np.concatenate([head_cols(EV_OFF["kc"], gg[0])[sw], head_cols(EV_OFF["kc"], gg[1])[sw]]))
    chunks.append(np.concatenate([head_cols(EV_OFF["vc"], gg[0]), head_cols(EV_OFF["vc"], gg[1])]))
    wfm = tile_w_fm(w_in[:, np.concatenate(chunks)])
    tmc = np.concatenate([head_cols(EV_OFF["vsl"], gg[0]), head_cols(EV_OFF["vsl"], gg[1]),
                          head_cols(EV_OFF["vwn"], gg[0]), head_cols(EV_OFF["vwn"], gg[1]),
                          EV_OFF["gate"] + 24 * hh + np.arange(24)])
    wtm = tile_w_tm(w_in[:, tmc], NTM)
    w1 = cmp_w1.reshape(2, 32, 64, 256).transpose(0, 2, 1, 3).reshape(2, 64, 32 * 256)
    w1 = np.ascontiguousarray(np.concatenate([w1, w1], axis=1))
    w2k = cmp_w2[0].reshape(2, 128, 64).transpose(1, 0, 2)
    w2k = np.ascontiguousarray(np.concatenate([w2k, w2k], axis=2).reshape(128, 256))
    w2v = np.ascontiguousarray(cmp_w2[1].reshape(2, 128, 64).transpose(1, 0, 2).reshape(128, 128))
    posT = cmp_pos.transpose(2, 0, 1).reshape(64, 64)
    posT = np.ascontiguousarray(np.concatenate([posT, posT], axis=0))
    inv = (1.0 / (10000.0 ** (np.arange(0, 64, 2, dtype=np.float32) / np.float32(64)))).astype(np.float32)
    ang = np.arange(S_len, dtype=np.float32)[:, None] * inv[None, :]
    cos, sin = np.cos(ang).astype(np.float32).T, np.sin(ang).astype(np.float32).T
    p = np.arange(128)
    Ct = cos[p % 32]
    St = sin[p % 32] * np.where((p % 64) < 32, -1.0, 1.0).astype(np.float32)[:, None]
    j = np.arange(128)
    E = (np.arange(S_len)[None, :] // 64 == j[:, None]).astype(np.float32).astype(bf)
    c = np.arange(512)
    Wb = np.zeros((128, 8, 512), np.float32)
    for o in range(8):
        dlt = c[None, :] + 512 - 128 * o - p[:, None]
        Wb[:, o, :] = np.where((dlt >= 0) & (dlt < 512), 0.0, NEG)
    x = np.arange(1024) - 512
    Cw = np.where(16 * x[None, :] + 31 <= p[:, None], 0.0, NEG).astype(np.float32)
    jr = np.arange(256) - 128
    hi = (p[:, None] >= 64).astype(np.int64)
    A = (jr[None, :] <= hi - 2).astype(np.float32)
    forced = (jr[None, :] == hi) | (jr[None, :] == hi - 1)
    B = np.where(forced, 1e4, np.where(jr[None, :] > hi, -1.0, 0.0)).astype(np.float32)
    return {"hmT": hm_bf_T, "wfm": wfm, "wtm": wtm, "w1": w1, "w2k": w2k, "w2v": w2v, "posT": posT,
            "t_cq": np.ascontiguousarray(Ct * np.float32(0.125)), "t_sq": np.ascontiguousarray(St * np.float32(0.125)),
            "t_ck": np.ascontiguousarray(Ct), "t_sk": np.ascontiguousarray(St),
            "c_ident": np.eye(128, dtype=np.float32).astype(bf), "c_E": E,
            "c_Wb": Wb.reshape(128, 4096).astype(bf), "c_Cw": Cw.astype(bf),
            "c_AB": np.ascontiguousarray(np.concatenate([A, B], axis=1))}


NCORES = 4
SEQ = 8192
_PROGS = {}


def build_fused(S_len):
    nc = bass.Bass("TRN2", target_bir_lowering=False)
    ext = []
    nt = S_len // TT
    xT = nc.dram_tensor("xT", [D, S_len], F32, kind="ExternalInput").ap()
    xo = nc.dram_tensor("xo", [D, S_len], F32, kind="ExternalOutput").ap()
    x1 = nc.dram_tensor("x1", [D, S_len], F32, kind="Internal").ap()
    x2 = nc.dram_tensor("x2", [D, S_len], F32, kind="Internal").ap()
    hm0 = nc.dram_tensor("hm0", [D, S_len], BF16, kind="Internal").ap()
    hm1 = nc.dram_tensor("hm1", [D, S_len], BF16, kind="Internal").ap()
    oT0 = nc.dram_tensor("oT0", [2048, S_len], BF16, kind="Internal").ap()
    oT1 = nc.dram_tensor("oT1", [4096, S_len], BF16, kind="Internal").ap()

    def phase(prefix, fn, bind):
        P = Prog(nc, prefix, bind, ext)
        fn(P)
        nc.all_engine_barrier()
        nc.clear_and_free_semaphores(P.sem_handles)
        nc.all_engine_barrier()

    phase("A_", lambda P: build_tok(nt, 0, 1, True, P=P), {"xT": xT, "xo": x1, "hm": hm0})
    for hh in range(2):
        phase(f"N{hh}_", lambda P: build_nsa(S_len, P=P), {"hmT": hm0, "oT": oT0[hh * 512:(hh + 1) * 512, :]})
    for hh in range(2):
        phase(f"S{hh}_", lambda P: build_ssd(S_len, P=P), {"hmT": hm0, "oT": oT0[1024 + hh * 512:1024 + (hh + 1) * 512, :]})
    phase("C_", lambda P: build_tok(nt, 16, 2, True, P=P), {"xT": x1, "oT": oT0, "xo": x2, "hm": hm1})
    for hh in range(2):
        phase(f"R{hh}_", lambda P: build_ret(S_len, P=P), {"hmT": hm1, "oT": oT1[hh * 2048:(hh + 1) * 2048, :]})
    phase("E_", lambda P: build_tok(nt, 32, 1, False, P=P), {"xT": x2, "oT": oT1, "xo": xo})
    return nc, ext


def _ffn_maps(m, i, wg, wu, wd):
    m[f"wg{i}"] = tile_w_in_out(wg, KD, KF)
    m[f"wu{i}"] = tile_w_in_out(wu, KD, KF)
    m[f"wd{i}"] = tile_w_in_out(wd, KF, KD)


def fused_inputs(S_len, norm_g, wgs, wus, wds, ev_w_in, ev_cmp_pos, ev_cmp_w1, ev_cmp_w2, ev_conv_w, ev_conv_b,
                 ev_dt_bias, ev_a_log, ev_d_skip, ev_ssm_norm, ev_w_out, od_w_in, od_w_out):
    ph = {}
    a = {}
    _ffn_maps(a, 0, wgs[0, 0], wus[0, 0], wds[0, 0])
    a["g_all"] = np.concatenate([gain_cols(norm_g[0, 0]), gain_cols(norm_g[0, 1]), gain_cols(norm_g[0, 2])], axis=1)
    ph["A_"] = a
    for hh in range(2):
        ph[f"N{hh}_"] = nsa_inputs(None, ev_w_in, ev_cmp_pos, ev_cmp_w1, ev_cmp_w2, hh, S_len)
        ph[f"S{hh}_"] = ssd_inputs(None, ev_w_in, ev_conv_w, ev_conv_b, ev_dt_bias, ev_a_log, ev_d_skip, ev_ssm_norm, hh)
        ph[f"R{hh}_"] = ret_inputs(None, od_w_in, 4 * hh, S_len)
    c = {"wout": tile_w_in_out(ev_w_out, 16, KD)}
    _ffn_maps(c, 0, wgs[0, 1], wus[0, 1], wds[0, 1])
    _ffn_maps(c, 1, wgs[1, 0], wus[1, 0], wds[1, 0])
    c["g_all"] = np.concatenate([gain_cols(norm_g[0, 3]), gain_cols(norm_g[0, 4]), gain_cols(norm_g[0, 5]),
                                 gain_cols(norm_g[1, 0]), gain_cols(norm_g[1, 1]), gain_cols(norm_g[1, 2])], axis=1)
    ph["C_"] = c
    e = {"wout": tile_w_in_out(od_w_out, 32, KD)}
    _ffn_maps(e, 0, wgs[1, 1], wus[1, 1], wds[1, 1])
    e["g_all"] = np.concatenate([gain_cols(norm_g[1, 3]), gain_cols(norm_g[1, 4]), gain_cols(norm_g[1, 5])], axis=1)
    ph["E_"] = e
    return ph


def kernel(x, norm_g, ffn_w_gate, ffn_w_up, ffn_w_down, ev_w_in, ev_cmp_pos, ev_cmp_w1, ev_cmp_w2,
           ev_conv_w, ev_conv_b, ev_dt_bias, ev_a_log, ev_d_skip, ev_ssm_norm, ev_w_out, od_w_in, od_w_out):
    f32 = np.float32
    A = lambda t: np.asarray(t, f32)
    x = A(x)
    B, S_len = x.shape[0], x.shape[1]
    if "F" not in _PROGS:
        _PROGS["F"] = build_fused(S_len)
    nc, ext = _PROGS["F"]
    ph = fused_inputs(S_len, A(norm_g), A(ffn_w_gate), A(ffn_w_up), A(ffn_w_down), A(ev_w_in)[0], A(ev_cmp_pos)[0],
                      A(ev_cmp_w1)[0], A(ev_cmp_w2)[0], A(ev_conv_w)[0], A(ev_conv_b)[0], A(ev_dt_bias)[0],
                      A(ev_a_log)[0], A(ev_d_skip)[0], A(ev_ssm_norm)[0], A(ev_w_out)[0], A(od_w_in)[0], A(od_w_out)[0])
    shared = {}
    for full, name in ext:
        prefix = full[:len(full) - len(name)]
        shared[full] = ph[prefix][name]
    maps = [dict(shared, xT=np.ascontiguousarray(x[b].T)) for b in range(B)]
    res = run_bass_kernel_spmd(nc, maps, core_ids=list(range(B)))
    out = np.empty((B, S_len, D), f32)
    for b in range(B):
        out[b] = res.results[b]["xo"].T
    return out
```

```python
from contextlib import ExitStack
import numpy as np
import concourse.bass as bass
import concourse.mybir as mybir
from concourse.bass_utils import run_bass_kernel_spmd

F32 = mybir.dt.float32
BF16 = mybir.dt.bfloat16
AF = mybir.ActivationFunctionType
ALU = mybir.AluOpType
AX = mybir.AxisListType

D = 2048
DFF = 5632
KD = D // 128
KF = DFF // 128
TT = 512
EPS = 1e-6

EPOCH = 30000
SAME_ENGINE_SYNC = ("act", "dve", "pool")


class Prog:
    ENGS = ("pe", "act", "dve", "pool", "sp")

    def __init__(self, nc, prefix="", bind=None, ext=None):
        self.nc = nc
        self.prefix = prefix
        self.bind = bind
        self.ext = ext
        self.ops = {e: [] for e in self.ENGS}
        self.res = {}
        self.dma_cnt = {}
        self.dma_rr = {}
        self.dkeys = {}
        self.stack = ExitStack()
        self.nm = 0

    def sb(self, name, shape, dt):
        return self.stack.enter_context(self.nc.sbuf_tensor(self.prefix + name, list(shape), dt))

    def ps(self, name, shape, dt=F32):
        return self.stack.enter_context(self.nc.psum_tensor(self.prefix + name, list(shape), dt))

    def dram(self, name, shape, dt, kind="Internal"):
        if self.bind is not None and name in self.bind:
            ap = self.bind[name]
            assert list(ap.shape) == list(shape), (name, ap.shape, shape)
            return ap
        if self.bind is not None and kind == "ExternalOutput":
            raise AssertionError(f"unbound output {name}")
        if self.ext is not None and kind == "ExternalInput":
            self.ext.append((self.prefix + name, name))
        return self.nc.dram_tensor(self.prefix + name, list(shape), dt, kind=kind).ap()

    def _deps(self, reads, writes):
        deps = []
        for r in reads:
            st = self.res.get(r)
            if st and st["w"] is not None:
                deps.append(st["w"])
        for w in writes:
            st = self.res.get(w)
            if st:
                if st["w"] is not None:
                    deps.append(st["w"])
                deps.extend(st["r"])
        return deps

    def _commit(self, tok, reads, writes):
        for r in reads:
            st = self.res.setdefault(r, {"w": None, "r": []})
            st["r"].append(tok)
        for w in writes:
            self.res[w] = {"w": tok, "r": []}

    def op(self, eng, fn, reads=(), writes=()):
        deps = self._deps(reads, writes)
        idx = len(self.ops[eng])
        self.ops[eng].append({"fn": fn, "deps": deps, "sig": False, "dma": None})
        self._commit(("e", eng, idx), reads, writes)

    NSLOT = 20

    def dma(self, eng, fn, stream, reads=(), writes=()):
        deps = self._deps(reads, writes)
        k = self.dma_rr.get(eng, 0)
        self.dma_rr[eng] = k + 1
        slot = (eng, k % self.NSLOT)
        n = self.dma_cnt.get(slot, 0) + 1
        self.dma_cnt[slot] = n
        if n > 1:
            deps.append(("d", slot, 16 * (n - 1)))
        self.ops[eng].append({"fn": fn, "deps": deps, "sig": False, "dma": slot})
        self._commit(("d", slot, 16 * n), reads, writes)

    def emit(self, final_waits=()):
        nc = self.nc
        for e in self.ENGS:
            for o in self.ops[e]:
                for d in o["deps"]:
                    if d[0] == "e":
                        if d[1] == e and e not in SAME_ENGINE_SYNC:
                            continue
                        self.ops[d[1]][d[2]]["sig"] = True
        sigval = {}
        nep = {}
        for e in self.ENGS:
            n = 0
            for i, o in enumerate(self.ops[e]):
                if o["sig"]:
                    sigval[(e, i)] = (n // EPOCH, n % EPOCH + 1)
                    n += 1
            nep[e] = (n + EPOCH - 1) // EPOCH
        sems = {}
        for e in self.ENGS:
            for k in range(nep[e]):
                sems[("e", e, k)] = nc.alloc_semaphore(name=f"{self.prefix}s_{e}_{k}")
        for s in self.dma_cnt:
            sems[("d", s)] = nc.alloc_semaphore(name=f"{self.prefix}d_{s[0]}_{s[1]}")
        self.sem_handles = list(sems.values())
        ops = self.ops
        dma_cnt = self.dma_cnt

        def run(e, eng):
            clock = {}
            for i, o in enumerate(ops[e]):
                need = {}
                for d in o["deps"]:
                    if d[0] == "e":
                        if d[1] == e and e not in SAME_ENGINE_SYNC:
                            continue
                        key = ("e", d[1])
                        val = sigval[(d[1], d[2])]
                    else:
                        key = ("d", d[1])
                        val = (0, d[2])
                    if clock.get(key, (-1, 0)) >= val:
                        continue
                    if need.get(key, (-1, 0)) < val:
                        need[key] = val
                for key, val in need.items():
                    clock[key] = val
                    if key[0] == "e":
                        eng.wait_ge(sems[("e", key[1], val[0])], val[1])
                    else:
                        eng.wait_ge(sems[("d", key[1])], val[1])
                ins = o["fn"](eng)
                if o["dma"] is not None:
                    ins.then_inc(sems[("d", o["dma"])], 16)
                elif o["sig"]:
                    ins.then_inc(sems[("e", e, sigval[(e, i)][0])], 1)
            if e == "sp":
                for s in dma_cnt:
                    if clock.get(("d", s), (-1, 0)) < (0, 16 * dma_cnt[s]):
                        eng.wait_ge(sems[("d", s)], 16 * dma_cnt[s])

        with nc.Block() as block:
            @block.tensor
            def _(eng):
                run("pe", eng)

            @block.scalar
            def _(eng):
                run("act", eng)

            @block.vector
            def _(eng):
                run("dve", eng)

            @block.gpsimd
            def _(eng):
                run("pool", eng)

            @block.sync
            def _(eng):
                run("sp", eng)
        self.stack.close()


class Conv:
    def __init__(self, P, width=1024, nbuf=3):
        self.P = P
        self.w = width
        self.nb = nbuf
        self.f = [P.sb(f"cvf{i}", [128, width], F32) for i in range(nbuf)]
        self.b = [P.sb(f"cvb{i}", [128, width], BF16) for i in range(nbuf)]
        self.i = 0

    def run(self, dst, src, tag):
        P = self.P
        n = src.shape[1]
        c0 = 0
        keys = P.dkeys.setdefault(tag, [])
        while c0 < n:
            w = min(self.w, n - c0)
            i = self.i % self.nb
            self.i += 1
            f, b = self.f[i], self.b[i]
            s_ap = src[:, c0:c0 + w]
            d_ap = dst[:, c0:c0 + w]
            P.dma("sp", lambda e, f=f, s_ap=s_ap, w=w: e.dma_start(out=f[:, :w], in_=s_ap),
                  "cvl", writes=[("cvf", i)])
            eng = ("dve", "pool", "act")[self.i % 3]
            if eng == "act":
                P.op("act", lambda e, f=f, b=b, w=w: e.copy(out=b[:, :w], in_=f[:, :w]),
                     reads=[("cvf", i)], writes=[("cvb", i)])
            else:
                P.op(eng, lambda e, f=f, b=b, w=w: e.tensor_copy(out=b[:, :w], in_=f[:, :w]),
                     reads=[("cvf", i)], writes=[("cvb", i)])
            key = ("dram", tag, len(keys))
            keys.append(key)
            P.dma("sp", lambda e, b=b, d_ap=d_ap, w=w: e.dma_start(out=d_ap, in_=b[:, :w]),
                  "cvs", reads=[("cvb", i)], writes=[key])
            c0 += w


def rms_rstd(P, S, src_chunks, src_res, nk, out_rstd, out_res, dim):
    for k in range(nk):
        j = S.sqi % 2
        S.sqi += 1
        sq = S.sq[j]
        src = src_chunks[k]
        P.op("act", lambda e, sq=sq, src=src: e.activation(out=sq[:], in_=src, func=AF.Square),
             reads=[src_res[k]], writes=[("sq", j)])
        P.op("pe", lambda e, sq=sq, k=k: e.matmul(S.ps_ss[:], lhsT=S.ones[:], rhs=sq[:],
                                                  start=(k == 0), stop=(k == nk - 1)),
             reads=[("sq", j)], writes=[("ps_ss",)])
    P.op("act", lambda e: e.activation(out=S.lnt[:], in_=S.ps_ss[:], func=AF.Ln,
                                       bias=S.epsb[:], scale=1.0 / dim),
         reads=[("ps_ss",)], writes=[("lnt",)])
    P.op("act", lambda e: e.activation(out=out_rstd[:], in_=S.lnt[:], func=AF.Exp, scale=-0.5),
         reads=[("lnt",)], writes=[out_res])


class TokState:
    pass


def build_tok(n_tiles, mix_kc, n_ffn, emit_hm, P=None):
    if P is None:
        P = Prog(bass.Bass("TRN2", target_bir_lowering=False))
    nc = P.nc
    NT = n_tiles * TT
    n_g = (1 if mix_kc else 0) + 2 * n_ffn + (1 if emit_hm else 0)
    xT = P.dram("xT", [D, NT], F32, "ExternalInput")
    g_all = P.dram("g_all", [128, n_g * KD], F32, "ExternalInput")
    xo = P.dram("xo", [D, NT], F32, "ExternalOutput")
    if emit_hm:
        hm_o = P.dram("hm", [D, NT], BF16, "ExternalOutput")
    if mix_kc:
        oT = P.dram("oT", [mix_kc * 128, NT], BF16, "ExternalInput")
        wo_f = P.dram("wout", [KD, 128, mix_kc * 128], F32, "ExternalInput")
        wo_b = P.dram("wout_b", [KD, 128, mix_kc * 128], BF16)
    wg_f, wu_f, wd_f, wg_b, wu_b, wd_b = [], [], [], [], [], []
    for i in range(n_ffn):
        wg_f.append(P.dram(f"wg{i}", [KF, 128, KD * 128], F32, "ExternalInput"))
        wu_f.append(P.dram(f"wu{i}", [KF, 128, KD * 128], F32, "ExternalInput"))
        wd_f.append(P.dram(f"wd{i}", [KD, 128, KF * 128], F32, "ExternalInput"))
        wg_b.append(P.dram(f"wg{i}_b", [KF, 128, KD * 128], BF16))
        wu_b.append(P.dram(f"wu{i}_b", [KF, 128, KD * 128], BF16))
        wd_b.append(P.dram(f"wd{i}_b", [KD, 128, KF * 128], BF16))

    S = TokState()
    S.sqi = 0
    S.ones = P.sb("ones", [128, 128], BF16)
    S.epsb = P.sb("epsb", [128, 1], F32)
    S.g = P.sb("sb_g", [128, n_g * KD], F32)
    S.x = P.sb("sb_x", [128, KD, TT], F32)
    S.xn = P.sb("sb_xn", [128, KD, TT], BF16)
    S.y = P.sb("sb_y", [128, KD, TT], F32)
    S.h = P.sb("sb_h", [128, KF, TT], BF16)
    S.sq = [P.sb(f"sq{i}", [128, TT], BF16) for i in range(2)]
    S.lnt = P.sb("lnt", [128, TT], F32)
    S.rstd = P.sb("rstd", [128, TT], F32)
    S.sg = [P.sb(f"sg{i}", [128, TT], F32) for i in range(2)]
    S.wgu = [P.sb(f"sb_wgu{i}", [128, 2, KD * 128], BF16) for i in range(2)]
    S.wd = [P.sb(f"sb_wd{i}", [128, KF * 128], BF16) for i in range(2)]
    S.ps_ss = P.ps("ps_ss", [128, TT])
    S.ps_g = [P.ps(f"ps_g{i}", [128, TT]) for i in range(2)]
    S.ps_u = [P.ps(f"ps_u{i}", [128, TT]) for i in range(2)]
    S.ps_y = [P.ps(f"ps_y{i}", [128, TT]) for i in range(2)]
    cv = Conv(P)

    P.op("dve", lambda e: e.memset(S.ones[:], 1.0), writes=[("ones",)])
    P.op("dve", lambda e: e.memset(S.epsb[:], EPS), writes=[("epsb",)])
    P.dma("pool", lambda e: e.dma_start(out=S.g[:], in_=g_all), "gl", writes=[("g",)])
    if mix_kc:
        for o in range(KD):
            cv.run(wo_b[o], wo_f[o], "wo")
    for i in range(n_ffn):
        for f in range(KF):
            cv.run(wg_b[i][f], wg_f[i][f], f"wg{i}")
            cv.run(wu_b[i][f], wu_f[i][f], f"wu{i}")
        for o in range(KD):
            cv.run(wd_b[i][o], wd_f[i][o], f"wd{i}")

    cnt = {"wgu": 0, "wd": 0, "psg": 0, "psy": 0}

    def proj_norm_res(src, src_res, nk, w_b, w_tag, gcol, coef):
        for o in range(KD):
            j = cnt["wd"] % 2
            cnt["wd"] += 1
            wt = S.wd[j]
            P.dma("sp", lambda e, wt=wt, o=o: e.dma_start(out=wt[:, :nk * 128], in_=w_b[o]),
                  f"wd{j}", reads=P.dkeys[w_tag], writes=[("wd", j)])
            pj = cnt["psy"] % 2
            cnt["psy"] += 1
            py = S.ps_y[pj]
            for k in range(nk):
                P.op("pe", lambda e, py=py, wt=wt, k=k: e.matmul(
                    py[:], lhsT=wt[:, k * 128:(k + 1) * 128], rhs=src[:, k, :],
                    start=(k == 0), stop=(k == nk - 1)),
                    reads=[("wd", j), src_res], writes=[("psy", pj)])
            P.op("act", lambda e, py=py, o=o: e.copy(out=S.y[:, o, :], in_=py[:]),
                 reads=[("psy", pj)], writes=[("y", o)])
        rms_rstd(P, S, [S.y[:, k, :] for k in range(KD)], [("y", k) for k in range(KD)], KD,
                 S.rstd, ("rstd",), D)
        for k in range(KD):
            P.op("dve", lambda e, k=k: e.scalar_tensor_tensor(
                out=S.y[:, k, :], in0=S.y[:, k, :], scalar=S.g[:, gcol * KD + k:gcol * KD + k + 1],
                in1=S.rstd[:], op0=ALU.mult, op1=ALU.mult),
                reads=[("y", k), ("rstd",), ("g",)], writes=[("y", k)])
            P.op("dve", lambda e, k=k: e.scalar_tensor_tensor(
                out=S.x[:, k, :], in0=S.y[:, k, :], scalar=float(coef),
                in1=S.x[:, k, :], op0=ALU.mult, op1=ALU.add),
                reads=[("y", k), ("x", k)], writes=[("x", k)])

    def norm_to(dst, dst_res, gcol):
        rms_rstd(P, S, [S.x[:, k, :] for k in range(KD)], [("x", k) for k in range(KD)], KD,
                 S.rstd, ("rstd",), D)
        for k in range(KD):
            P.op("dve", lambda e, k=k: e.scalar_tensor_tensor(
                out=dst[:, k, :], in0=S.x[:, k, :], scalar=S.g[:, gcol * KD + k:gcol * KD + k + 1],
                in1=S.rstd[:], op0=ALU.mult, op1=ALU.mult),
                reads=[("x", k), ("rstd",), ("g",)], writes=[dst_res])

    def ffn(i, gcol):
        norm_to(S.xn, ("xn",), gcol)
        for f in range(KF):
            j = cnt["wgu"] % 2
            cnt["wgu"] += 1
            wt = S.wgu[j]
            P.dma("sp", lambda e, wt=wt, f=f: e.dma_start(out=wt[:, 0, :], in_=wg_b[i][f]),
                  f"wgu{j}", reads=P.dkeys[f"wg{i}"], writes=[("wgu", j)])
            P.dma("sp", lambda e, wt=wt, f=f: e.dma_start(out=wt[:, 1, :], in_=wu_b[i][f]),
                  f"wgu{j}", reads=P.dkeys[f"wu{i}"], writes=[("wgu", j)])
            pj = cnt["psg"] % 2
            cnt["psg"] += 1
            pg, pu, sg = S.ps_g[pj], S.ps_u[pj], S.sg[pj]
            for k in range(KD):
                P.op("pe", lambda e, pg=pg, wt=wt, k=k: e.matmul(
                    pg[:], lhsT=wt[:, 0, k * 128:(k + 1) * 128], rhs=S.xn[:, k, :],
                    start=(k == 0), stop=(k == KD - 1)),
                    reads=[("wgu", j), ("xn",)], writes=[("psg", pj)])
            for k in range(KD):
                P.op("pe", lambda e, pu=pu, wt=wt, k=k: e.matmul(
                    pu[:], lhsT=wt[:, 1, k * 128:(k + 1) * 128], rhs=S.xn[:, k, :],
                    start=(k == 0), stop=(k == KD - 1)),
                    reads=[("wgu", j), ("xn",)], writes=[("psu", pj)])
            P.op("act", lambda e, pg=pg, sg=sg: e.activation(out=sg[:], in_=pg[:], func=AF.Silu),
                 reads=[("psg", pj)], writes=[("sg", pj)])
            P.op("dve", lambda e, pu=pu, sg=sg, f=f: e.tensor_tensor(
                out=S.h[:, f, :], in0=sg[:], in1=pu[:], op=ALU.mult),
                reads=[("sg", pj), ("psu", pj)], writes=[("h",)])
        proj_norm_res(S.h, ("h",), KF, wd_b[i], f"wd{i}", gcol + 1, 0.5)

    for t in range(n_tiles):
        ts = slice(t * TT, (t + 1) * TT)
        for k in range(KD):
            P.dma("pool", lambda e, k=k, ts=ts: e.dma_start(out=S.x[:, k, :], in_=xT[k * 128:(k + 1) * 128, ts]),
                  "xl", writes=[("x", k)])
        gc = 0
        if mix_kc:
            src = S.h
            for k in range(mix_kc):
                P.dma("pool", lambda e, k=k, ts=ts: e.dma_start(out=S.h[:, k, :], in_=oT[k * 128:(k + 1) * 128, ts]),
                      "ol", writes=[("h",)])
            proj_norm_res(S.h, ("h",), mix_kc, wo_b, "wo", gc, 1.0)
            gc += 1
        for i in range(n_ffn):
            ffn(i, gc)
            gc += 2
        if emit_hm:
            norm_to(S.xn, ("xn",), gc)
            for k in range(KD):
                P.dma("pool", lambda e, k=k, ts=ts: e.dma_start(out=hm_o[k * 128:(k + 1) * 128, ts], in_=S.xn[:, k, :]),
                      "hs", reads=[("xn",)], writes=[("dram", "hm")])
        for k in range(KD):
            P.dma("pool", lambda e, k=k, ts=ts: e.dma_start(out=xo[k * 128:(k + 1) * 128, ts], in_=S.x[:, k, :]),
                  "xs", reads=[("x", k)], writes=[("dram", "xo")])
    P.emit()
    return nc


def tile_w_in_out(W, kc_in, kc_out):
    return np.ascontiguousarray(
        W.reshape(kc_in, 128, kc_out, 128).transpose(2, 1, 0, 3).reshape(kc_out, 128, kc_in * 128))


def gain_cols(g):
    return np.ascontiguousarray(g.reshape(-1, 128).T)


RH = 4
RDK = 256
RDV = 512


def build_ret(S_len, P=None):
    if P is None:
        P = Prog(bass.Bass("TRN2", target_bir_lowering=False))
    nc = P.nc
    n_tiles = S_len // TT
    hmT = P.dram("hmT", [D, S_len], BF16, "ExternalInput")
    wq_f = P.dram("wq", [2 * RH, 128, D], F32, "ExternalInput")
    wk_f = P.dram("wk", [2 * RH, 128, D], F32, "ExternalInput")
    wv_f = P.dram("wv", [RH, 128, KD * RDV], F32, "ExternalInput")
    wg_f = P.dram("wgt", [RH, 128, KD * RDV], F32, "ExternalInput")
    wq_b = P.dram("wq_b", [2 * RH, 128, D], BF16)
    wk_b = P.dram("wk_b", [2 * RH, 128, D], BF16)
    wv_b = P.dram("wv_b", [RH, 128, KD * RDV], BF16)
    wg_b = P.dram("wg_b", [RH, 128, KD * RDV], BF16)
    cosq = P.dram("cosq", [128, S_len], F32, "ExternalInput")
    sinq = P.dram("sinq", [128, S_len], F32, "ExternalInput")
    cosk = P.dram("cosk", [128, S_len], F32, "ExternalInput")
    sink = P.dram("sink", [128, S_len], F32, "ExternalInput")
    c_inner = P.dram("c_inner", [128, RH * 128], F32, "ExternalInput")
    c_cross = P.dram("c_cross", [128, RH * 128], F32, "ExternalInput")
    c_misc = P.dram("c_misc", [128, RH * 2], F32, "ExternalInput")
    c_ident = P.dram("c_ident", [128, 128], BF16, "ExternalInput")
    oT = P.dram("oT", [RH * RDV, S_len], BF16, "ExternalOutput")

    hm = P.sb("r_hm", [128, KD, TT], BF16)
    wfm = [P.sb(f"r_wfm{i}", [128, D], BF16) for i in range(2)]
    wtm = [P.sb(f"r_wtm{i}", [128, KD * RDV], BF16) for i in range(2)]
    inner = P.sb("r_inner", [128, RH * 128], F32)
    cross = P.sb("r_cross", [128, RH * 128], F32)
    misc = P.sb("r_misc", [128, RH * 2], F32)
    ident = P.sb("r_ident", [128, 128], BF16)
    epsb = P.sb("r_epsb", [128, 1], F32)
    tab = P.sb("r_tab", [128, 4, TT], F32)
    raw = P.sb("r_raw", [128, 2, TT], F32)
    tmp = P.sb("r_tmp", [128, 2, TT], F32)
    qT = P.sb("r_qT", [128, RH, 2, TT], BF16)
    kT = P.sb("r_kT", [128, RH, 2, TT], BF16)
    v_sb = P.sb("r_v", [128, 4, RH, RDV], BF16)
    g_sb = P.sb("r_g", [128, 4, RH, RDV], BF16)
    ktl = P.sb("r_ktl", [128, RDK], BF16)
    attT = P.sb("r_attT", [128, 128], BF16)
    qs = P.sb("r_qs", [128, 2, 128], BF16)
    state = P.sb("r_state", [128, RH, 2, RDV], F32)
    state_b = P.sb("r_state_b", [128, RH, 2, RDV], BF16)
    stats = P.sb("r_stats", [128, 6], F32)
    mv = P.sb("r_mv", [128, 2], F32)
    rstd = P.sb("r_rstd", [128, 1], F32)
    yn = P.sb("r_yn", [128, RDV], F32)
    o_tm = P.sb("r_otm", [128, RDV], BF16)
    o_fm = P.sb("r_ofm", [128, RH * 4, TT], BF16)
    ps_a = [P.ps(f"r_psa{i}", [128, TT]) for i in range(2)]
    ps_y = P.ps("r_psy", [128, RDV])
    ps_st = [P.ps(f"r_psst{i}", [128, RDV]) for i in range(2)]
    ps_att = P.ps("r_psatt", [128, 128])
    ps_tr = [P.ps(f"r_pstr{i}", [128, 512], BF16) for i in range(2)]
    cv = Conv(P)

    P.op("dve", lambda e: e.memset(epsb[:], EPS), writes=[("epsb",)])
    P.op("dve", lambda e: e.memset(state[:], 0.0), writes=[("state", h, hf) for h in range(RH) for hf in range(2)])
    P.op("pool", lambda e: e.memset(state_b[:], 0.0), writes=[("state_b", h, hf) for h in range(RH) for hf in range(2)])
    P.dma("pool", lambda e: e.dma_start(out=inner[:], in_=c_inner), "cl", writes=[("inner",)])
    P.dma("pool", lambda e: e.dma_start(out=cross[:], in_=c_cross), "cl", writes=[("cross",)])
    P.dma("pool", lambda e: e.dma_start(out=misc[:], in_=c_misc), "cl", writes=[("misc",)])
    P.dma("pool", lambda e: e.dma_start(out=ident[:], in_=c_ident), "cl", writes=[("ident",)])
    for c in range(2 * RH):
        cv.run(wq_b[c], wq_f[c], "wq")
        cv.run(wk_b[c], wk_f[c], "wk")
    for h in range(RH):
        cv.run(wv_b[h], wv_f[h], "wv")
        cv.run(wg_b[h], wg_f[h], "wg")

    cnt = {"wfm": 0, "wtm": 0, "psa": 0, "pstr": 0, "psst": 0}

    for t in range(n_tiles):
        ts = slice(t * TT, (t + 1) * TT)
        for k in range(KD):
            P.dma("pool", lambda e, k=k, ts=ts: e.dma_start(out=hm[:, k, :], in_=hmT[k * 128:(k + 1) * 128, ts]),
                  "hl", writes=[("hm",)])
        for i, src in enumerate((cosq, sinq, cosk, sink)):
            P.dma("pool", lambda e, i=i, src=src, ts=ts: e.dma_start(out=tab[:, i, :], in_=src[:, ts]),
                  "tl", writes=[("tab",)])
        for which, w_b, dst, tag, ci, si in (("q", wq_b, qT, "wq", 0, 1), ("k", wk_b, kT, "wk", 2, 3)):
            for h in range(RH):
                for hf in range(2):
                    j = cnt["wfm"] % 2
                    cnt["wfm"] += 1
                    wt = wfm[j]
                    P.dma("sp", lambda e, wt=wt, w_b=w_b, c=2 * h + hf: e.dma_start(out=wt[:], in_=w_b[c]),
                          f"wfm{j}", reads=P.dkeys[tag], writes=[("wfm", j)])
                    pj = cnt["psa"] % 2
                    cnt["psa"] += 1
                    pa = ps_a[pj]
                    for k in range(KD):
                        P.op("pe", lambda e, pa=pa, wt=wt, k=k: e.matmul(
                            pa[:], lhsT=wt[:, k * 128:(k + 1) * 128], rhs=hm[:, k, :],
                            start=(k == 0), stop=(k == KD - 1)),
                            reads=[("wfm", j), ("hm",)], writes=[("psa", pj)])
                    P.op("act", lambda e, pa=pa, hf=hf: e.copy(out=raw[:, hf, :], in_=pa[:]),
                         reads=[("psa", pj)], writes=[("raw", hf)])
                P.op("dve", lambda e, ci=ci: e.tensor_tensor(out=tmp[:, 0, :], in0=raw[:, 0, :], in1=tab[:, ci, :], op=ALU.mult),
                     reads=[("raw", 0), ("tab",)], writes=[("tmp", 0)])
                P.op("dve", lambda e, si=si: e.tensor_tensor(out=tmp[:, 1, :], in0=raw[:, 1, :], in1=tab[:, si, :], op=ALU.mult),
                     reads=[("raw", 1), ("tab",)], writes=[("tmp", 1)])
                P.op("dve", lambda e, dst=dst, h=h: e.tensor_tensor(out=dst[:, h, 0, :], in0=tmp[:, 0, :], in1=tmp[:, 1, :], op=ALU.subtract),
                     reads=[("tmp", 0), ("tmp", 1)], writes=[(which, h)])
                P.op("dve", lambda e, ci=ci: e.tensor_tensor(out=tmp[:, 0, :], in0=raw[:, 1, :], in1=tab[:, ci, :], op=ALU.mult),
                     reads=[("raw", 1), ("tab",), (which, h)], writes=[("tmp", 0)])
                P.op("dve", lambda e, si=si: e.tensor_tensor(out=tmp[:, 1, :], in0=raw[:, 0, :], in1=tab[:, si, :], op=ALU.mult),
                     reads=[("raw", 0), ("tab",), (which, h)], writes=[("tmp", 1)])
                P.op("dve", lambda e, dst=dst, h=h: e.tensor_tensor(out=dst[:, h, 1, :], in0=tmp[:, 0, :], in1=tmp[:, 1, :], op=ALU.add),
                     reads=[("tmp", 0), ("tmp", 1)], writes=[(which, h)])
        for which, w_b, tag in (("v", wv_b, "wv"), ("g", wg_b, "wg")):
            for h in range(RH):
                j = cnt["wtm"] % 2
                cnt["wtm"] += 1
                wt = wtm[j]
                P.dma("sp", lambda e, wt=wt, w_b=w_b, h=h: e.dma_start(out=wt[:], in_=w_b[h]),
                      f"wtm{j}", reads=P.dkeys[tag], writes=[("wtm", j)])
                for sub in range(4):
                    pj = cnt["psa"] % 2
                    cnt["psa"] += 1
                    pa = ps_a[pj]
                    for k in range(KD):
                        P.op("pe", lambda e, pa=pa, wt=wt, k=k, sub=sub: e.matmul(
                            pa[:], lhsT=hm[:, k, sub * 128:(sub + 1) * 128], rhs=wt[:, k * RDV:(k + 1) * RDV],
                            start=(k == 0), stop=(k == KD - 1)),
                            reads=[("wtm", j), ("hm",)], writes=[("psa", pj)])
                    if which == "v":
                        P.op("act", lambda e, pa=pa, sub=sub, h=h: e.copy(out=v_sb[:, sub, h, :], in_=pa[:]),
                             reads=[("psa", pj)], writes=[("v", sub, h)])
                    else:
                        P.op("act", lambda e, pa=pa, sub=sub, h=h: e.activation(out=g_sb[:, sub, h, :], in_=pa[:], func=AF.Silu),
                             reads=[("psa", pj)], writes=[("g", sub, h)])
        for sub in range(4):
            cs = slice(sub * 128, (sub + 1) * 128)
            for h in range(RH):
                pj = cnt["pstr"] % 2
                cnt["pstr"] += 1
                ptr = ps_tr[pj]
                for hf in range(2):
                    P.op("pe", lambda e, ptr=ptr, h=h, hf=hf, cs=cs: e.transpose(
                        ptr[:, hf * 128:(hf + 1) * 128], kT[:, h, hf, cs], ident[:]),
                        reads=[("k", h), ("ident",)], writes=[("pstr", pj)])
                P.op("dve", lambda e, ptr=ptr, h=h: e.tensor_scalar(
                    out=ktl[:], in0=ptr[:, 0:RDK], scalar1=misc[:, 2 * h:2 * h + 1], scalar2=None, op0=ALU.mult),
                    reads=[("pstr", pj), ("misc",)], writes=[("ktl",)])
                for hf in range(2):
                    P.op("pe", lambda e, h=h, hf=hf, cs=cs: e.matmul(
                        ps_att[:], lhsT=kT[:, h, hf, cs], rhs=qT[:, h, hf, cs], start=(hf == 0), stop=(hf == 1)),
                        reads=[("k", h), ("q", h)], writes=[("psatt",)])
                P.op("dve", lambda e, h=h: e.tensor_tensor(
                    out=attT[:], in0=ps_att[:], in1=inner[:, h * 128:(h + 1) * 128], op=ALU.mult),
                    reads=[("psatt",), ("inner",)], writes=[("attT",)])
                for hf in range(2):
                    P.op("pool", lambda e, h=h, hf=hf, cs=cs: e.tensor_tensor(
                        out=qs[:, hf, :], in0=qT[:, h, hf, cs], in1=cross[:, h * 128:(h + 1) * 128], op=ALU.mult),
                        reads=[("q", h), ("cross",)], writes=[("qs", hf)])
                P.op("pe", lambda e, sub=sub, h=h: e.matmul(
                    ps_y[:], lhsT=attT[:], rhs=v_sb[:, sub, h, :], start=True, stop=False),
                    reads=[("attT",), ("v", sub, h)], writes=[("psy",)])
                for hf in range(2):
                    P.op("pe", lambda e, h=h, hf=hf: e.matmul(
                        ps_y[:], lhsT=qs[:, hf, :], rhs=state_b[:, h, hf, :], start=False, stop=(hf == 1)),
                        reads=[("qs", hf), ("state_b", h, hf)], writes=[("psy",)])
                for hf in range(2):
                    sj = cnt["psst"] % 2
                    cnt["psst"] += 1
                    pst = ps_st[sj]
                    P.op("pe", lambda e, pst=pst, sub=sub, h=h, hf=hf: e.matmul(
                        pst[:], lhsT=ktl[:, hf * 128:(hf + 1) * 128], rhs=v_sb[:, sub, h, :], start=True, stop=True),
                        reads=[("ktl",), ("v", sub, h)], writes=[("psst", sj)])
                    P.op("dve", lambda e, pst=pst, h=h, hf=hf: e.scalar_tensor_tensor(
                        out=state[:, h, hf, :], in0=state[:, h, hf, :], scalar=misc[:, 2 * h + 1:2 * h + 2],
                        in1=pst[:], op0=ALU.mult, op1=ALU.add),
                        reads=[("psst", sj), ("state", h, hf), ("misc",)], writes=[("state", h, hf)])
                    P.op("act", lambda e, h=h, hf=hf: e.copy(out=state_b[:, h, hf, :], in_=state[:, h, hf, :]),
                         reads=[("state", h, hf)], writes=[("state_b", h, hf)])
                P.op("dve", lambda e: e.bn_stats(out=stats[:], in_=ps_y[:]),
                     reads=[("psy",)], writes=[("stats",)])
                P.op("dve", lambda e: e.bn_aggr(out=mv[:], in_=stats[:]),
                     reads=[("stats",)], writes=[("mv",)])
                P.op("act", lambda e: e.activation(out=rstd[:], in_=mv[:, 1:2], func=AF.Ln, bias=epsb[:], scale=1.0),
                     reads=[("mv",), ("epsb",)], writes=[("rstd",)])
                P.op("act", lambda e: e.activation(out=rstd[:], in_=rstd[:], func=AF.Exp, scale=-0.5),
                     reads=[("rstd",)], writes=[("rstd",)])
                P.op("dve", lambda e: e.tensor_scalar(
                    out=yn[:], in0=ps_y[:], scalar1=mv[:, 0:1], scalar2=rstd[:, 0:1], op0=ALU.subtract, op1=ALU.mult),
                    reads=[("psy",), ("mv",), ("rstd",)], writes=[("yn",)])
                P.op("pool", lambda e, sub=sub, h=h: e.tensor_tensor(
                    out=o_tm[:], in0=yn[:], in1=g_sb[:, sub, h, :], op=ALU.mult),
                    reads=[("yn",), ("g", sub, h)], writes=[("otm",)])
                pj = cnt["pstr"] % 2
                cnt["pstr"] += 1
                ptr = ps_tr[pj]
                for c in range(4):
                    P.op("pe", lambda e, ptr=ptr, c=c: e.transpose(
                        ptr[:, c * 128:(c + 1) * 128], o_tm[:, c * 128:(c + 1) * 128], ident[:]),
                        reads=[("otm",), ("ident",)], writes=[("pstr", pj)])
                for c in range(4):
                    P.op("act", lambda e, ptr=ptr, c=c, h=h, cs=cs: e.copy(
                        out=o_fm[:, h * 4 + c, cs], in_=ptr[:, c * 128:(c + 1) * 128]),
                        reads=[("pstr", pj)], writes=[("ofm",)])
        for c in range(RH * 4):
            P.dma("pool", lambda e, c=c, ts=ts: e.dma_start(out=oT[c * 128:(c + 1) * 128, ts], in_=o_fm[:, c, :]),
                  "os", reads=[("ofm",)], writes=[("dram", "oT")])
    P.emit()
    return nc


def ret_consts(h0):
    L = 128
    idx = np.arange(L, dtype=np.float32)
    lg = np.log1p(-np.exp2(-5.0 - np.arange(8, dtype=np.float32))).astype(np.float32)
    inner = np.zeros((128, RH * 128), np.float32)
    cross = np.zeros((128, RH * 128), np.float32)
    misc = np.zeros((128, RH * 2), np.float32)
    for h in range(RH):
        g = lg[h0 + h]
        diff = idx[:, None] - idx[None, :]
        dec = np.where(diff >= 0, np.exp(np.maximum(diff, 0.0) * g), 0.0).astype(np.float32)
        inner[:, h * 128:(h + 1) * 128] = dec.T
        cross[:, h * 128:(h + 1) * 128] = np.exp((idx + 1.0) * g)[None, :]
        misc[:, 2 * h] = np.exp((L - 1.0 - idx) * g)
        misc[:, 2 * h + 1] = np.exp(L * g)
    return inner, cross, misc


def rope_tables_T(S_len, dim, scale):
    inv = (1.0 / (10000.0 ** (np.arange(0, dim, 2, dtype=np.float32) / np.float32(dim)))).astype(np.float32)
    ang = np.arange(S_len, dtype=np.float32)[:, None] * inv[None, :]
    return (np.ascontiguousarray(np.cos(ang).T.astype(np.float32)) * np.float32(scale),
            np.ascontiguousarray(np.sin(ang).T.astype(np.float32)) * np.float32(scale))


def tile_w_fm(W):
    n = W.shape[1] // 128
    return tile_w_in_out(W, KD, n)


def tile_w_tm(W, ncol):
    g = W.shape[1] // ncol
    return np.ascontiguousarray(W.reshape(KD, 128, g, ncol).transpose(2, 1, 0, 3).reshape(g, 128, KD * ncol))


def ret_inputs(hm_bf_T, od_w_in, h0, S_len):
    import ml_dtypes
    W = od_w_in
    q0, k0, v0, g0 = 0, 2048, 4096, 8192
    wq = W[:, q0 + h0 * RDK: q0 + (h0 + RH) * RDK]
    wk = W[:, k0 + h0 * RDK: k0 + (h0 + RH) * RDK]
    wv = W[:, v0 + h0 * RDV: v0 + (h0 + RH) * RDV]
    wg = W[:, g0 + h0 * RDV: g0 + (h0 + RH) * RDV]
    cq, sq = rope_tables_T(S_len, RDK, 1.0)
    ck, sk = rope_tables_T(S_len, RDK, RDK ** -0.5)
    inner, cross, misc = ret_consts(h0)
    return {"hmT": hm_bf_T, "wq": tile_w_fm(wq), "wk": tile_w_fm(wk), "wv": tile_w_tm(wv, RDV),
            "wgt": tile_w_tm(wg, RDV), "cosq": cq, "sinq": sq, "cosk": ck, "sink": sk,
            "c_inner": inner, "c_cross": cross, "c_misc": misc,
            "c_ident": np.eye(128, dtype=np.float32).astype(ml_dtypes.bfloat16)}


SH = 8
NEG = -30000.0


def build_ssd(S_len, dbg=99, P=None):
    if P is None:
        P = Prog(bass.Bass("TRN2", target_bir_lowering=False))
    nc = P.nc
    n_tiles = S_len // TT
    hmT = P.dram("hmT", [D, S_len], BF16, "ExternalInput")
    wfm_f = P.dram("wfm", [8, 128, D], F32, "ExternalInput")
    wz_f = P.dram("wz", [1, 128, KD * 512], F32, "ExternalInput")
    wdt_f = P.dram("wdt", [1, 128, KD * 8], F32, "ExternalInput")
    wfm_b = P.dram("wfm_b", [8, 128, D], BF16)
    wz_b = P.dram("wz_b", [1, 128, KD * 512], BF16)
    wdt_b = P.dram("wdt_b", [1, 128, KD * 8], BF16)
    c_conv = P.dram("c_conv", [128, 8 * 5], F32, "ExternalInput")
    c_hp = P.dram("c_hp", [128, 3 * 8], F32, "ExternalInput")
    c_nw = P.dram("c_nw", [128, 512], F32, "ExternalInput")
    c_U = P.dram("c_U", [128, 128], F32, "ExternalInput")
    c_nm = P.dram("c_nm", [128, 128], F32, "ExternalInput")
    c_idf = P.dram("c_idf", [128, 128], F32, "ExternalInput")
    c_ident = P.dram("c_ident", [128, 128], BF16, "ExternalInput")
    oT = P.dram("oT", [512, S_len], BF16, "ExternalOutput")

    hm = P.sb("s_hm", [128, KD, TT], BF16)
    wfm = [P.sb(f"s_wfm{i}", [128, D], BF16) for i in range(2)]
    wz = P.sb("s_wz", [128, KD * 512], BF16)
    wdt = P.sb("s_wdt", [128, KD * 8], BF16)
    conv = P.sb("s_conv", [128, 40], F32)
    hp = P.sb("s_hp", [128, 24], F32)
    nw = P.sb("s_nw", [128, 512], F32)
    U = P.sb("s_U", [128, 128], F32)
    nm = P.sb("s_nm", [128, 128], F32)
    idf = P.sb("s_idf", [128, 128], F32)
    ident = P.sb("s_ident", [128, 128], BF16)
    ones = P.sb("s_ones", [128, 128], F32)
    epsb = P.sb("s_epsb", [128, 1], F32)
    oneb = P.sb("s_oneb", [128, 1], F32)
    aneg = P.sb("s_aneg", [128, 8], F32)
    xc = P.sb("s_xc", [128, 8, TT + 3], F32)
    acc = P.sb("s_acc", [128, TT], F32)
    xact = P.sb("s_xact", [128, 8, TT], BF16)
    zs = P.sb("s_zs", [128, 4, 512], F32)
    dt = P.sb("s_dt", [128, 4, 8], F32)
    la = P.sb("s_la", [128, 4, 8], F32)
    cum = P.sb("s_cum", [128, 8], F32)
    ncum = P.sb("s_ncum", [128, 8], F32)
    ecum = P.sb("s_ecum", [128, 8], F32)
    cbT = P.sb("s_cbT", [128, 2, 128], F32)
    Btm = P.sb("s_Btm", [128, 2, 128], BF16)
    xs_tm = P.sb("s_xstm", [128, 512], BF16)
    LAb = [P.sb(f"s_LAb{i}", [128, 128], F32) for i in range(2)]
    decT = [P.sb(f"s_decT{i}", [128, 128], F32) for i in range(2)]
    MT = [P.sb(f"s_MT{i}", [128, 128], BF16) for i in range(2)]
    xd = [P.sb(f"s_xd{i}", [128, 64], BF16) for i in range(2)]
    wxd = [P.sb(f"s_wxd{i}", [128, 64], BF16) for i in range(2)]
    ecl = [P.sb(f"s_ecl{i}", [128, 1], F32) for i in range(2)]
    ysb = [P.sb(f"s_ysb{i}", [128, 64], F32) for i in range(2)]
    state = P.sb("s_state", [128, SH, 64], F32)
    state_b = P.sb("s_state_b", [128, SH, 64], BF16)
    y_all = P.sb("s_yall", [128, 512], F32)
    sqj = P.sb("s_sqj", [128, 256], F32)
    ss = P.sb("s_ss", [128, 2], F32)
    o_tm = P.sb("s_otm", [128, 512], BF16)
    o_fm = P.sb("s_ofm", [128, 4, TT], BF16)
    ps_a = [P.ps(f"s_psa{i}", [128, TT]) for i in range(2)]
    ps_seg = [P.ps(f"s_psseg{i}", [128, 128]) for i in range(2)]
    ps_cb = P.ps("s_pscb", [128, 128])
    ps_y = [P.ps(f"s_psy{i}", [128, 192]) for i in range(2)]
    ps_tr = P.ps("s_pstr", [128, 512], BF16)
    cv = Conv(P)

    P.op("dve", lambda e: e.memset(epsb[:], EPS), writes=[("epsb",)])
    P.op("dve", lambda e: e.memset(oneb[:], 1.0), writes=[("oneb",)])
    P.op("dve", lambda e: e.memset(ones[:], 1.0), writes=[("ones",)])
    P.op("dve", lambda e: e.memset(state[:], 0.0), writes=[("state", h) for h in range(SH)])
    P.op("pool", lambda e: e.memset(state_b[:], 0.0), writes=[("state_b", h) for h in range(SH)])
    P.op("pool", lambda e: e.memset(xc[:], 0.0), writes=[("xc", c) for c in range(8)])
    for dst, src, tag in ((conv, c_conv, "conv"), (hp, c_hp, "hp"), (nw, c_nw, "nw"), (U, c_U, "U"),
                          (nm, c_nm, "nm"), (idf, c_idf, "idf"), (ident, c_ident, "ident")):
        P.dma("pool", lambda e, dst=dst, src=src: e.dma_start(out=dst[:], in_=src), "cl", writes=[(tag,)])
    P.op("act", lambda e: e.activation(out=aneg[:], in_=hp[:, 8:16], func=AF.Exp), reads=[("hp",)], writes=[("aneg",)])
    P.op("dve", lambda e: e.tensor_scalar(out=aneg[:], in0=aneg[:], scalar1=-1.0, scalar2=None, op0=ALU.mult),
         reads=[("aneg",)], writes=[("aneg",)])
    for c in range(8):
        cv.run(wfm_b[c], wfm_f[c], "wfm")
    cv.run(wz_b[0], wz_f[0], "wz")
    cv.run(wdt_b[0], wdt_f[0], "wdt")
    P.dma("sp", lambda e: e.dma_start(out=wz[:], in_=wz_b[0]), "wl", reads=P.dkeys["wz"], writes=[("wz",)])
    P.dma("sp", lambda e: e.dma_start(out=wdt[:], in_=wdt_b[0]), "wl", reads=P.dkeys["wdt"], writes=[("wdt",)])

    cnt = {"wfm": 0, "psa": 0, "hh": 0}

    for t in range(n_tiles):
        ts = slice(t * TT, (t + 1) * TT)
        for k in range(KD):
            P.dma("pool", lambda e, k=k, ts=ts: e.dma_start(out=hm[:, k, :], in_=hmT[k * 128:(k + 1) * 128, ts]),
                  "hl", writes=[("hm",)])
        for c in range(8):
            j = cnt["wfm"] % 2
            cnt["wfm"] += 1
            wt = wfm[j]
            P.dma("sp", lambda e, wt=wt, c=c: e.dma_start(out=wt[:], in_=wfm_b[c]),
                  "wl", reads=P.dkeys["wfm"], writes=[("wfm", j)])
            pj = cnt["psa"] % 2
            cnt["psa"] += 1
            pa = ps_a[pj]
            for k in range(KD):
                P.op("pe", lambda e, pa=pa, wt=wt, k=k: e.matmul(
                    pa[:], lhsT=wt[:, k * 128:(k + 1) * 128], rhs=hm[:, k, :], start=(k == 0), stop=(k == KD - 1)),
                    reads=[("wfm", j), ("hm",)], writes=[("psa", pj)])
            P.op("act", lambda e, pa=pa, c=c: e.copy(out=xc[:, c, 3:], in_=pa[:]),
                 reads=[("psa", pj)], writes=[("xc", c)])
            P.op("dve", lambda e, c=c: e.tensor_scalar(
                out=acc[:], in0=xc[:, c, 3:TT + 3], scalar1=conv[:, 5 * c + 3:5 * c + 4], scalar2=conv[:, 5 * c + 4:5 * c + 5],
                op0=ALU.mult, op1=ALU.add), reads=[("xc", c), ("conv",)], writes=[("acc",)])
            for jj in range(3):
                P.op("dve", lambda e, c=c, jj=jj: e.scalar_tensor_tensor(
                    out=acc[:], in0=xc[:, c, jj:TT + jj], scalar=conv[:, 5 * c + jj:5 * c + jj + 1], in1=acc[:],
                    op0=ALU.mult, op1=ALU.add), reads=[("xc", c), ("conv",), ("acc",)], writes=[("acc",)])
            P.op("act", lambda e, c=c: e.activation(out=xact[:, c, :], in_=acc[:], func=AF.Silu),
                 reads=[("acc",)], writes=[("xact", c)])
            P.op("pool", lambda e, c=c: e.tensor_copy(out=xc[:, c, 0:3], in_=xc[:, c, TT:TT + 3]),
                 reads=[("xc", c)], writes=[("xc", c)])
        for sub in range(4 if dbg >= 2 else 0):
            ssl = slice(sub * 128, (sub + 1) * 128)
            pj = cnt["psa"] % 2
            cnt["psa"] += 1
            pa = ps_a[pj]
            for k in range(KD):
                P.op("pe", lambda e, pa=pa, k=k, ssl=ssl: e.matmul(
                    pa[:], lhsT=hm[:, k, ssl], rhs=wz[:, k * 512:(k + 1) * 512], start=(k == 0), stop=(k == KD - 1)),
                    reads=[("wz",), ("hm",)], writes=[("psa", pj)])
            P.op("act", lambda e, pa=pa, sub=sub: e.activation(out=zs[:, sub, :], in_=pa[:], func=AF.Silu),
                 reads=[("psa", pj)], writes=[("zs", sub)])
            pj = cnt["psa"] % 2
            cnt["psa"] += 1
            pa = ps_a[pj]
            for k in range(KD):
                P.op("pe", lambda e, pa=pa, k=k, ssl=ssl: e.matmul(
                    pa[:, 0:8], lhsT=hm[:, k, ssl], rhs=wdt[:, k * 8:(k + 1) * 8], start=(k == 0), stop=(k == KD - 1)),
                    reads=[("wdt",), ("hm",)], writes=[("psa", pj)])
            P.op("dve", lambda e, pa=pa, sub=sub: e.tensor_tensor(out=dt[:, sub, :], in0=pa[:, 0:8], in1=hp[:, 0:8], op=ALU.add),
                 reads=[("psa", pj), ("hp",)], writes=[("dt", sub)])
            P.op("act", lambda e, sub=sub: e.activation(out=dt[:, sub, :], in_=dt[:, sub, :], func=AF.Exp),
                 reads=[("dt", sub)], writes=[("dt", sub)])
            P.op("act", lambda e, sub=sub: e.activation(out=dt[:, sub, :], in_=dt[:, sub, :], func=AF.Ln, bias=oneb[:], scale=1.0),
                 reads=[("dt", sub), ("oneb",)], writes=[("dt", sub)])
            P.op("dve", lambda e, sub=sub: e.tensor_tensor(out=la[:, sub, :], in0=dt[:, sub, :], in1=aneg[:], op=ALU.mult),
                 reads=[("dt", sub), ("aneg",)], writes=[("la", sub)])
        for sub in range(4 if dbg >= 3 else 0):
            cs = slice(sub * 128, (sub + 1) * 128)
            P.op("pe", lambda e, sub=sub: e.matmul(ps_cb[:, 0:8], lhsT=U[:], rhs=la[:, sub, :], start=True, stop=True),
                 reads=[("U",), ("la", sub)], writes=[("pscb",)])
            P.op("dve", lambda e: e.tensor_copy(out=cum[:], in_=ps_cb[:, 0:8]), reads=[("pscb",)], writes=[("cum",)])
            P.op("dve", lambda e: e.tensor_scalar(out=ncum[:], in0=cum[:], scalar1=-1.0, scalar2=None, op0=ALU.mult),
                 reads=[("cum",)], writes=[("ncum",)])
            P.op("act", lambda e: e.activation(out=ecum[:], in_=cum[:], func=AF.Exp), reads=[("cum",)], writes=[("ecum",)])
            if dbg < 3.2:
                continue
            for g in range(2):
                P.op("pe", lambda e, g=g, cs=cs: e.matmul(
                    ps_cb[:], lhsT=xact[:, 4 + g, cs], rhs=xact[:, 6 + g, cs], start=True, stop=True),
                    reads=[("xact", 4 + g), ("xact", 6 + g), ("cum",), ("ncum",), ("ecum",)], writes=[("pscb",)])
                P.op("act", lambda e, g=g: e.copy(out=cbT[:, g, :], in_=ps_cb[:]), reads=[("pscb",)], writes=[("cbT", g)])
                P.op("pe", lambda e, g=g, cs=cs: e.transpose(ps_tr[:, g * 128:(g + 1) * 128], xact[:, 4 + g, cs], ident[:]),
                     reads=[("xact", 4 + g), ("ident",)], writes=[("pstr",)])
            P.op("act", lambda e: e.copy(out=Btm[:].rearrange("p g n -> p (g n)"), in_=ps_tr[:, 0:256]),
                 reads=[("pstr",)], writes=[("Btm",)])
            if dbg < 3.4:
                continue
            for c in range(4):
                P.op("pe", lambda e, c=c, cs=cs: e.transpose(ps_tr[:, c * 128:(c + 1) * 128], xact[:, c, cs], ident[:]),
                     reads=[("xact", c), ("ident",), ("Btm",)], writes=[("pstr",)])
            P.op("act", lambda e: e.copy(out=xs_tm[:], in_=ps_tr[:]), reads=[("pstr",)], writes=[("xstm",)])
            for h in range(SH if dbg >= 4 else 0):
                g = h // 4
                i2 = cnt["hh"] % 2
                cnt["hh"] += 1
                hs = slice(h * 64, (h + 1) * 64)
                P.op("dve", lambda e, i2=i2, sub=sub, h=h: e.tensor_scalar(
                    out=LAb[i2][:], in0=ones[:], scalar1=la[:, sub, h:h + 1], scalar2=None, op0=ALU.mult),
                    reads=[("ones",), ("la", sub)], writes=[("LAb", i2)])
                P.op("pe", lambda e, i2=i2: e.matmul(ps_seg[i2][:], lhsT=LAb[i2][:], rhs=U[:], start=True, stop=True),
                     reads=[("LAb", i2), ("U",)], writes=[("psseg", i2)])
                P.op("act", lambda e, i2=i2: e.activation(out=ecl[i2][:], in_=ps_seg[i2][:, 127:128], func=AF.Exp),
                     writes=[("ecl", i2), ("psseg", i2)])
                P.op("dve", lambda e, i2=i2: e.tensor_tensor(out=decT[i2][:], in0=ps_seg[i2][:], in1=nm[:], op=ALU.add),
                     reads=[("psseg", i2), ("nm",)], writes=[("decT", i2)])
                P.op("act", lambda e, i2=i2, h=h: e.activation(
                    out=decT[i2][:], in_=decT[i2][:], func=AF.Exp, bias=ncum[:, h:h + 1], scale=1.0),
                    reads=[("decT", i2), ("ncum",)], writes=[("decT", i2)])
                P.op("dve", lambda e, i2=i2, g=g: e.tensor_tensor(out=MT[i2][:], in0=decT[i2][:], in1=cbT[:, g, :], op=ALU.mult),
                     reads=[("decT", i2), ("cbT", g)], writes=[("MT", i2)])
                if dbg < 4.2:
                    continue
                P.op("pool", lambda e, i2=i2, sub=sub, h=h, hs=hs: e.tensor_scalar(
                    out=xd[i2][:], in0=xs_tm[:, hs], scalar1=dt[:, sub, h:h + 1], scalar2=None, op0=ALU.mult),
                    reads=[("xstm",), ("dt", sub)], writes=[("xd", i2)])
                P.op("dve", lambda e, i2=i2: e.tensor_scalar(
                    out=wxd[i2][:], in0=xd[i2][:], scalar1=decT[i2][:, 127:128], scalar2=None, op0=ALU.mult),
                    reads=[("xd", i2), ("decT", i2)], writes=[("wxd", i2)])
                if dbg < 4.3:
                    continue
                py = ps_y[i2]
                P.op("pe", lambda e, py=py, i2=i2: e.matmul(py[:, 0:64], lhsT=MT[i2][:], rhs=xd[i2][:], start=True, stop=True),
                     reads=[("MT", i2), ("xd", i2)], writes=[("psy", i2)])
                P.op("pe", lambda e, py=py, g=g, cs=cs, h=h: e.matmul(
                    py[:, 64:128], lhsT=xact[:, 6 + g, cs], rhs=state_b[:, h, :], start=True, stop=True),
                    reads=[("xact", 6 + g), ("state_b", h)], writes=[("psy", i2)])
                P.op("pe", lambda e, py=py, g=g, i2=i2: e.matmul(
                    py[:, 128:192], lhsT=Btm[:, g, :], rhs=wxd[i2][:], start=True, stop=True),
                    reads=[("Btm",), ("wxd", i2)], writes=[("psy", i2)])
                if dbg < 4.4:
                    continue
                P.op("act", lambda e, py=py, i2=i2: e.copy(out=ysb[i2][:], in_=py[:, 0:64]),
                     writes=[("ysb", i2), ("psy", i2)])
                P.op("dve", lambda e, py=py, i2=i2, h=h, hs=hs: e.scalar_tensor_tensor(
                    out=y_all[:, hs], in0=py[:, 64:128], scalar=ecum[:, h:h + 1], in1=ysb[i2][:], op0=ALU.mult, op1=ALU.add),
                    reads=[("psy", i2), ("ecum",), ("ysb", i2)], writes=[("yall", h)])
                P.op("dve", lambda e, h=h, hs=hs: e.scalar_tensor_tensor(
                    out=y_all[:, hs], in0=xs_tm[:, hs], scalar=hp[:, 16 + h:17 + h], in1=y_all[:, hs], op0=ALU.mult, op1=ALU.add),
                    reads=[("xstm",), ("hp",), ("yall", h)], writes=[("yall", h)])
                P.op("dve", lambda e, py=py, i2=i2, h=h: e.scalar_tensor_tensor(
                    out=state[:, h, :], in0=state[:, h, :], scalar=ecl[i2][:, 0:1], in1=py[:, 128:192], op0=ALU.mult, op1=ALU.add),
                    reads=[("psy", i2), ("ecl", i2), ("state", h)], writes=[("state", h)])
                P.op("act", lambda e, h=h: e.copy(out=state_b[:, h, :], in_=state[:, h, :]),
                     reads=[("state", h)], writes=[("state_b", h)])
            if dbg < 5:
                continue
            yres = [("yall", h) for h in range(SH)]
            P.op("dve", lambda e, sub=sub: e.tensor_tensor(out=y_all[:], in0=y_all[:], in1=zs[:, sub, :], op=ALU.mult),
                 reads=yres + [("zs", sub)], writes=yres)
            for g in range(2):
                P.op("act", lambda e, g=g: e.activation(out=sqj[:], in_=y_all[:, g * 256:(g + 1) * 256], func=AF.Square,
                                                        accum_out=ss[:, g:g + 1]),
                     reads=yres, writes=[("sqj",), ("ss", g)])
            P.op("act", lambda e: e.activation(out=ss[:], in_=ss[:], func=AF.Ln, bias=epsb[:], scale=1.0 / 256),
                 reads=[("ss", 0), ("ss", 1), ("epsb",)], writes=[("ss", 0), ("ss", 1)])
            P.op("act", lambda e: e.activation(out=ss[:], in_=ss[:], func=AF.Exp, scale=-0.5),
                 reads=[("ss", 0), ("ss", 1)], writes=[("ss", 0), ("ss", 1)])
            for g in range(2):
                gs = slice(g * 256, (g + 1) * 256)
                P.op("dve", lambda e, g=g, gs=gs: e.scalar_tensor_tensor(
                    out=o_tm[:, gs], in0=y_all[:, gs], scalar=ss[:, g:g + 1], in1=nw[:, gs], op0=ALU.mult, op1=ALU.mult),
                    reads=yres + [("ss", g), ("nw",)], writes=[("otm", g)])
            for c in range(4):
                P.op("pe", lambda e, c=c: e.transpose(ps_tr[:, c * 128:(c + 1) * 128], o_tm[:, c * 128:(c + 1) * 128], ident[:]),
                     reads=[("otm", 0), ("otm", 1), ("ident",), ("xstm",)], writes=[("pstr",)])
            P.op("act", lambda e, cs=cs: e.copy(out=o_fm[:, :, cs], in_=ps_tr[:].rearrange("p (c t) -> p c t", c=4)),
                 reads=[("pstr",)], writes=[("ofm",)])
        for c in range(4):
            P.dma("pool", lambda e, c=c, ts=ts: e.dma_start(out=oT[c * 128:(c + 1) * 128, ts], in_=o_fm[:, c, :]),
                  "os", reads=[("ofm",)], writes=[("dram", "oT")])
    P.emit()
    return nc


EV_OFF = {}
_o = 0
for _n, _s in zip(("q", "kc", "vc", "ksl", "vsl", "kwn", "vwn", "gate", "z", "xbc", "dt"),
                  (1024, 256, 256, 256, 256, 256, 256, 48, 1024, 2048, 16)):
    EV_OFF[_n] = _o
    _o += _s


def ssd_inputs(hm_bf_T, w_in, conv_w, conv_b, dt_bias, a_log, d_skip, ssm_norm, hh):
    import ml_dtypes
    xo = EV_OFF["xbc"]
    xs_cols = np.arange(xo + hh * 512, xo + (hh + 1) * 512)
    b_cols = np.arange(xo + 1024 + hh * 256, xo + 1024 + (hh + 1) * 256)
    c_cols = np.arange(xo + 1536 + hh * 256, xo + 1536 + (hh + 1) * 256)
    cols = np.concatenate([xs_cols, b_cols, c_cols])
    wfm = tile_w_fm(w_in[:, cols])
    wz = tile_w_tm(w_in[:, EV_OFF["z"] + hh * 512: EV_OFF["z"] + (hh + 1) * 512], 512)
    wdt = tile_w_tm(w_in[:, EV_OFF["dt"] + hh * 8: EV_OFF["dt"] + (hh + 1) * 8], 8)
    cc = cols - xo
    cw = conv_w[:, cc]
    cb = conv_b[cc]
    c_conv = np.zeros((128, 40), np.float32)
    for c in range(8):
        c_conv[:, 5 * c:5 * c + 4] = cw[:, c * 128:(c + 1) * 128].T
        c_conv[:, 5 * c + 4] = cb[c * 128:(c + 1) * 128]
    hsl = slice(hh * 8, (hh + 1) * 8)
    c_hp = np.concatenate([np.tile(dt_bias[hsl][None], (128, 1)), np.tile(a_log[hsl][None], (128, 1)),
                           np.tile(d_skip[hsl][None], (128, 1))], axis=1).astype(np.float32)
    c_nw = np.tile(ssm_norm[hh * 512:(hh + 1) * 512][None], (128, 1)).astype(np.float32)
    r = np.arange(128)
    U = (r[:, None] <= r[None, :]).astype(np.float32)
    nm = np.where(r[:, None] <= r[None, :], 0.0, NEG).astype(np.float32)
    return {"hmT": hm_bf_T, "wfm": wfm, "wz": wz, "wdt": wdt, "c_conv": c_conv, "c_hp": c_hp, "c_nw": c_nw,
            "c_U": U, "c_nm": nm, "c_idf": np.eye(128, dtype=np.float32),
            "c_ident": np.eye(128, dtype=np.float32).astype(ml_dtypes.bfloat16)}


NFM = 19
NTM = 280


def build_nsa(S_len, P=None):
    if P is None:
        P = Prog(bass.Bass("TRN2", target_bir_lowering=False))
    nc = P.nc
    n_tiles = S_len // TT
    NB = S_len // 16
    NBLK = S_len // 64
    NKT = S_len // 128
    NBC = (NB + 127) // 128
    hmT = P.dram("hmT", [D, S_len], BF16, "ExternalInput")
    wfm_f = P.dram("wfm", [NFM, 128, D], F32, "ExternalInput")
    wtm_f = P.dram("wtm", [1, 128, KD * NTM], F32, "ExternalInput")
    w1_f = P.dram("w1", [2, 128, 32 * 256], F32, "ExternalInput")
    w2k_f = P.dram("w2k", [128, 2 * 128], F32, "ExternalInput")
    w2v_f = P.dram("w2v", [128, 2 * 64], F32, "ExternalInput")
    posT_f = P.dram("posT", [128, 2 * 32], F32, "ExternalInput")
    wfm_b = P.dram("wfm_b", [NFM, 128, D], BF16)
    wtm_b = P.dram("wtm_b", [1, 128, KD * NTM], BF16)
    w1_b = P.dram("w1_b", [2, 128, 32 * 256], BF16)
    t_cq = P.dram("t_cq", [128, S_len], F32, "ExternalInput")
    t_sq = P.dram("t_sq", [128, S_len], F32, "ExternalInput")
    t_ck = P.dram("t_ck", [128, S_len], F32, "ExternalInput")
    t_sk = P.dram("t_sk", [128, S_len], F32, "ExternalInput")
    c_ident = P.dram("c_ident", [128, 128], BF16, "ExternalInput")
    c_E = P.dram("c_E", [128, S_len], BF16, "ExternalInput")
    c_Wb = P.dram("c_Wb", [128, 8 * 512], BF16, "ExternalInput")
    c_Cw = P.dram("c_Cw", [128, 1024], BF16, "ExternalInput")
    c_AB = P.dram("c_AB", [128, 512], F32, "ExternalInput")
    oT = P.dram("oT", [512, S_len], BF16, "ExternalOutput")
    QT = P.dram("QT", [4, 128, S_len], BF16)
    KslT = P.dram("KslT", [2, 128, S_len], BF16)
    KwnT = P.dram("KwnT", [2, 128, S_len], BF16)
    KcT = P.dram("KcT", [128, S_len], BF16)
    VcT = P.dram("VcT", [128, S_len], BF16)
    Vsl = P.dram("Vsl", [2, S_len, 64], BF16)
    Vwn = P.dram("Vwn", [2, S_len, 64], BF16)
    Gate = P.dram("Gate", [S_len, 24], F32)

    ident = P.sb("n_ident", [128, 128], BF16)
    zer = P.sb("n_zer", [128, 260], BF16)
    ps_st = [P.ps(f"n_psst{i}", [128, 512]) for i in range(3)]
    ps_acc = [P.ps(f"n_psacc{i}", [128, 4, 65]) for i in range(2)]
    ps_c = P.ps("n_psc", [128, 512])
    ps_tr = P.ps("n_pstr", [128, 512], BF16)
    ps_oc = P.ps("n_psoc", [128, 64])
    P.dma("pool", lambda e: e.dma_start(out=ident[:], in_=c_ident), "cl", writes=[("ident",)])
    P.op("dve", lambda e: e.memset(zer[:], 0.0), writes=[("zer",)])

    hm = P.sb("n_hm", [128, KD, TT], BF16)
    wfm = [P.sb(f"n_wfm{i}", [128, D], BF16) for i in range(2)]
    wtm = P.sb("n_wtm", [128, KD * NTM], BF16)
    tab = P.sb("n_tab", [128, 4, TT], F32)
    raw = P.sb("n_raw", [128, TT], F32)
    tmp = P.sb("n_tmp", [128, TT], F32)
    fmo = [P.sb(f"n_fmo{i}", [128, TT], BF16) for i in range(2)]
    vg = [P.sb(f"n_vg{i}", [128, 256], BF16) for i in range(2)]
    gt = [P.sb(f"n_gt{i}", [128, 24], F32) for i in range(2)]
    cv = Conv(P)
    for c in range(NFM):
        cv.run(wfm_b[c], wfm_f[c], ("wfm", c))
    cv.run(wtm_b[0], wtm_f[0], "wtm")
    for kv in range(2):
        cv.run(w1_b[kv], w1_f[kv], ("w1", kv))
    P.dma("sp", lambda e: e.dma_start(out=wtm[:], in_=wtm_b[0]), "wl", reads=P.dkeys["wtm"], writes=[("wtm",)])
    cnt = {"wfm": 0, "pst": 0, "fmo": 0, "vg": 0}

    def fm_chunk(c):
        j = cnt["wfm"] % 2
        cnt["wfm"] += 1
        wt = wfm[j]
        P.dma("sp", lambda e, wt=wt, c=c: e.dma_start(out=wt[:], in_=wfm_b[c]),
              "wl", reads=P.dkeys[("wfm", c)], writes=[("wfm", j)])
        pj = cnt["pst"] % 2
        cnt["pst"] += 1
        pa = ps_st[pj]
        for k in range(KD):
            P.op("pe", lambda e, pa=pa, wt=wt, k=k: e.matmul(
                pa[:], lhsT=wt[:, k * 128:(k + 1) * 128], rhs=hm[:, k, :], start=(k == 0), stop=(k == KD - 1)),
                reads=[("wfm", j), ("hm",)], writes=[("psst", pj)])
        return pa, pj

    def roped(c_main, c_swap, ci, si, dst_ap, dst_key, ts):
        pa, pj = fm_chunk(c_main)
        P.op("dve", lambda e, pa=pa, ci=ci: e.tensor_tensor(out=raw[:], in0=pa[:], in1=tab[:, ci, :], op=ALU.mult),
             reads=[("psst", pj), ("tab",)], writes=[("raw",)])
        pb, pk = fm_chunk(c_swap)
        P.op("dve", lambda e, pb=pb, si=si: e.tensor_tensor(out=tmp[:], in0=pb[:], in1=tab[:, si, :], op=ALU.mult),
             reads=[("psst", pk), ("tab",)], writes=[("tmp",)])
        fj = cnt["fmo"] % 2
        cnt["fmo"] += 1
        fo = fmo[fj]
        P.op("dve", lambda e, fo=fo: e.tensor_tensor(out=fo[:], in0=raw[:], in1=tmp[:], op=ALU.add),
             reads=[("raw",), ("tmp",)], writes=[("fmo", fj)])
        P.dma("pool", lambda e, fo=fo, dst_ap=dst_ap: e.dma_start(out=dst_ap, in_=fo[:]),
              "sc", reads=[("fmo", fj)], writes=[("dram", dst_key)])

    for t in range(n_tiles):
        ts = slice(t * TT, (t + 1) * TT)
        for k in range(KD):
            P.dma("pool", lambda e, k=k, ts=ts: e.dma_start(out=hm[:, k, :], in_=hmT[k * 128:(k + 1) * 128, ts]),
                  "hl", writes=[("hm",)])
        for i, src in enumerate((t_cq, t_sq, t_ck, t_sk)):
            P.dma("pool", lambda e, i=i, src=src, ts=ts: e.dma_start(out=tab[:, i, :], in_=src[:, ts]),
                  "tl", writes=[("tab",)])
        for j in range(4):
            roped(j, 4 + j, 0, 1, QT[j][:, ts], ("QT", j, t), ts)
        for g in range(2):
            roped(8 + g, 10 + g, 2, 3, KslT[g][:, ts], ("KslT", g, t), ts)
            roped(12 + g, 14 + g, 2, 3, KwnT[g][:, ts], ("KwnT", g, t), ts)
        roped(16, 17, 2, 3, KcT[:, ts], ("KcT", t), ts)
        pa, pj = fm_chunk(18)
        fj = cnt["fmo"] % 2
        cnt["fmo"] += 1
        fo = fmo[fj]
        P.op("act", lambda e, pa=pa, fo=fo: e.copy(out=fo[:], in_=pa[:]), reads=[("psst", pj)], writes=[("fmo", fj)])
        P.dma("pool", lambda e, fo=fo, ts=ts: e.dma_start(out=VcT[:, ts], in_=fo[:]),
              "sc", reads=[("fmo", fj)], writes=[("dram", ("VcT", t))])
        for sub in range(4):
            ssl = slice(sub * 128, (sub + 1) * 128)
            r0 = t * TT + sub * 128
            pj = cnt["pst"] % 2
            cnt["pst"] += 1
            pa = ps_st[pj]
            for k in range(KD):
                P.op("pe", lambda e, pa=pa, k=k, ssl=ssl: e.matmul(
                    pa[:, 0:NTM], lhsT=hm[:, k, ssl], rhs=wtm[:, k * NTM:(k + 1) * NTM], start=(k == 0), stop=(k == KD - 1)),
                    reads=[("wtm",), ("hm",)], writes=[("psst", pj)])
            vj = cnt["vg"] % 2
            cnt["vg"] += 1
            P.op("act", lambda e, pa=pa, vj=vj: e.copy(out=vg[vj][:], in_=pa[:, 0:256]),
                 reads=[("psst", pj)], writes=[("vg", vj)])
            P.op("act", lambda e, pa=pa, vj=vj: e.activation(out=gt[vj][:], in_=pa[:, 256:280], func=AF.Sigmoid),
                 reads=[("psst", pj)], writes=[("gt", vj)])
            for g in range(2):
                P.dma("pool", lambda e, vj=vj, g=g, r0=r0: e.dma_start(out=Vsl[g][r0:r0 + 128, :], in_=vg[vj][:, g * 64:(g + 1) * 64]),
                      "sc", reads=[("vg", vj)], writes=[("dram", ("Vsl", g, t, sub))])
                P.dma("pool", lambda e, vj=vj, g=g, r0=r0: e.dma_start(out=Vwn[g][r0:r0 + 128, :], in_=vg[vj][:, 128 + g * 64:128 + (g + 1) * 64]),
                      "sc", reads=[("vg", vj)], writes=[("dram", ("Vwn", g, t, sub))])
            P.dma("pool", lambda e, vj=vj, r0=r0: e.dma_start(out=Gate[r0:r0 + 128, :], in_=gt[vj][:]),
                  "sc", reads=[("gt", vj)], writes=[("dram", ("Gate", t, sub))])

    kcv = P.sb("n_kcv", [128, 2, S_len + 16], BF16)
    w1 = P.sb("n_w1", [128, 32 * 256], BF16)
    w2k = P.sb("n_w2k", [128, 256], BF16)
    w2v = P.sb("n_w2v", [128, 128], BF16)
    w2f = P.sb("n_w2f", [128, 256], F32)
    posT = P.sb("n_posT", [128, 64], BF16)
    posf = P.sb("n_posf", [128, 64], F32)
    c1 = P.sb("n_c1", [128, 2], F32)
    hid = P.sb("n_hid", [128, 2, NB], BF16)
    KcmpT = P.sb("n_KcmpT", [128, 2, NB], BF16)
    Vcmp = P.sb("n_Vcmp", [128, 2, NBC, 64], BF16)
    P.op("dve", lambda e: e.memset(kcv[:, :, S_len:], 0.0), writes=[("kcv_pad",)])
    P.dma("pool", lambda e: e.dma_start(out=kcv[:, 0, 0:S_len], in_=KcT), "kl",
          reads=[("dram", ("KcT", t)) for t in range(n_tiles)], writes=[("kcv", 0)])
    P.dma("pool", lambda e: e.dma_start(out=kcv[:, 1, 0:S_len], in_=VcT), "kl",
          reads=[("dram", ("VcT", t)) for t in range(n_tiles)], writes=[("kcv", 1)])
    P.dma("pool", lambda e: e.dma_start(out=posf[:], in_=posT_f), "cl", writes=[("posf",)])
    P.op("dve", lambda e: e.tensor_copy(out=posT[:], in_=posf[:]), reads=[("posf",)], writes=[("posT",)])
    P.dma("pool", lambda e: e.dma_start(out=w2f[:], in_=w2k_f), "cl", writes=[("w2f",)])
    P.op("dve", lambda e: e.tensor_copy(out=w2k[:], in_=w2f[:]), reads=[("w2f",)], writes=[("w2k",)])
    P.dma("pool", lambda e: e.dma_start(out=w2f[:, 0:128], in_=w2v_f), "cl", reads=[("w2k",)], writes=[("w2f",)])
    P.op("dve", lambda e: e.tensor_copy(out=w2v[:], in_=w2f[:, 0:128]), reads=[("w2f",)], writes=[("w2v",)])
    for kv in range(2):
        P.dma("sp", lambda e, kv=kv: e.dma_start(out=w1[:], in_=w1_b[kv]), "wl",
              reads=P.dkeys[("w1", kv)], writes=[("w1",)])
        for hc in range(2):
            for l in range(32):
                P.op("pe", lambda e, hc=hc, l=l, kv=kv: e.matmul(
                    ps_oc[:, hc:hc + 1], lhsT=w1[0:64, l * 256 + hc * 128:l * 256 + (hc + 1) * 128],
                    rhs=posT[0:64, kv * 32 + l:kv * 32 + l + 1], start=(l == 0), stop=(l == 31)),
                    reads=[("w1",), ("posT",)], writes=[("psoc",)])
        P.op("dve", lambda e: e.tensor_copy(out=c1[:], in_=ps_oc[:, 0:2]), reads=[("psoc",)], writes=[("c1",)])
        for g in range(2):
            gp = slice(g * 64, (g + 1) * 64)
            for hc in range(2):
                for l in range(32):
                    P.op("pe", lambda e, hc=hc, l=l, kv=kv, gp=gp: e.matmul(
                        ps_c[:, 0:NB], lhsT=w1[gp, l * 256 + hc * 128:l * 256 + (hc + 1) * 128],
                        rhs=kcv[gp, kv, l:l + 16 * (NB - 1) + 1:16], start=(l == 0), stop=(l == 31)),
                        reads=[("w1",), ("kcv", kv), ("kcv_pad",)], writes=[("psc",)])
                P.op("act", lambda e, hc=hc: e.activation(out=hid[:, hc, :], in_=ps_c[:, 0:NB], func=AF.Silu,
                                                          bias=c1[:, hc:hc + 1], scale=1.0),
                     reads=[("psc",), ("c1",)], writes=[("hid", hc)])
            if kv == 0:
                for hc in range(2):
                    P.op("pe", lambda e, hc=hc: e.matmul(ps_c[:, 0:NB], lhsT=w2k[:, hc * 128:(hc + 1) * 128], rhs=hid[:, hc, :],
                                                         start=(hc == 0), stop=(hc == 1)),
                         reads=[("w2k",), ("hid", 0), ("hid", 1)], writes=[("psc",)])
                P.op("act", lambda e, g=g: e.copy(out=KcmpT[:, g, :], in_=ps_c[:, 0:NB]), reads=[("psc",)], writes=[("KcmpT", g)])
            else:
                for ncn in range(NBC):
                    nn = min(128, NB - ncn * 128)
                    for hc in range(2):
                        P.op("pe", lambda e, hc=hc, ncn=ncn, nn=nn: e.matmul(
                            ps_oc[0:nn, :], lhsT=hid[:, hc, ncn * 128:ncn * 128 + nn], rhs=w2v[:, hc * 64:(hc + 1) * 64],
                            start=(hc == 0), stop=(hc == 1)),
                            reads=[("w2v",), ("hid", 0), ("hid", 1)], writes=[("psoc",)])
                    P.op("act", lambda e, g=g, ncn=ncn, nn=nn: e.copy(out=Vcmp[0:nn, g, ncn, :], in_=ps_oc[0:nn, :]),
                         reads=[("psoc",)], writes=[("Vcmp", g)])

    E = hm[:].rearrange("p k t -> p (k t)")[:, 0:S_len]
    Wb = P.sb("n_Wb", [128, 8, 512], BF16)
    Cw = P.sb("n_Cw", [128, 1024], BF16)
    AB = P.sb("n_AB", [128, 512], F32)
    P.dma("pool", lambda e: e.dma_start(out=E, in_=c_E), "cl", writes=[("E",), ("hm",)])
    for dst, src, tag in ((Wb, c_Wb, "Wb"), (Cw, c_Cw, "Cw"), (AB, c_AB, "AB")):
        P.dma("pool", lambda e, dst=dst, src=src: e.dma_start(
            out=dst[:] if len(dst.shape) == 2 else dst[:].rearrange("p a b -> p (a b)"), in_=src), "cl", writes=[(tag,)])
    Ks = kcv[:, 0, 0:S_len]
    Kw = kcv[:, 1, 0:S_len]
    Vs1 = P.sb("n_Vs1", [128, NKT, 65], BF16)
    Vw1 = P.sb("n_Vw1", [128, NKT, 65], BF16)
    Qt = P.sb("n_Qt", [128, 2, TT], BF16)
    gtile = P.sb("n_gtile", [128, 4, 24], F32)
    pc = P.sb("n_pc", [128, NB], F32)
    pcb = P.sb("n_pcb", [128, NBC * 128], BF16)
    pcT = P.sb("n_pcT", [128, NBC * 128], BF16)
    imp = P.sb("n_imp", [128, NB], F32)
    rs = P.sb("n_rs", [128, 2], F32)
    vals = P.sb("n_vals", [128, NBLK], F32)
    vtmp = P.sb("n_vtmp", [128, NBLK], F32)
    m8 = P.sb("n_m8", [128, 16], F32)
    sel = P.sb("n_sel", [128, NBLK], F32)
    negb = P.sb("n_negb", [128, 128], BF16)
    negbT = P.sb("n_negbT", [128, TT], BF16)
    PT = [P.sb(f"n_PT{i}", [128, TT], BF16) for i in range(3)]
    oacc = P.sb("n_oacc", [128, 4, 4, 64], F32)
    oaccb = P.sb("n_oaccb", [128, 4, 256], BF16)
    w4 = P.sb("n_w4", [128, 4], F32)
    o_fm = P.sb("n_ofm", [128, 2, TT], BF16)
    P.op("dve", lambda e: e.memset(Vs1[:], 1.0), writes=[("Vs1",)])
    P.op("dve", lambda e: e.memset(Vw1[:], 1.0), writes=[("Vw1",)])
    P.op("dve", lambda e: e.memset(negb[:], 0.0), writes=[("negb",)])
    cnt.update({"acc": 0, "PT": 0, "pst3": 0})

    for g in range(2):
        P.dma("pool", lambda e, g=g: e.dma_start(out=Ks, in_=KslT[g]), "kl",
              reads=[("dram", ("KslT", g, t)) for t in range(n_tiles)], writes=[("Ks",), ("kcv", 0), ("kcv_pad",)])
        P.dma("pool", lambda e, g=g: e.dma_start(out=Kw, in_=KwnT[g]), "kl",
              reads=[("dram", ("KwnT", g, t)) for t in range(n_tiles)], writes=[("Kw",), ("kcv", 1), ("kcv_pad",)])
        vdeps = lambda nm: [("dram", (nm, g, t, s)) for t in range(n_tiles) for s in range(4)]
        P.dma("pool", lambda e, g=g: e.dma_start(out=Vs1[:, :, 0:64], in_=Vsl[g].rearrange("(k p) d -> p k d", p=128)), "kl",
              reads=vdeps("Vsl"), writes=[("Vs1",)])
        P.dma("pool", lambda e, g=g: e.dma_start(out=Vw1[:, :, 0:64], in_=Vwn[g].rearrange("(k p) d -> p k d", p=128)), "kl",
              reads=vdeps("Vwn"), writes=[("Vw1",)])
        for qt in range(n_tiles):
            T0 = qt * TT
            ts = slice(T0, T0 + TT)
            for j in range(2):
                P.dma("pool", lambda e, j=j, g=g, ts=ts: e.dma_start(out=Qt[:, j, :], in_=QT[2 * g + j][:, ts]), "ql",
                      reads=[("dram", ("QT", 2 * g + j, qt))], writes=[("Qt",)])
            P.dma("pool", lambda e, ts=ts: e.dma_start(out=gtile[:], in_=Gate[ts, :].rearrange("(s p) c -> p s c", p=128)), "ql",
                  reads=[("dram", ("Gate", qt, s)) for s in range(4)], writes=[("gtile",)])
            NBv = min(NB, 32 * (qt + 1))
            nch = (NBv + 127) // 128
            for sub in range(4):
                m = 4 * qt + sub
                qs = slice(sub * 128, (sub + 1) * 128)
                P.op("pool", lambda e: e.memset(imp[:], 0.0), writes=[("imp",)])
                for hl in range(4):
                    j, half = hl // 2, hl % 2
                    hp_ = slice(half * 64, (half + 1) * 64)
                    gcol = 3 * (4 * g + hl)
                    P.op("pe", lambda e, j=j, hp_=hp_, qs=qs, g=g, NBv=NBv: e.matmul(
                        ps_c[:, 0:NBv], lhsT=Qt[hp_, j, qs], rhs=KcmpT[hp_, g, 0:NBv], start=True, stop=False),
                        reads=[("Qt",), ("KcmpT", g)], writes=[("psc",)])
                    P.op("pe", lambda e, m=m, NBv=NBv: e.matmul(
                        ps_c[:, 0:NBv], lhsT=ident[:], rhs=Cw[:, 512 - 8 * m:512 - 8 * m + NBv], start=False, stop=True),
                        reads=[("ident",), ("Cw",)], writes=[("psc",)])
                    P.op("act", lambda e, NBv=NBv: e.activation(out=pc[:, 0:NBv], in_=ps_c[:, 0:NBv], func=AF.Exp,
                                                                accum_out=rs[:, 0:1]),
                         reads=[("psc",)], writes=[("pc",), ("rs",)])
                    P.op("dve", lambda e: e.tensor_scalar(out=rs[:, 0:1], in0=rs[:, 0:1], scalar1=1e-30, scalar2=None, op0=ALU.max),
                         reads=[("rs",)], writes=[("rs",)])
                    P.op("dve", lambda e: e.reciprocal(out=rs[:, 1:2], in_=rs[:, 0:1]), reads=[("rs",)], writes=[("rs",)])
                    P.op("dve", lambda e, NBv=NBv: e.tensor_scalar(out=pc[:, 0:NBv], in0=pc[:, 0:NBv], scalar1=rs[:, 1:2],
                                                                   scalar2=None, op0=ALU.mult),
                         reads=[("pc",), ("rs",)], writes=[("pc",)])
                    P.op("pool", lambda e, NBv=NBv: e.tensor_tensor(out=imp[:, 0:NBv], in0=imp[:, 0:NBv], in1=pc[:, 0:NBv], op=ALU.add),
                         reads=[("pc",), ("imp",)], writes=[("imp",)])
                    P.op("act", lambda e, NBv=NBv: e.copy(out=pcb[:, 0:NBv], in_=pc[:, 0:NBv]), reads=[("pc",)], writes=[("pcb",)])
                    for c in range(nch):
                        nn = min(128, NBv - c * 128)
                        P.op("pe", lambda e, c=c, nn=nn: e.transpose(ps_tr[0:nn, c * 128:(c + 1) * 128], pcb[:, c * 128:c * 128 + nn], ident[:]),
                             reads=[("pcb",), ("ident",)], writes=[("pstr",)])
                    for c in range(nch):
                        nn = min(128, NBv - c * 128)
                        P.op("act", lambda e, c=c, nn=nn: e.copy(out=pcT[0:nn, c * 128:(c + 1) * 128], in_=ps_tr[0:nn, c * 128:(c + 1) * 128]),
                             reads=[("pstr",)], writes=[("pcT",)])
                    for c in range(nch):
                        nn = min(128, NBv - c * 128)
                        P.op("pe", lambda e, c=c, nn=nn, g=g: e.matmul(
                            ps_oc[:], lhsT=pcT[0:nn, c * 128:(c + 1) * 128], rhs=Vcmp[0:nn, g, c, :], start=(c == 0), stop=(c == nch - 1)),
                            reads=[("pcT",), ("Vcmp", g)], writes=[("psoc",)])
                    P.op("dve", lambda e, sub=sub, hl=hl, gcol=gcol: e.tensor_scalar(
                        out=oacc[:, sub, hl, :], in0=ps_oc[:], scalar1=gtile[:, sub, gcol:gcol + 1], scalar2=None, op0=ALU.mult),
                        reads=[("psoc",), ("gtile",)], writes=[("oacc", sub, hl)])
                P.op("dve", lambda e: e.tensor_reduce(out=vals[:], in_=imp[:].rearrange("p (j f) -> p j f", f=4), axis=AX.X, op=ALU.add),
                     reads=[("imp",)], writes=[("vals",)])
                a0 = 128 - 2 * m
                P.op("dve", lambda e, a0=a0: e.tensor_tensor(out=vals[:], in0=vals[:], in1=AB[:, a0:a0 + NBLK], op=ALU.mult),
                     reads=[("vals",), ("AB",)], writes=[("vals",)])
                P.op("dve", lambda e, a0=a0: e.tensor_tensor(out=vals[:], in0=vals[:], in1=AB[:, 256 + a0:256 + a0 + NBLK], op=ALU.add),
                     reads=[("vals",), ("AB",)], writes=[("vals",)])
                P.op("dve", lambda e: e.memset(vals[:, 0:1], 1e4), reads=[("vals",)], writes=[("vals",)])
                P.op("dve", lambda e: e.max(out=m8[:, 0:8], in_=vals[:]), reads=[("vals",)], writes=[("m8",)])
                P.op("dve", lambda e: e.match_replace(out=vtmp[:], in_to_replace=m8[:, 0:8], in_values=vals[:], imm_value=-2.0),
                     reads=[("vals",), ("m8",)], writes=[("vtmp",)])
                P.op("dve", lambda e: e.max(out=m8[:, 8:16], in_=vtmp[:]), reads=[("vtmp",)], writes=[("m8",)])
                P.op("dve", lambda e: e.tensor_scalar(out=sel[:], in0=vals[:], scalar1=m8[:, 15:16], scalar2=None, op0=ALU.is_ge),
                     reads=[("vals",), ("m8",)], writes=[("sel",)])
                P.op("dve", lambda e: e.tensor_scalar(out=vtmp[:], in0=vals[:], scalar1=0.0, scalar2=None, op0=ALU.is_ge),
                     reads=[("vals",), ("sel",)], writes=[("vtmp",)])
                P.op("dve", lambda e: e.tensor_tensor(out=sel[:], in0=sel[:], in1=vtmp[:], op=ALU.mult),
                     reads=[("sel",), ("vtmp",)], writes=[("sel",)])
                P.op("dve", lambda e: e.tensor_scalar(out=negb[:, 0:NBLK], in0=sel[:], scalar1=-1.0, scalar2=-NEG, op0=ALU.add, op1=ALU.mult),
                     reads=[("sel",)], writes=[("negb",)])
                P.op("pe", lambda e: e.transpose(ps_tr[:, 0:128], negb[:], ident[:]), reads=[("negb",), ("ident",)], writes=[("pstr",)])
                P.op("act", lambda e, qs=qs: e.copy(out=negbT[:, qs], in_=ps_tr[:, 0:128]), reads=[("pstr",)], writes=[("negbT",)])
            for hl in range(4):
                j, half = hl // 2, hl % 2
                hp_ = slice(half * 64, (half + 1) * 64)
                for br in range(2):
                    gcol = 3 * (4 * g + hl) + 1 + br
                    Kt, V1 = (Ks, Vs1) if br == 0 else (Kw, Vw1)
                    kres, vres = (("Ks",), ("Vs1",)) if br == 0 else (("Kw",), ("Vw1",))
                    if br == 0:
                        units = [(kt, kt - 4 * qt + 4) for kt in range(4 * qt + 4)]
                    else:
                        units = [(4 * qt - 4 + o, o) for o in range(8) if 4 * qt - 4 + o >= 0]
                    aj = cnt["acc"] % 2
                    cnt["acc"] += 1
                    acc = ps_acc[aj]
                    P.op("pe", lambda e, acc=acc: e.matmul(acc[:].rearrange("p a b -> p (a b)"), lhsT=zer[:, 0:128], rhs=zer[:],
                                                           start=True, stop=False),
                         reads=[("zer",)], writes=[("psacc", aj)])
                    pend = []

                    def emit_pv(item, last_u, acc=acc, aj=aj, V1=V1, vres=vres, br=br):
                        pt, tj, kt, o = item
                        subs = [s_ for s_ in range(4) if (o - 4 <= s_ <= o if br == 1 else s_ >= o - 4)]
                        for s_ in subs:
                            P.op("pe", lambda e, acc=acc, pt=pt, V1=V1, kt=kt, s_=s_, last=(last_u and s_ == subs[-1]): e.matmul(
                                acc[:, s_, :], lhsT=pt[:, s_ * 128:(s_ + 1) * 128], rhs=V1[:, kt, :], start=False, stop=last),
                                reads=[("PT", tj), vres], writes=[("psacc", aj)])

                    for ui, (kt, o) in enumerate(units):
                        pj = cnt["pst3"] % 3
                        cnt["pst3"] += 1
                        pst = ps_st[pj]
                        ks = slice(kt * 128, (kt + 1) * 128)
                        need_wb = (o >= 4) if br == 0 else True
                        P.op("pe", lambda e, pst=pst, Kt=Kt, hp_=hp_, ks=ks, j=j, br=br, need_wb=need_wb: e.matmul(
                            pst[:], lhsT=Kt[hp_, ks], rhs=Qt[hp_, j, :], start=True, stop=(br == 1 and not need_wb)),
                            reads=[kres, ("Qt",)], writes=[("psst", pj)])
                        if br == 0:
                            P.op("pe", lambda e, pst=pst, ks=ks, need_wb=need_wb: e.matmul(
                                pst[:], lhsT=E[0:NBLK, ks], rhs=negbT[0:NBLK, :], start=False, stop=not need_wb),
                                reads=[("E",), ("negbT",)], writes=[("psst", pj)])
                        if need_wb:
                            P.op("pe", lambda e, pst=pst, o=o: e.matmul(pst[:], lhsT=ident[:], rhs=Wb[:, o, :], start=False, stop=True),
                                 reads=[("ident",), ("Wb",)], writes=[("psst", pj)])
                        tj = cnt["PT"] % 3
                        cnt["PT"] += 1
                        pt = PT[tj]
                        P.op("act", lambda e, pt=pt, pst=pst: e.activation(out=pt[:], in_=pst[:], func=AF.Exp),
                             reads=[("psst", pj)], writes=[("PT", tj)])
                        pend.append((pt, tj, kt, o))
                        if len(pend) > 2:
                            emit_pv(pend.pop(0), False)
                    while pend:
                        emit_pv(pend.pop(0), len(pend) == 0)
                    P.op("dve", lambda e, acc=acc: e.tensor_scalar(out=w4[:], in0=acc[:, :, 64], scalar1=1e-30, scalar2=None, op0=ALU.max),
                         reads=[("psacc", aj)], writes=[("w4",)])
                    P.op("dve", lambda e: e.reciprocal(out=w4[:], in_=w4[:]), reads=[("w4",)], writes=[("w4",)])
                    P.op("dve", lambda e, gcol=gcol: e.tensor_tensor(out=w4[:], in0=w4[:], in1=gtile[:, :, gcol], op=ALU.mult),
                         reads=[("w4",), ("gtile",)], writes=[("w4",)])
                    for s in range(4):
                        P.op("dve", lambda e, acc=acc, s=s, hl=hl: e.scalar_tensor_tensor(
                            out=oacc[:, s, hl, :], in0=acc[:, s, 0:64], scalar=w4[:, s:s + 1], in1=oacc[:, s, hl, :],
                            op0=ALU.mult, op1=ALU.add),
                            reads=[("psacc", aj), ("w4",), ("oacc", s, hl)], writes=[("oacc", s, hl)])
            ores = [("oacc", s, hl) for s in range(4) for hl in range(4)]
            P.op("act", lambda e: e.copy(out=oaccb[:], in_=oacc[:].rearrange("p s h d -> p s (h d)")), reads=ores, writes=[("oaccb",)])
            for jj in range(2):
                for s in range(4):
                    P.op("pe", lambda e, jj=jj, s=s: e.transpose(ps_tr[:, s * 128:(s + 1) * 128], oaccb[:, s, jj * 128:(jj + 1) * 128], ident[:]),
                         reads=[("oaccb",), ("ident",)], writes=[("pstr",)])
                P.op("act", lambda e, jj=jj: e.copy(out=o_fm[:, jj, :], in_=ps_tr[:]), reads=[("pstr",)], writes=[("ofm", jj)])
                P.dma("pool", lambda e, jj=jj, g=g, ts=ts: e.dma_start(out=oT[(2 * g + jj) * 128:(2 * g + jj + 1) * 128, ts], in_=o_fm[:, jj, :]),
                      "os", reads=[("ofm", jj)], writes=[("dram", "oT")])
    P.emit()
    return nc


def nsa_inputs(hm_bf_T, w_in, cmp_pos, cmp_w1, cmp_w2, hh, S_len):
    import ml_dtypes
    bf = ml_dtypes.bfloat16
    sw = np.concatenate([np.arange(32, 64), np.arange(0, 32)])

    def head_cols(base, h):
        return base + h * 64 + np.arange(64)

    chunks = []
    qh = [8 * hh + i for i in range(8)]
    for j in range(4):
        chunks.append(np.concatenate([head_cols(EV_OFF["q"], qh[2 * j]), head_cols(EV_OFF["q"], qh[2 * j + 1])]))
    for j in range(4):
        chunks.append(np.concatenate([head_cols(EV_OFF["q"], qh[2 * j])[sw], head_cols(EV_OFF["q"], qh[2 * j + 1])[sw]]))
    gg = [2 * hh, 2 * hh + 1]
    for nm in ("ksl", "kwn"):
        for g in gg:
            c = head_cols(EV_OFF[nm], g)
            chunks.append(np.concatenate([c, c]))
        for g in gg:
            c = head_cols(EV_OFF[nm], g)[sw]
            chunks.append(np.concatenate([c, c]))
    chunks.append(np.concatenate([head_cols(EV_OFF["kc"], gg[0]), head_cols(EV_OFF["kc"], gg[1])]))
    chunks.append(np.concatenate([head_cols(EV_OFF["kc"], gg[0])[sw], head_cols(EV_OFF["kc"], gg[1])[sw]]))
    chunks.append(np.concatenate([head_cols(EV_OFF["vc"], gg[0]), head_cols(EV_OFF["vc"], gg[1])]))
    wfm = tile_w_fm(w_in[:, np.concatenate(chunks)])
    tmc = np.concatenate([head_cols(EV_OFF["vsl"], gg[0]), head_cols(EV_OFF["vsl"], gg[1]),
                          head_cols(EV_OFF["vwn"], gg[0]), head_cols(EV_OFF["vwn"], gg[1]),
                          EV_OFF["gate"] + 24 * hh + np.arange(24)])
    wtm = tile_w_tm(w_in[:, tmc], NTM)
    w1 = cmp_w1.reshape(2, 32, 64, 256).transpose(0, 2, 1, 3).reshape(2, 64, 32 * 256)
    w1 = np.ascontiguousarray(np.concatenate([w1, w1], axis=1))
    w2k = cmp_w2[0].reshape(2, 128, 64).transpose(1, 0, 2)
    w2k = np.ascontiguousarray(np.concatenate([w2k, w2k], axis=2).reshape(128, 256))
    w2v = np.ascontiguousarray(cmp_w2[1].reshape(2, 128, 64).transpose(1, 0, 2).reshape(128, 128))
    posT = cmp_pos.transpose(2, 0, 1).reshape(64, 64)
    posT = np.ascontiguousarray(np.concatenate([posT, posT], axis=0))
    inv = (1.0 / (10000.0 ** (np.arange(0, 64, 2, dtype=np.float32) / np.float32(64)))).astype(np.float32)
    ang = np.arange(S_len, dtype=np.float32)[:, None] * inv[None, :]
    cos, sin = np.cos(ang).astype(np.float32).T, np.sin(ang).astype(np.float32).T
    p = np.arange(128)
    Ct = cos[p % 32]
    St = sin[p % 32] * np.where((p % 64) < 32, -1.0, 1.0).astype(np.float32)[:, None]
    j = np.arange(128)
    E = (np.arange(S_len)[None, :] // 64 == j[:, None]).astype(np.float32).astype(bf)
    c = np.arange(512)
    Wb = np.zeros((128, 8, 512), np.float32)
    for o in range(8):
        dlt = c[None, :] + 512 - 128 * o - p[:, None]
        Wb[:, o, :] = np.where((dlt >= 0) & (dlt < 512), 0.0, NEG)
    x = np.arange(1024) - 512
    Cw = np.where(16 * x[None, :] + 31 <= p[:, None], 0.0, NEG).astype(np.float32)
    jr = np.arange(256) - 128
    hi = (p[:, None] >= 64).astype(np.int64)
    A = (jr[None, :] <= hi - 2).astype(np.float32)
    forced = (jr[None, :] == hi) | (jr[None, :] == hi - 1)
    B = np.where(forced, 1e4, np.where(jr[None, :] > hi, -1.0, 0.0)).astype(np.float32)
    return {"hmT": hm_bf_T, "wfm": wfm, "wtm": wtm, "w1": w1, "w2k": w2k, "w2v": w2v, "posT": posT,
            "t_cq": np.ascontiguousarray(Ct * np.float32(0.125)), "t_sq": np.ascontiguousarray(St * np.float32(0.125)),
            "t_ck": np.ascontiguousarray(Ct), "t_sk": np.ascontiguousarray(St),
            "c_ident": np.eye(128, dtype=np.float32).astype(bf), "c_E": E,
            "c_Wb": Wb.reshape(128, 4096).astype(bf), "c_Cw": Cw.astype(bf),
            "c_AB": np.ascontiguousarray(np.concatenate([A, B], axis=1))}


NCORES = 4
SEQ = 8192
_PROGS = {}


def build_fused(S_len):
    nc = bass.Bass("TRN2", target_bir_lowering=False)
    ext = []
    nt = S_len // TT
    xT = nc.dram_tensor("xT", [D, S_len], F32, kind="ExternalInput").ap()
    xo = nc.dram_tensor("xo", [D, S_len], F32, kind="ExternalOutput").ap()
    x1 = nc.dram_tensor("x1", [D, S_len], F32, kind="Internal").ap()
    x2 = nc.dram_tensor("x2", [D, S_len], F32, kind="Internal").ap()
    hm0 = nc.dram_tensor("hm0", [D, S_len], BF16, kind="Internal").ap()
    hm1 = nc.dram_tensor("hm1", [D, S_len], BF16, kind="Internal").ap()
    oT0 = nc.dram_tensor("oT0", [2048, S_len], BF16, kind="Internal").ap()
    oT1 = nc.dram_tensor("oT1", [4096, S_len], BF16, kind="Internal").ap()

    def phase(prefix, fn, bind):
        P = Prog(nc, prefix, bind, ext)
        fn(P)
        nc.all_engine_barrier()
        nc.clear_and_free_semaphores(P.sem_handles)
        nc.all_engine_barrier()

    phase("A_", lambda P: build_tok(nt, 0, 1, True, P=P), {"xT": xT, "xo": x1, "hm": hm0})
    for hh in range(2):
        phase(f"N{hh}_", lambda P: build_nsa(S_len, P=P), {"hmT": hm0, "oT": oT0[hh * 512:(hh + 1) * 512, :]})
    for hh in range(2):
        phase(f"S{hh}_", lambda P: build_ssd(S_len, P=P), {"hmT": hm0, "oT": oT0[1024 + hh * 512:1024 + (hh + 1) * 512, :]})
    phase("C_", lambda P: build_tok(nt, 16, 2, True, P=P), {"xT": x1, "oT": oT0, "xo": x2, "hm": hm1})
    for hh in range(2):
        phase(f"R{hh}_", lambda P: build_ret(S_len, P=P), {"hmT": hm1, "oT": oT1[hh * 2048:(hh + 1) * 2048, :]})
    phase("E_", lambda P: build_tok(nt, 32, 1, False, P=P), {"xT": x2, "oT": oT1, "xo": xo})
    return nc, ext


def _ffn_maps(m, i, wg, wu, wd):
    m[f"wg{i}"] = tile_w_in_out(wg, KD, KF)
    m[f"wu{i}"] = tile_w_in_out(wu, KD, KF)
    m[f"wd{i}"] = tile_w_in_out(wd, KF, KD)


def fused_inputs(S_len, norm_g, wgs, wus, wds, ev_w_in, ev_cmp_pos, ev_cmp_w1, ev_cmp_w2, ev_conv_w, ev_conv_b,
                 ev_dt_bias, ev_a_log, ev_d_skip, ev_ssm_norm, ev_w_out, od_w_in, od_w_out):
    ph = {}
    a = {}
    _ffn_maps(a, 0, wgs[0, 0], wus[0, 0], wds[0, 0])
    a["g_all"] = np.concatenate([gain_cols(norm_g[0, 0]), gain_cols(norm_g[0, 1]), gain_cols(norm_g[0, 2])], axis=1)
    ph["A_"] = a
    for hh in range(2):
        ph[f"N{hh}_"] = nsa_inputs(None, ev_w_in, ev_cmp_pos, ev_cmp_w1, ev_cmp_w2, hh, S_len)
        ph[f"S{hh}_"] = ssd_inputs(None, ev_w_in, ev_conv_w, ev_conv_b, ev_dt_bias, ev_a_log, ev_d_skip, ev_ssm_norm, hh)
        ph[f"R{hh}_"] = ret_inputs(None, od_w_in, 4 * hh, S_len)
    c = {"wout": tile_w_in_out(ev_w_out, 16, KD)}
    _ffn_maps(c, 0, wgs[0, 1], wus[0, 1], wds[0, 1])
    _ffn_maps(c, 1, wgs[1, 0], wus[1, 0], wds[1, 0])
    c["g_all"] = np.concatenate([gain_cols(norm_g[0, 3]), gain_cols(norm_g[0, 4]), gain_cols(norm_g[0, 5]),
                                 gain_cols(norm_g[1, 0]), gain_cols(norm_g[1, 1]), gain_cols(norm_g[1, 2])], axis=1)
    ph["C_"] = c
    e = {"wout": tile_w_in_out(od_w_out, 32, KD)}
    _ffn_maps(e, 0, wgs[1, 1], wus[1, 1], wds[1, 1])
    e["g_all"] = np.concatenate([gain_cols(norm_g[1, 3]), gain_cols(norm_g[1, 4]), gain_cols(norm_g[1, 5])], axis=1)
    ph["E_"] = e
    return ph


def kernel(x, norm_g, ffn_w_gate, ffn_w_up, ffn_w_down, ev_w_in, ev_cmp_pos, ev_cmp_w1, ev_cmp_w2,
           ev_conv_w, ev_conv_b, ev_dt_bias, ev_a_log, ev_d_skip, ev_ssm_norm, ev_w_out, od_w_in, od_w_out):
    f32 = np.float32
    A = lambda t: np.asarray(t, f32)
    x = A(x)
    B, S_len = x.shape[0], x.shape[1]
    if "F" not in _PROGS:
        _PROGS["F"] = build_fused(S_len)
    nc, ext = _PROGS["F"]
    ph = fused_inputs(S_len, A(norm_g), A(ffn_w_gate), A(ffn_w_up), A(ffn_w_down), A(ev_w_in)[0], A(ev_cmp_pos)[0],
                      A(ev_cmp_w1)[0], A(ev_cmp_w2)[0], A(ev_conv_w)[0], A(ev_conv_b)[0], A(ev_dt_bias)[0],
                      A(ev_a_log)[0], A(ev_d_skip)[0], A(ev_ssm_norm)[0], A(ev_w_out)[0], A(od_w_in)[0], A(od_w_out)[0])
    shared = {}
    for full, name in ext:
        prefix = full[:len(full) - len(name)]
        shared[full] = ph[prefix][name]
    maps = [dict(shared, xT=np.ascontiguousarray(x[b].T)) for b in range(B)]
    res = run_bass_kernel_spmd(nc, maps, core_ids=list(range(B)))
    out = np.empty((B, S_len, D), f32)
    for b in range(B):
        out[b] = res.results[b]["xo"].T
    return out
```

```python
from contextlib import ExitStack
import numpy as np
import concourse.bass as bass
import concourse.mybir as mybir
from concourse.bass_utils import run_bass_kernel_spmd

F32 = mybir.dt.float32
BF16 = mybir.dt.bfloat16
AF = mybir.ActivationFunctionType
ALU = mybir.AluOpType
AX = mybir.AxisListType

D = 2048
DFF = 5632
KD = D // 128
KF = DFF // 128
TT = 512
EPS = 1e-6

EPOCH = 30000
SAME_ENGINE_SYNC = ("act", "dve", "pool")


class Prog:
    ENGS = ("pe", "act", "dve", "pool", "sp")

    def __init__(self, nc, prefix="", bind=None, ext=None):
        self.nc = nc
        self.prefix = prefix
        self.bind = bind
        self.ext = ext
        self.ops = {e: [] for e in self.ENGS}
        self.res = {}
        self.dma_cnt = {}
        self.dma_rr = {}
        self.dkeys = {}
        self.stack = ExitStack()
        self.nm = 0

    def sb(self, name, shape, dt):
        return self.stack.enter_context(self.nc.sbuf_tensor(self.prefix + name, list(shape), dt))

    def ps(self, name, shape, dt=F32):
        return self.stack.enter_context(self.nc.psum_tensor(self.prefix + name, list(shape), dt))

    def dram(self, name, shape, dt, kind="Internal"):
        if self.bind is not None and name in self.bind:
            ap = self.bind[name]
            assert list(ap.shape) == list(shape), (name, ap.shape, shape)
            return ap
        if self.bind is not None and kind == "ExternalOutput":
            raise AssertionError(f"unbound output {name}")
        if self.ext is not None and kind == "ExternalInput":
            self.ext.append((self.prefix + name, name))
        return self.nc.dram_tensor(self.prefix + name, list(shape), dt, kind=kind).ap()

    def _deps(self, reads, writes):
        deps = []
        for r in reads:
            st = self.res.get(r)
            if st and st["w"] is not None:
                deps.append(st["w"])
        for w in writes:
            st = self.res.get(w)
            if st:
                if st["w"] is not None:
                    deps.append(st["w"])
                deps.extend(st["r"])
        return deps

    def _commit(self, tok, reads, writes):
        for r in reads:
            st = self.res.setdefault(r, {"w": None, "r": []})
            st["r"].append(tok)
        for w in writes:
            self.res[w] = {"w": tok, "r": []}

    def op(self, eng, fn, reads=(), writes=()):
        deps = self._deps(reads, writes)
        idx = len(self.ops[eng])
        self.ops[eng].append({"fn": fn, "deps": deps, "sig": False, "dma": None})
        self._commit(("e", eng, idx), reads, writes)

    NSLOT = 20

    def dma(self, eng, fn, stream, reads=(), writes=()):
        deps = self._deps(reads, writes)
        k = self.dma_rr.get(eng, 0)
        self.dma_rr[eng] = k + 1
        slot = (eng, k % self.NSLOT)
        n = self.dma_cnt.get(slot, 0) + 1
        self.dma_cnt[slot] = n
        if n > 1:
            deps.append(("d", slot, 16 * (n - 1)))
        self.ops[eng].append({"fn": fn, "deps": deps, "sig": False, "dma": slot})
        self._commit(("d", slot, 16 * n), reads, writes)

    def emit(self, final_waits=()):
        nc = self.nc
        for e in self.ENGS:
            for o in self.ops[e]:
                for d in o["deps"]:
                    if d[0] == "e":
                        if d[1] == e and e not in SAME_ENGINE_SYNC:
                            continue
                        self.ops[d[1]][d[2]]["sig"] = True
        sigval = {}
        nep = {}
        for e in self.ENGS:
            n = 0
            for i, o in enumerate(self.ops[e]):
                if o["sig"]:
                    sigval[(e, i)] = (n // EPOCH, n % EPOCH + 1)
                    n += 1
            nep[e] = (n + EPOCH - 1) // EPOCH
        sems = {}
        for e in self.ENGS:
            for k in range(nep[e]):
                sems[("e", e, k)] = nc.alloc_semaphore(name=f"{self.prefix}s_{e}_{k}")
        for s in self.dma_cnt:
            sems[("d", s)] = nc.alloc_semaphore(name=f"{self.prefix}d_{s[0]}_{s[1]}")
        self.sem_handles = list(sems.values())
        ops = self.ops
        dma_cnt = self.dma_cnt

        def run(e, eng):
            clock = {}
            for i, o in enumerate(ops[e]):
                need = {}
                for d in o["deps"]:
                    if d[0] == "e":
                        if d[1] == e and e not in SAME_ENGINE_SYNC:
                            continue
                        key = ("e", d[1])
                        val = sigval[(d[1], d[2])]
                    else:
                        key = ("d", d[1])
                        val = (0, d[2])
                    if clock.get(key, (-1, 0)) >= val:
                        continue
                    if need.get(key, (-1, 0)) < val:
                        need[key] = val
                for key, val in need.items():
                    clock[key] = val
                    if key[0] == "e":
                        eng.wait_ge(sems[("e", key[1], val[0])], val[1])
                    else:
                        eng.wait_ge(sems[("d", key[1])], val[1])
                ins = o["fn"](eng)
                if o["dma"] is not None:
                    ins.then_inc(sems[("d", o["dma"])], 16)
                elif o["sig"]:
                    ins.then_inc(sems[("e", e, sigval[(e, i)][0])], 1)
            if e == "sp":
                for s in dma_cnt:
                    if clock.get(("d", s), (-1, 0)) < (0, 16 * dma_cnt[s]):
                        eng.wait_ge(sems[("d", s)], 16 * dma_cnt[s])

        with nc.Block() as block:
            @block.tensor
            def _(eng):
                run("pe", eng)

            @block.scalar
            def _(eng):
                run("act", eng)

            @block.vector
            def _(eng):
                run("dve", eng)

            @block.gpsimd
            def _(eng):
                run("pool", eng)

            @block.sync
            def _(eng):
                run("sp", eng)
        self.stack.close()


class Conv:
    def __init__(self, P, width=1024, nbuf=3):
        self.P = P
        self.w = width
        self.nb = nbuf
        self.f = [P.sb(f"cvf{i}", [128, width], F32) for i in range(nbuf)]
        self.b = [P.sb(f"cvb{i}", [128, width], BF16) for i in range(nbuf)]
        self.i = 0

    def pieces(self, dst, src, tag):
        n = src.shape[1]
        out = []
        c0 = 0
        while c0 < n:
            w = min(self.w, n - c0)
            out.append((dst[:, c0:c0 + w], src[:, c0:c0 + w], w, tag))
            c0 += w
        return out

    def piece(self, d_ap, s_ap, w, tag):
        P = self.P
        keys = P.dkeys.setdefault(tag, [])
        i = self.i % self.nb
        self.i += 1
        f, b = self.f[i], self.b[i]
        P.dma("sp", lambda e, f=f, s_ap=s_ap, w=w: e.dma_start(out=f[:, :w], in_=s_ap),
              "cvl", writes=[("cvf", i)])
        eng = ("dve", "pool", "act")[self.i % 3]
        if eng == "act":
            P.op("act", lambda e, f=f, b=b, w=w: e.copy(out=b[:, :w], in_=f[:, :w]),
                 reads=[("cvf", i)], writes=[("cvb", i)])
        else:
            P.op(eng, lambda e, f=f, b=b, w=w: e.tensor_copy(out=b[:, :w], in_=f[:, :w]),
                 reads=[("cvf", i)], writes=[("cvb", i)])
        key = ("dram", tag, len(keys))
        keys.append(key)
        P.dma("sp", lambda e, b=b, d_ap=d_ap, w=w: e.dma_start(out=d_ap, in_=b[:, :w]),
              "cvs", reads=[("cvb", i)], writes=[key])

    def run(self, dst, src, tag):
        for pc in self.pieces(dst, src, tag):
            self.piece(*pc)


class Extra:
    def __init__(self, cv, jobs, n_iters):
        self.cv = cv
        self.todo = []
        for d, s_, tag in jobs:
            self.todo.extend(cv.pieces(d, s_, tag))
        self.per = (len(self.todo) + n_iters - 1) // max(1, n_iters)

    def step(self):
        for _ in range(self.per):
            if self.todo:
                self.cv.piece(*self.todo.pop(0))

    def flush(self):
        while self.todo:
            self.cv.piece(*self.todo.pop(0))


def tok_cast_jobs(P, which, mix_kc):
    jobs = []
    for nm in which:
        if nm == "wout":
            f = P.dram("wout", [KD, 128, mix_kc * 128], F32, "ExternalInput")
            b = P.dram("wout_b", [KD, 128, mix_kc * 128], BF16)
            n = KD
        elif nm[:2] in ("wg", "wu"):
            f = P.dram(nm, [KF, 128, KD * 128], F32, "ExternalInput")
            b = P.dram(nm + "_b", [KF, 128, KD * 128], BF16)
            n = KF
        else:
            f = P.dram(nm, [KD, 128, KF * 128], F32, "ExternalInput")
            b = P.dram(nm + "_b", [KD, 128, KF * 128], BF16)
            n = KD
        for c in range(n):
            jobs.append((b[c], f[c], ("x", nm)))
    return jobs


def rms_rstd(P, S, src_chunks, src_res, nk, out_rstd, out_res, dim):
    for k in range(nk):
        j = S.sqi % 2
        S.sqi += 1
        sq = S.sq[j]
        src = src_chunks[k]
        P.op("act", lambda e, sq=sq, src=src: e.activation(out=sq[:], in_=src, func=AF.Square),
             reads=[src_res[k]], writes=[("sq", j)])
        P.op("pe", lambda e, sq=sq, k=k: e.matmul(S.ps_ss[:], lhsT=S.ones[:], rhs=sq[:],
                                                  start=(k == 0), stop=(k == nk - 1)),
             reads=[("sq", j), ("ones",)], writes=[("ps_ss",)])
    P.op("act", lambda e: e.activation(out=S.lnt[:], in_=S.ps_ss[:], func=AF.Ln,
                                       bias=S.epsb[:], scale=1.0 / dim),
         reads=[("ps_ss",), ("epsb",)], writes=[("lnt",)])
    P.op("act", lambda e: e.activation(out=out_rstd[:], in_=S.lnt[:], func=AF.Exp, scale=-0.5),
         reads=[("lnt",)], writes=[out_res])


class TokState:
    pass


def build_tok(n_tiles, mix_kc, n_ffn, emit_hm, P=None, preconv=False):
    if P is None:
        P = Prog(bass.Bass("TRN2", target_bir_lowering=False))
    nc = P.nc
    NT = n_tiles * TT
    n_g = (1 if mix_kc else 0) + 2 * n_ffn + (1 if emit_hm else 0)
    xT = P.dram("xT", [D, NT], F32, "ExternalInput")
    g_all = P.dram("g_all", [128, n_g * KD], F32, "ExternalInput")
    xo = P.dram("xo", [D, NT], F32, "ExternalOutput")
    if emit_hm:
        hm_o = P.dram("hm", [D, NT], BF16, "ExternalOutput")
    if mix_kc:
        oT = P.dram("oT", [mix_kc * 128, NT], BF16, "ExternalInput")
        wo_f = None if preconv else P.dram("wout", [KD, 128, mix_kc * 128], F32, "ExternalInput")
        wo_b = P.dram("wout_b", [KD, 128, mix_kc * 128], BF16)
    wg_f, wu_f, wd_f, wg_b, wu_b, wd_b = [], [], [], [], [], []
    for i in range(n_ffn):
        if not preconv:
            wg_f.append(P.dram(f"wg{i}", [KF, 128, KD * 128], F32, "ExternalInput"))
            wu_f.append(P.dram(f"wu{i}", [KF, 128, KD * 128], F32, "ExternalInput"))
            wd_f.append(P.dram(f"wd{i}", [KD, 128, KF * 128], F32, "ExternalInput"))
        wg_b.append(P.dram(f"wg{i}_b", [KF, 128, KD * 128], BF16))
        wu_b.append(P.dram(f"wu{i}_b", [KF, 128, KD * 128], BF16))
        wd_b.append(P.dram(f"wd{i}_b", [KD, 128, KF * 128], BF16))

    S = TokState()
    S.sqi = 0
    S.ones = P.sb("ones", [128, 128], BF16)
    S.epsb = P.sb("epsb", [128, 1], F32)
    S.g = P.sb("sb_g", [128, n_g * KD], F32)
    S.x = P.sb("sb_x", [128, KD, TT], F32)
    S.xn = P.sb("sb_xn", [128, KD, TT], BF16)
    S.y = P.sb("sb_y", [128, KD, TT], F32)
    S.h = P.sb("sb_h", [128, KF, TT], BF16)
    S.sq = [P.sb(f"sq{i}", [128, TT], BF16) for i in range(2)]
    S.lnt = P.sb("lnt", [128, TT], F32)
    S.rstd = P.sb("rstd", [128, TT], F32)
    S.sg = [P.sb(f"sg{i}", [128, TT], F32) for i in range(2)]
    S.wgu = [P.sb(f"sb_wgu{i}", [128, 2, KD * 128], BF16) for i in range(2)]
    S.wd = [P.sb(f"sb_wd{i}", [128, KF * 128], BF16) for i in range(2)]
    S.ps_ss = P.ps("ps_ss", [128, TT])
    S.ps_g = [P.ps(f"ps_g{i}", [128, TT]) for i in range(2)]
    S.ps_u = [P.ps(f"ps_u{i}", [128, TT]) for i in range(2)]
    S.ps_y = [P.ps(f"ps_y{i}", [128, TT]) for i in range(2)]
    cv = None if preconv else Conv(P)

    P.op("dve", lambda e: e.memset(S.ones[:], 1.0), writes=[("ones",)])
    P.op("dve", lambda e: e.memset(S.epsb[:], EPS), writes=[("epsb",)])
    P.dma("pool", lambda e: e.dma_start(out=S.g[:], in_=g_all), "gl", writes=[("g",)])
    if mix_kc and not preconv:
        for o in range(KD):
            cv.run(wo_b[o], wo_f[o], "wo")
    for i in range(n_ffn if not preconv else 0):
        for f in range(KF):
            cv.run(wg_b[i][f], wg_f[i][f], f"wg{i}")
            cv.run(wu_b[i][f], wu_f[i][f], f"wu{i}")
        for o in range(KD):
            cv.run(wd_b[i][o], wd_f[i][o], f"wd{i}")

    cnt = {"wgu": 0, "wd": 0, "psg": 0, "psy": 0}

    def proj_norm_res(src, src_res, nk, w_b, w_tag, gcol, coef):
        for o in range(KD):
            j = cnt["wd"] % 2
            cnt["wd"] += 1
            wt = S.wd[j]
            P.dma("sp", lambda e, wt=wt, o=o: e.dma_start(out=wt[:, :nk * 128], in_=w_b[o]),
                  f"wd{j}", reads=P.dkeys.get(w_tag, []), writes=[("wd", j)])
            pj = cnt["psy"] % 2
            cnt["psy"] += 1
            py = S.ps_y[pj]
            for k in range(nk):
                P.op("pe", lambda e, py=py, wt=wt, k=k: e.matmul(
                    py[:], lhsT=wt[:, k * 128:(k + 1) * 128], rhs=src[:, k, :],
                    start=(k == 0), stop=(k == nk - 1)),
                    reads=[("wd", j), src_res], writes=[("psy", pj)])
            P.op("act", lambda e, py=py, o=o: e.copy(out=S.y[:, o, :], in_=py[:]),
                 reads=[("psy", pj)], writes=[("y", o)])
        rms_rstd(P, S, [S.y[:, k, :] for k in range(KD)], [("y", k) for k in range(KD)], KD,
                 S.rstd, ("rstd",), D)
        for k in range(KD):
            P.op("dve", lambda e, k=k: e.scalar_tensor_tensor(
                out=S.y[:, k, :], in0=S.y[:, k, :], scalar=S.g[:, gcol * KD + k:gcol * KD + k + 1],
                in1=S.rstd[:], op0=ALU.mult, op1=ALU.mult),
                reads=[("y", k), ("rstd",), ("g",)], writes=[("y", k)])
            P.op("dve", lambda e, k=k: e.scalar_tensor_tensor(
                out=S.x[:, k, :], in0=S.y[:, k, :], scalar=float(coef),
                in1=S.x[:, k, :], op0=ALU.mult, op1=ALU.add),
                reads=[("y", k), ("x", k)], writes=[("x", k)])

    def norm_to(dst, dst_res, gcol):
        rms_rstd(P, S, [S.x[:, k, :] for k in range(KD)], [("x", k) for k in range(KD)], KD,
                 S.rstd, ("rstd",), D)
        for k in range(KD):
            P.op("dve", lambda e, k=k: e.scalar_tensor_tensor(
                out=dst[:, k, :], in0=S.x[:, k, :], scalar=S.g[:, gcol * KD + k:gcol * KD + k + 1],
                in1=S.rstd[:], op0=ALU.mult, op1=ALU.mult),
                reads=[("x", k), ("rstd",), ("g",)], writes=[dst_res])

    def ffn(i, gcol):
        norm_to(S.xn, ("xn",), gcol)
        for f in range(KF):
            j = cnt["wgu"] % 2
            cnt["wgu"] += 1
            wt = S.wgu[j]
            P.dma("sp", lambda e, wt=wt, f=f: e.dma_start(out=wt[:, 0, :], in_=wg_b[i][f]),
                  f"wgu{j}", reads=P.dkeys.get(f"wg{i}", []), writes=[("wgu", j)])
            P.dma("sp", lambda e, wt=wt, f=f: e.dma_start(out=wt[:, 1, :], in_=wu_b[i][f]),
                  f"wgu{j}", reads=P.dkeys.get(f"wu{i}", []), writes=[("wgu", j)])
            pj = cnt["psg"] % 2
            cnt["psg"] += 1
            pg, pu, sg = S.ps_g[pj], S.ps_u[pj], S.sg[pj]
            for k in range(KD):
                P.op("pe", lambda e, pg=pg, wt=wt, k=k: e.matmul(
                    pg[:], lhsT=wt[:, 0, k * 128:(k + 1) * 128], rhs=S.xn[:, k, :],
                    start=(k == 0), stop=(k == KD - 1)),
                    reads=[("wgu", j), ("xn",)], writes=[("psg", pj)])
            for k in range(KD):
                P.op("pe", lambda e, pu=pu, wt=wt, k=k: e.matmul(
                    pu[:], lhsT=wt[:, 1, k * 128:(k + 1) * 128], rhs=S.xn[:, k, :],
                    start=(k == 0), stop=(k == KD - 1)),
                    reads=[("wgu", j), ("xn",)], writes=[("psu", pj)])
            P.op("act", lambda e, pg=pg, sg=sg: e.activation(out=sg[:], in_=pg[:], func=AF.Silu),
                 reads=[("psg", pj)], writes=[("sg", pj)])
            P.op("dve", lambda e, pu=pu, sg=sg, f=f: e.tensor_tensor(
                out=S.h[:, f, :], in0=sg[:], in1=pu[:], op=ALU.mult),
                reads=[("sg", pj), ("psu", pj)], writes=[("h",)])
        proj_norm_res(S.h, ("h",), KF, wd_b[i], f"wd{i}", gcol + 1, 0.5)

    for t in range(n_tiles):
        ts = slice(t * TT, (t + 1) * TT)
        for k in range(KD):
            P.dma("pool", lambda e, k=k, ts=ts: e.dma_start(out=S.x[:, k, :], in_=xT[k * 128:(k + 1) * 128, ts]),
                  "xl", writes=[("x", k)])
        gc = 0
        if mix_kc:
            src = S.h
            for k in range(mix_kc):
                P.dma("pool", lambda e, k=k, ts=ts: e.dma_start(out=S.h[:, k, :], in_=oT[k * 128:(k + 1) * 128, ts]),
                      "ol", writes=[("h",)])
            proj_norm_res(S.h, ("h",), mix_kc, wo_b, "wo", gc, 1.0)
            gc += 1
        for i in range(n_ffn):
            ffn(i, gc)
            gc += 2
        if emit_hm:
            norm_to(S.xn, ("xn",), gc)
            for k in range(KD):
                P.dma("pool", lambda e, k=k, ts=ts: e.dma_start(out=hm_o[k * 128:(k + 1) * 128, ts], in_=S.xn[:, k, :]),
                      "hs", reads=[("xn",)], writes=[("dram", "hm")])
        for k in range(KD):
            P.dma("pool", lambda e, k=k, ts=ts: e.dma_start(out=xo[k * 128:(k + 1) * 128, ts], in_=S.x[:, k, :]),
                  "xs", reads=[("x", k)], writes=[("dram", "xo")])
    P.emit()
    return nc


def tile_w_in_out(W, kc_in, kc_out):
    return np.ascontiguousarray(
        W.reshape(kc_in, 128, kc_out, 128).transpose(2, 1, 0, 3).reshape(kc_out, 128, kc_in * 128))


def gain_cols(g):
    return np.ascontiguousarray(g.reshape(-1, 128).T)


RH = 4
RDK = 256
RDV = 512


def build_ret(S_len, P=None, extra_fn=None):
    if P is None:
        P = Prog(bass.Bass("TRN2", target_bir_lowering=False))
    nc = P.nc
    n_tiles = S_len // TT
    hmT = P.dram("hmT", [D, S_len], BF16, "ExternalInput")
    wq_f = P.dram("wq", [2 * RH, 128, D], F32, "ExternalInput")
    wk_f = P.dram("wk", [2 * RH, 128, D], F32, "ExternalInput")
    wv_f = P.dram("wv", [RH, 128, KD * RDV], F32, "ExternalInput")
    wg_f = P.dram("wgt", [RH, 128, KD * RDV], F32, "ExternalInput")
    wq_b = P.dram("wq_b", [2 * RH, 128, D], BF16)
    wk_b = P.dram("wk_b", [2 * RH, 128, D], BF16)
    wv_b = P.dram("wv_b", [RH, 128, KD * RDV], BF16)
    wg_b = P.dram("wg_b", [RH, 128, KD * RDV], BF16)
    cosq = P.dram("cosq", [128, S_len], F32, "ExternalInput")
    sinq = P.dram("sinq", [128, S_len], F32, "ExternalInput")
    cosk = P.dram("cosk", [128, S_len], F32, "ExternalInput")
    sink = P.dram("sink", [128, S_len], F32, "ExternalInput")
    c_inner = P.dram("c_inner", [128, RH * 128], F32, "ExternalInput")
    c_cross = P.dram("c_cross", [128, RH * 128], F32, "ExternalInput")
    c_misc = P.dram("c_misc", [128, RH * 2], F32, "ExternalInput")
    c_ident = P.dram("c_ident", [128, 128], BF16, "ExternalInput")
    oT = P.dram("oT", [RH * RDV, S_len], BF16, "ExternalOutput")

    hm = P.sb("r_hm", [128, KD, TT], BF16)
    wfm = [P.sb(f"r_wfm{i}", [128, D], BF16) for i in range(2)]
    wtm = [P.sb(f"r_wtm{i}", [128, KD * RDV], BF16) for i in range(2)]
    inner = P.sb("r_inner", [128, RH * 128], F32)
    cross = P.sb("r_cross", [128, RH * 128], F32)
    misc = P.sb("r_misc", [128, RH * 2], F32)
    ident = P.sb("r_ident", [128, 128], BF16)
    epsb = P.sb("r_epsb", [128, 1], F32)
    tab = P.sb("r_tab", [128, 4, TT], F32)
    raw = P.sb("r_raw", [128, 2, TT], F32)
    tmp = P.sb("r_tmp", [128, 2, TT], F32)
    qT = P.sb("r_qT", [128, RH, 2, TT], BF16)
    kT = P.sb("r_kT", [128, RH, 2, TT], BF16)
    v_sb = P.sb("r_v", [128, 4, RH, RDV], BF16)
    g_sb = P.sb("r_g", [128, 4, RH, RDV], BF16)
    ktl = P.sb("r_ktl", [128, RDK], BF16)
    attT = P.sb("r_attT", [128, 128], BF16)
    qs = P.sb("r_qs", [128, 2, 128], BF16)
    state = P.sb("r_state", [128, RH, 2, RDV], F32)
    state_b = P.sb("r_state_b", [128, RH, 2, RDV], BF16)
    stats = P.sb("r_stats", [128, 6], F32)
    mv = P.sb("r_mv", [128, 2], F32)
    rstd = P.sb("r_rstd", [128, 1], F32)
    yn = P.sb("r_yn", [128, RDV], F32)
    o_tm = P.sb("r_otm", [128, RDV], BF16)
    o_fm = P.sb("r_ofm", [128, RH * 4, TT], BF16)
    ps_a = [P.ps(f"r_psa{i}", [128, TT]) for i in range(2)]
    ps_y = P.ps("r_psy", [128, RDV])
    ps_st = [P.ps(f"r_psst{i}", [128, RDV]) for i in range(2)]
    ps_att = P.ps("r_psatt", [128, 128])
    ps_tr = [P.ps(f"r_pstr{i}", [128, 512], BF16) for i in range(2)]
    cv = Conv(P)

    P.op("dve", lambda e: e.memset(epsb[:], EPS), writes=[("epsb",)])
    P.op("dve", lambda e: e.memset(state[:], 0.0), writes=[("state", h, hf) for h in range(RH) for hf in range(2)])
    P.op("pool", lambda e: e.memset(state_b[:], 0.0), writes=[("state_b", h, hf) for h in range(RH) for hf in range(2)])
    P.dma("pool", lambda e: e.dma_start(out=inner[:], in_=c_inner), "cl", writes=[("inner",)])
    P.dma("pool", lambda e: e.dma_start(out=cross[:], in_=c_cross), "cl", writes=[("cross",)])
    P.dma("pool", lambda e: e.dma_start(out=misc[:], in_=c_misc), "cl", writes=[("misc",)])
    P.dma("pool", lambda e: e.dma_start(out=ident[:], in_=c_ident), "cl", writes=[("ident",)])
    for c in range(2 * RH):
        cv.run(wq_b[c], wq_f[c], "wq")
        cv.run(wk_b[c], wk_f[c], "wk")
    for h in range(RH):
        cv.run(wv_b[h], wv_f[h], "wv")
        cv.run(wg_b[h], wg_f[h], "wg")
    extra = Extra(cv, extra_fn(P) if extra_fn else [], n_tiles * 4)

    cnt = {"wfm": 0, "wtm": 0, "psa": 0, "pstr": 0, "psst": 0}

    for t in range(n_tiles):
        ts = slice(t * TT, (t + 1) * TT)
        for k in range(KD):
            P.dma("pool", lambda e, k=k, ts=ts: e.dma_start(out=hm[:, k, :], in_=hmT[k * 128:(k + 1) * 128, ts]),
                  "hl", writes=[("hm",)])
        for i, src in enumerate((cosq, sinq, cosk, sink)):
            P.dma("pool", lambda e, i=i, src=src, ts=ts: e.dma_start(out=tab[:, i, :], in_=src[:, ts]),
                  "tl", writes=[("tab",)])
        for which, w_b, dst, tag, ci, si in (("q", wq_b, qT, "wq", 0, 1), ("k", wk_b, kT, "wk", 2, 3)):
            for h in range(RH):
                for hf in range(2):
                    j = cnt["wfm"] % 2
                    cnt["wfm"] += 1
                    wt = wfm[j]
                    P.dma("sp", lambda e, wt=wt, w_b=w_b, c=2 * h + hf: e.dma_start(out=wt[:], in_=w_b[c]),
                          f"wfm{j}", reads=P.dkeys[tag], writes=[("wfm", j)])
                    pj = cnt["psa"] % 2
                    cnt["psa"] += 1
                    pa = ps_a[pj]
                    for k in range(KD):
                        P.op("pe", lambda e, pa=pa, wt=wt, k=k: e.matmul(
                            pa[:], lhsT=wt[:, k * 128:(k + 1) * 128], rhs=hm[:, k, :],
                            start=(k == 0), stop=(k == KD - 1)),
                            reads=[("wfm", j), ("hm",)], writes=[("psa", pj)])
                    P.op("act", lambda e, pa=pa, hf=hf: e.copy(out=raw[:, hf, :], in_=pa[:]),
                         reads=[("psa", pj)], writes=[("raw", hf)])
                P.op("dve", lambda e, ci=ci: e.tensor_tensor(out=tmp[:, 0, :], in0=raw[:, 0, :], in1=tab[:, ci, :], op=ALU.mult),
                     reads=[("raw", 0), ("tab",)], writes=[("tmp", 0)])
                P.op("dve", lambda e, si=si: e.tensor_tensor(out=tmp[:, 1, :], in0=raw[:, 1, :], in1=tab[:, si, :], op=ALU.mult),
                     reads=[("raw", 1), ("tab",)], writes=[("tmp", 1)])
                P.op("dve", lambda e, dst=dst, h=h: e.tensor_tensor(out=dst[:, h, 0, :], in0=tmp[:, 0, :], in1=tmp[:, 1, :], op=ALU.subtract),
                     reads=[("tmp", 0), ("tmp", 1)], writes=[(which, h)])
                P.op("dve", lambda e, ci=ci: e.tensor_tensor(out=tmp[:, 0, :], in0=raw[:, 1, :], in1=tab[:, ci, :], op=ALU.mult),
                     reads=[("raw", 1), ("tab",), (which, h)], writes=[("tmp", 0)])
                P.op("dve", lambda e, si=si: e.tensor_tensor(out=tmp[:, 1, :], in0=raw[:, 0, :], in1=tab[:, si, :], op=ALU.mult),
                     reads=[("raw", 0), ("tab",), (which, h)], writes=[("tmp", 1)])
                P.op("dve", lambda e, dst=dst, h=h: e.tensor_tensor(out=dst[:, h, 1, :], in0=tmp[:, 0, :], in1=tmp[:, 1, :], op=ALU.add),
                     reads=[("tmp", 0), ("tmp", 1)], writes=[(which, h)])
        for which, w_b, tag in (("v", wv_b, "wv"), ("g", wg_b, "wg")):
            for h in range(RH):
                j = cnt["wtm"] % 2
                cnt["wtm"] += 1
                wt = wtm[j]
                P.dma("sp", lambda e, wt=wt, w_b=w_b, h=h: e.dma_start(out=wt[:], in_=w_b[h]),
                      f"wtm{j}", reads=P.dkeys[tag], writes=[("wtm", j)])
                for sub in range(4):
                    pj = cnt["psa"] % 2
                    cnt["psa"] += 1
                    pa = ps_a[pj]
                    for k in range(KD):
                        P.op("pe", lambda e, pa=pa, wt=wt, k=k, sub=sub: e.matmul(
                            pa[:], lhsT=hm[:, k, sub * 128:(sub + 1) * 128], rhs=wt[:, k * RDV:(k + 1) * RDV],
                            start=(k == 0), stop=(k == KD - 1)),
                            reads=[("wtm", j), ("hm",)], writes=[("psa", pj)])
                    if which == "v":
                        P.op("act", lambda e, pa=pa, sub=sub, h=h: e.copy(out=v_sb[:, sub, h, :], in_=pa[:]),
                             reads=[("psa", pj)], writes=[("v", sub, h)])
                    else:
                        P.op("act", lambda e, pa=pa, sub=sub, h=h: e.activation(out=g_sb[:, sub, h, :], in_=pa[:], func=AF.Silu),
                             reads=[("psa", pj)], writes=[("g", sub, h)])
        for sub in range(4):
            cs = slice(sub * 128, (sub + 1) * 128)
            extra.step()
            for h in range(RH):
                pj = cnt["pstr"] % 2
                cnt["pstr"] += 1
                ptr = ps_tr[pj]
                for hf in range(2):
                    P.op("pe", lambda e, ptr=ptr, h=h, hf=hf, cs=cs: e.transpose(
                        ptr[:, hf * 128:(hf + 1) * 128], kT[:, h, hf, cs], ident[:]),
                        reads=[("k", h), ("ident",)], writes=[("pstr", pj)])
                P.op("dve", lambda e, ptr=ptr, h=h: e.tensor_scalar(
                    out=ktl[:], in0=ptr[:, 0:RDK], scalar1=misc[:, 2 * h:2 * h + 1], scalar2=None, op0=ALU.mult),
                    reads=[("pstr", pj), ("misc",)], writes=[("ktl",)])
                for hf in range(2):
                    P.op("pe", lambda e, h=h, hf=hf, cs=cs: e.matmul(
                        ps_att[:], lhsT=kT[:, h, hf, cs], rhs=qT[:, h, hf, cs], start=(hf == 0), stop=(hf == 1)),
                        reads=[("k", h), ("q", h)], writes=[("psatt",)])
                P.op("dve", lambda e, h=h: e.tensor_tensor(
                    out=attT[:], in0=ps_att[:], in1=inner[:, h * 128:(h + 1) * 128], op=ALU.mult),
                    reads=[("psatt",), ("inner",)], writes=[("attT",)])
                for hf in range(2):
                    P.op("pool", lambda e, h=h, hf=hf, cs=cs: e.tensor_tensor(
                        out=qs[:, hf, :], in0=qT[:, h, hf, cs], in1=cross[:, h * 128:(h + 1) * 128], op=ALU.mult),
                        reads=[("q", h), ("cross",)], writes=[("qs", hf)])
                P.op("pe", lambda e, sub=sub, h=h: e.matmul(
                    ps_y[:], lhsT=attT[:], rhs=v_sb[:, sub, h, :], start=True, stop=False),
                    reads=[("attT",), ("v", sub, h)], writes=[("psy",)])
                for hf in range(2):
                    P.op("pe", lambda e, h=h, hf=hf: e.matmul(
                        ps_y[:], lhsT=qs[:, hf, :], rhs=state_b[:, h, hf, :], start=False, stop=(hf == 1)),
                        reads=[("qs", hf), ("state_b", h, hf)], writes=[("psy",)])
                for hf in range(2):
                    sj = cnt["psst"] % 2
                    cnt["psst"] += 1
                    pst = ps_st[sj]
                    P.op("pe", lambda e, pst=pst, sub=sub, h=h, hf=hf: e.matmul(
                        pst[:], lhsT=ktl[:, hf * 128:(hf + 1) * 128], rhs=v_sb[:, sub, h, :], start=True, stop=True),
                        reads=[("ktl",), ("v", sub, h)], writes=[("psst", sj)])
                    P.op("dve", lambda e, pst=pst, h=h, hf=hf: e.scalar_tensor_tensor(
                        out=state[:, h, hf, :], in0=state[:, h, hf, :], scalar=misc[:, 2 * h + 1:2 * h + 2],
                        in1=pst[:], op0=ALU.mult, op1=ALU.add),
                        reads=[("psst", sj), ("state", h, hf), ("misc",)], writes=[("state", h, hf)])
                    P.op("act", lambda e, h=h, hf=hf: e.copy(out=state_b[:, h, hf, :], in_=state[:, h, hf, :]),
                         reads=[("state", h, hf)], writes=[("state_b", h, hf)])
                P.op("dve", lambda e: e.bn_stats(out=stats[:], in_=ps_y[:]),
                     reads=[("psy",)], writes=[("stats",)])
                P.op("dve", lambda e: e.bn_aggr(out=mv[:], in_=stats[:]),
                     reads=[("stats",)], writes=[("mv",)])
                P.op("act", lambda e: e.activation(out=rstd[:], in_=mv[:, 1:2], func=AF.Ln, bias=epsb[:], scale=1.0),
                     reads=[("mv",), ("epsb",)], writes=[("rstd",)])
                P.op("act", lambda e: e.activation(out=rstd[:], in_=rstd[:], func=AF.Exp, scale=-0.5),
                     reads=[("rstd",)], writes=[("rstd",)])
                P.op("dve", lambda e: e.tensor_scalar(
                    out=yn[:], in0=ps_y[:], scalar1=mv[:, 0:1], scalar2=rstd[:, 0:1], op0=ALU.subtract, op1=ALU.mult),
                    reads=[("psy",), ("mv",), ("rstd",)], writes=[("yn",)])
                P.op("pool", lambda e, sub=sub, h=h: e.tensor_tensor(
                    out=o_tm[:], in0=yn[:], in1=g_sb[:, sub, h, :], op=ALU.mult),
                    reads=[("yn",), ("g", sub, h)], writes=[("otm",)])
                pj = cnt["pstr"] % 2
                cnt["pstr"] += 1
                ptr = ps_tr[pj]
                for c in range(4):
                    P.op("pe", lambda e, ptr=ptr, c=c: e.transpose(
                        ptr[:, c * 128:(c + 1) * 128], o_tm[:, c * 128:(c + 1) * 128], ident[:]),
                        reads=[("otm",), ("ident",)], writes=[("pstr", pj)])
                for c in range(4):
                    P.op("act", lambda e, ptr=ptr, c=c, h=h, cs=cs: e.copy(
                        out=o_fm[:, h * 4 + c, cs], in_=ptr[:, c * 128:(c + 1) * 128]),
                        reads=[("pstr", pj)], writes=[("ofm",)])
        for c in range(RH * 4):
            P.dma("pool", lambda e, c=c, ts=ts: e.dma_start(out=oT[c * 128:(c + 1) * 128, ts], in_=o_fm[:, c, :]),
                  "os", reads=[("ofm",)], writes=[("dram", "oT")])
    extra.flush()
    P.emit()
    return nc


def ret_consts(h0):
    L = 128
    idx = np.arange(L, dtype=np.float32)
    lg = np.log1p(-np.exp2(-5.0 - np.arange(8, dtype=np.float32))).astype(np.float32)
    inner = np.zeros((128, RH * 128), np.float32)
    cross = np.zeros((128, RH * 128), np.float32)
    misc = np.zeros((128, RH * 2), np.float32)
    for h in range(RH):
        g = lg[h0 + h]
        diff = idx[:, None] - idx[None, :]
        dec = np.where(diff >= 0, np.exp(np.maximum(diff, 0.0) * g), 0.0).astype(np.float32)
        inner[:, h * 128:(h + 1) * 128] = dec.T
        cross[:, h * 128:(h + 1) * 128] = np.exp((idx + 1.0) * g)[None, :]
        misc[:, 2 * h] = np.exp((L - 1.0 - idx) * g)
        misc[:, 2 * h + 1] = np.exp(L * g)
    return inner, cross, misc


def rope_tables_T(S_len, dim, scale):
    inv = (1.0 / (10000.0 ** (np.arange(0, dim, 2, dtype=np.float32) / np.float32(dim)))).astype(np.float32)
    ang = np.arange(S_len, dtype=np.float32)[:, None] * inv[None, :]
    return (np.ascontiguousarray(np.cos(ang).T.astype(np.float32)) * np.float32(scale),
            np.ascontiguousarray(np.sin(ang).T.astype(np.float32)) * np.float32(scale))


def tile_w_fm(W):
    n = W.shape[1] // 128
    return tile_w_in_out(W, KD, n)


def tile_w_tm(W, ncol):
    g = W.shape[1] // ncol
    return np.ascontiguousarray(W.reshape(KD, 128, g, ncol).transpose(2, 1, 0, 3).reshape(g, 128, KD * ncol))


def ret_inputs(hm_bf_T, od_w_in, h0, S_len):
    import ml_dtypes
    W = od_w_in
    q0, k0, v0, g0 = 0, 2048, 4096, 8192
    wq = W[:, q0 + h0 * RDK: q0 + (h0 + RH) * RDK]
    wk = W[:, k0 + h0 * RDK: k0 + (h0 + RH) * RDK]
    wv = W[:, v0 + h0 * RDV: v0 + (h0 + RH) * RDV]
    wg = W[:, g0 + h0 * RDV: g0 + (h0 + RH) * RDV]
    cq, sq = rope_tables_T(S_len, RDK, 1.0)
    ck, sk = rope_tables_T(S_len, RDK, RDK ** -0.5)
    inner, cross, misc = ret_consts(h0)
    return {"hmT": hm_bf_T, "wq": tile_w_fm(wq), "wk": tile_w_fm(wk), "wv": tile_w_tm(wv, RDV),
            "wgt": tile_w_tm(wg, RDV), "cosq": cq, "sinq": sq, "cosk": ck, "sink": sk,
            "c_inner": inner, "c_cross": cross, "c_misc": misc,
            "c_ident": np.eye(128, dtype=np.float32).astype(ml_dtypes.bfloat16)}


SH = 8
NEG = -30000.0


def build_ssd(S_len, dbg=99, P=None, extra_fn=None):
    if P is None:
        P = Prog(bass.Bass("TRN2", target_bir_lowering=False))
    nc = P.nc
    n_tiles = S_len // TT
    hmT = P.dram("hmT", [D, S_len], BF16, "ExternalInput")
    wfm_f = P.dram("wfm", [8, 128, D], F32, "ExternalInput")
    wz_f = P.dram("wz", [1, 128, KD * 512], F32, "ExternalInput")
    wdt_f = P.dram("wdt", [1, 128, KD * 8], F32, "ExternalInput")
    wfm_b = P.dram("wfm_b", [8, 128, D], BF16)
    wz_b = P.dram("wz_b", [1, 128, KD * 512], BF16)
    wdt_b = P.dram("wdt_b", [1, 128, KD * 8], BF16)
    c_conv = P.dram("c_conv", [128, 8 * 5], F32, "ExternalInput")
    c_hp = P.dram("c_hp", [128, 3 * 8], F32, "ExternalInput")
    c_nw = P.dram("c_nw", [128, 512], F32, "ExternalInput")
    c_U = P.dram("c_U", [128, 128], F32, "ExternalInput")
    c_nm = P.dram("c_nm", [128, 128], F32, "ExternalInput")
    c_idf = P.dram("c_idf", [128, 128], F32, "ExternalInput")
    c_ident = P.dram("c_ident", [128, 128], BF16, "ExternalInput")
    oT = P.dram("oT", [512, S_len], BF16, "ExternalOutput")

    hm = P.sb("s_hm", [128, KD, TT], BF16)
    wfm = [P.sb(f"s_wfm{i}", [128, D], BF16) for i in range(2)]
    wz = P.sb("s_wz", [128, KD * 512], BF16)
    wdt = P.sb("s_wdt", [128, KD * 8], BF16)
    conv = P.sb("s_conv", [128, 40], F32)
    hp = P.sb("s_hp", [128, 24], F32)
    nw = P.sb("s_nw", [128, 512], F32)
    U = P.sb("s_U", [128, 128], F32)
    nm = P.sb("s_nm", [128, 128], F32)
    idf = P.sb("s_idf", [128, 128], F32)
    ident = P.sb("s_ident", [128, 128], BF16)
    ones = P.sb("s_ones", [128, 128], F32)
    epsb = P.sb("s_epsb", [128, 1], F32)
    oneb = P.sb("s_oneb", [128, 1], F32)
    aneg = P.sb("s_aneg", [128, 8], F32)
    xc = P.sb("s_xc", [128, 8, TT + 3], F32)
    acc = P.sb("s_acc", [128, TT], F32)
    xact = P.sb("s_xact", [128, 8, TT], BF16)
    zs = P.sb("s_zs", [128, 4, 512], F32)
    dt = P.sb("s_dt", [128, 4, 8], F32)
    la = P.sb("s_la", [128, 4, 8], F32)
    cum = P.sb("s_cum", [128, 8], F32)
    ncum = P.sb("s_ncum", [128, 8], F32)
    ecum = P.sb("s_ecum", [128, 8], F32)
    cbT = P.sb("s_cbT", [128, 2, 128], F32)
    Btm = P.sb("s_Btm", [128, 2, 128], BF16)
    xs_tm = P.sb("s_xstm", [128, 512], BF16)
    LAb = [P.sb(f"s_LAb{i}", [128, 128], F32) for i in range(2)]
    decT = [P.sb(f"s_decT{i}", [128, 128], F32) for i in range(2)]
    MT = [P.sb(f"s_MT{i}", [128, 128], BF16) for i in range(2)]
    xd = [P.sb(f"s_xd{i}", [128, 64], BF16) for i in range(2)]
    wxd = [P.sb(f"s_wxd{i}", [128, 64], BF16) for i in range(2)]
    ecl = [P.sb(f"s_ecl{i}", [128, 1], F32) for i in range(2)]
    ysb = [P.sb(f"s_ysb{i}", [128, 64], F32) for i in range(2)]
    state = P.sb("s_state", [128, SH, 64], F32)
    state_b = P.sb("s_state_b", [128, SH, 64], BF16)
    y_all = P.sb("s_yall", [128, 512], F32)
    sqj = P.sb("s_sqj", [128, 256], F32)
    ss = P.sb("s_ss", [128, 2], F32)
    o_tm = P.sb("s_otm", [128, 512], BF16)
    o_fm = P.sb("s_ofm", [128, 4, TT], BF16)
    ps_a = [P.ps(f"s_psa{i}", [128, TT]) for i in range(2)]
    ps_seg = [P.ps(f"s_psseg{i}", [128, 128]) for i in range(2)]
    ps_cb = P.ps("s_pscb", [128, 128])
    ps_y = [P.ps(f"s_psy{i}", [128, 192]) for i in range(2)]
    ps_tr = P.ps("s_pstr", [128, 512], BF16)
    cv = Conv(P)

    P.op("dve", lambda e: e.memset(epsb[:], EPS), writes=[("epsb",)])
    P.op("dve", lambda e: e.memset(oneb[:], 1.0), writes=[("oneb",)])
    P.op("dve", lambda e: e.memset(ones[:], 1.0), writes=[("ones",)])
    P.op("dve", lambda e: e.memset(state[:], 0.0), writes=[("state", h) for h in range(SH)])
    P.op("pool", lambda e: e.memset(state_b[:], 0.0), writes=[("state_b", h) for h in range(SH)])
    P.op("pool", lambda e: e.memset(xc[:], 0.0), writes=[("xc", c) for c in range(8)])
    for dst, src, tag in ((conv, c_conv, "conv"), (hp, c_hp, "hp"), (nw, c_nw, "nw"), (U, c_U, "U"),
                          (nm, c_nm, "nm"), (idf, c_idf, "idf"), (ident, c_ident, "ident")):
        P.dma("pool", lambda e, dst=dst, src=src: e.dma_start(out=dst[:], in_=src), "cl", writes=[(tag,)])
    P.op("act", lambda e: e.activation(out=aneg[:], in_=hp[:, 8:16], func=AF.Exp), reads=[("hp",)], writes=[("aneg",)])
    P.op("dve", lambda e: e.tensor_scalar(out=aneg[:], in0=aneg[:], scalar1=-1.0, scalar2=None, op0=ALU.mult),
         reads=[("aneg",)], writes=[("aneg",)])
    for c in range(8):
        cv.run(wfm_b[c], wfm_f[c], "wfm")
    cv.run(wz_b[0], wz_f[0], "wz")
    cv.run(wdt_b[0], wdt_f[0], "wdt")
    extra = Extra(cv, extra_fn(P) if extra_fn else [], n_tiles * 4)
    P.dma("sp", lambda e: e.dma_start(out=wz[:], in_=wz_b[0]), "wl", reads=P.dkeys["wz"], writes=[("wz",)])
    P.dma("sp", lambda e: e.dma_start(out=wdt[:], in_=wdt_b[0]), "wl", reads=P.dkeys["wdt"], writes=[("wdt",)])

    cnt = {"wfm": 0, "psa": 0, "hh": 0}

    for t in range(n_tiles):
        ts = slice(t * TT, (t + 1) * TT)
        for k in range(KD):
            P.dma("pool", lambda e, k=k, ts=ts: e.dma_start(out=hm[:, k, :], in_=hmT[k * 128:(k + 1) * 128, ts]),
                  "hl", writes=[("hm",)])
        for c in range(8):
            j = cnt["wfm"] % 2
            cnt["wfm"] += 1
            wt = wfm[j]
            P.dma("sp", lambda e, wt=wt, c=c: e.dma_start(out=wt[:], in_=wfm_b[c]),
                  "wl", reads=P.dkeys["wfm"], writes=[("wfm", j)])
            pj = cnt["psa"] % 2
            cnt["psa"] += 1
            pa = ps_a[pj]
            for k in range(KD):
                P.op("pe", lambda e, pa=pa, wt=wt, k=k: e.matmul(
                    pa[:], lhsT=wt[:, k * 128:(k + 1) * 128], rhs=hm[:, k, :], start=(k == 0), stop=(k == KD - 1)),
                    reads=[("wfm", j), ("hm",)], writes=[("psa", pj)])
            P.op("act", lambda e, pa=pa, c=c: e.copy(out=xc[:, c, 3:], in_=pa[:]),
                 reads=[("psa", pj)], writes=[("xc", c)])
            P.op("dve", lambda e, c=c: e.tensor_scalar(
                out=acc[:], in0=xc[:, c, 3:TT + 3], scalar1=conv[:, 5 * c + 3:5 * c + 4], scalar2=conv[:, 5 * c + 4:5 * c + 5],
                op0=ALU.mult, op1=ALU.add), reads=[("xc", c), ("conv",)], writes=[("acc",)])
            for jj in range(3):
                P.op("dve", lambda e, c=c, jj=jj: e.scalar_tensor_tensor(
                    out=acc[:], in0=xc[:, c, jj:TT + jj], scalar=conv[:, 5 * c + jj:5 * c + jj + 1], in1=acc[:],
                    op0=ALU.mult, op1=ALU.add), reads=[("xc", c), ("conv",), ("acc",)], writes=[("acc",)])
            P.op("act", lambda e, c=c: e.activation(out=xact[:, c, :], in_=acc[:], func=AF.Silu),
                 reads=[("acc",)], writes=[("xact", c)])
            P.op("pool", lambda e, c=c: e.tensor_copy(out=xc[:, c, 0:3], in_=xc[:, c, TT:TT + 3]),
                 reads=[("xc", c)], writes=[("xc", c)])
        for sub in range(4 if dbg >= 2 else 0):
            ssl = slice(sub * 128, (sub + 1) * 128)
            pj = cnt["psa"] % 2
            cnt["psa"] += 1
            pa = ps_a[pj]
            for k in range(KD):
                P.op("pe", lambda e, pa=pa, k=k, ssl=ssl: e.matmul(
                    pa[:], lhsT=hm[:, k, ssl], rhs=wz[:, k * 512:(k + 1) * 512], start=(k == 0), stop=(k == KD - 1)),
                    reads=[("wz",), ("hm",)], writes=[("psa", pj)])
            P.op("act", lambda e, pa=pa, sub=sub: e.activation(out=zs[:, sub, :], in_=pa[:], func=AF.Silu),
                 reads=[("psa", pj)], writes=[("zs", sub)])
            pj = cnt["psa"] % 2
            cnt["psa"] += 1
            pa = ps_a[pj]
            for k in range(KD):
                P.op("pe", lambda e, pa=pa, k=k, ssl=ssl: e.matmul(
                    pa[:, 0:8], lhsT=hm[:, k, ssl], rhs=wdt[:, k * 8:(k + 1) * 8], start=(k == 0), stop=(k == KD - 1)),
                    reads=[("wdt",), ("hm",)], writes=[("psa", pj)])
            P.op("dve", lambda e, pa=pa, sub=sub: e.tensor_tensor(out=dt[:, sub, :], in0=pa[:, 0:8], in1=hp[:, 0:8], op=ALU.add),
                 reads=[("psa", pj), ("hp",)], writes=[("dt", sub)])
            P.op("act", lambda e, sub=sub: e.activation(out=dt[:, sub, :], in_=dt[:, sub, :], func=AF.Exp),
                 reads=[("dt", sub)], writes=[("dt", sub)])
            P.op("act", lambda e, sub=sub: e.activation(out=dt[:, sub, :], in_=dt[:, sub, :], func=AF.Ln, bias=oneb[:], scale=1.0),
                 reads=[("dt", sub), ("oneb",)], writes=[("dt", sub)])
            P.op("dve", lambda e, sub=sub: e.tensor_tensor(out=la[:, sub, :], in0=dt[:, sub, :], in1=aneg[:], op=ALU.mult),
                 reads=[("dt", sub), ("aneg",)], writes=[("la", sub)])
        for sub in range(4 if dbg >= 3 else 0):
            cs = slice(sub * 128, (sub + 1) * 128)
            extra.step()
            P.op("pe", lambda e, sub=sub: e.matmul(ps_cb[:, 0:8], lhsT=U[:], rhs=la[:, sub, :], start=True, stop=True),
                 reads=[("U",), ("la", sub)], writes=[("pscb",)])
            P.op("dve", lambda e: e.tensor_copy(out=cum[:], in_=ps_cb[:, 0:8]), reads=[("pscb",)], writes=[("cum",)])
            P.op("dve", lambda e: e.tensor_scalar(out=ncum[:], in0=cum[:], scalar1=-1.0, scalar2=None, op0=ALU.mult),
                 reads=[("cum",)], writes=[("ncum",)])
            P.op("act", lambda e: e.activation(out=ecum[:], in_=cum[:], func=AF.Exp), reads=[("cum",)], writes=[("ecum",)])
            if dbg < 3.2:
                continue
            for g in range(2):
                P.op("pe", lambda e, g=g, cs=cs: e.matmul(
                    ps_cb[:], lhsT=xact[:, 4 + g, cs], rhs=xact[:, 6 + g, cs], start=True, stop=True),
                    reads=[("xact", 4 + g), ("xact", 6 + g), ("cum",), ("ncum",), ("ecum",)], writes=[("pscb",)])
                P.op("act", lambda e, g=g: e.copy(out=cbT[:, g, :], in_=ps_cb[:]), reads=[("pscb",)], writes=[("cbT", g)])
                P.op("pe", lambda e, g=g, cs=cs: e.transpose(ps_tr[:, g * 128:(g + 1) * 128], xact[:, 4 + g, cs], ident[:]),
                     reads=[("xact", 4 + g), ("ident",)], writes=[("pstr",)])
            P.op("act", lambda e: e.copy(out=Btm[:].rearrange("p g n -> p (g n)"), in_=ps_tr[:, 0:256]),
                 reads=[("pstr",)], writes=[("Btm",)])
            if dbg < 3.4:
                continue
            for c in range(4):
                P.op("pe", lambda e, c=c, cs=cs: e.transpose(ps_tr[:, c * 128:(c + 1) * 128], xact[:, c, cs], ident[:]),
                     reads=[("xact", c), ("ident",), ("Btm",)], writes=[("pstr",)])
            P.op("act", lambda e: e.copy(out=xs_tm[:], in_=ps_tr[:]), reads=[("pstr",)], writes=[("xstm",)])
            for h in range(SH if dbg >= 4 else 0):
                g = h // 4
                i2 = cnt["hh"] % 2
                cnt["hh"] += 1
                hs = slice(h * 64, (h + 1) * 64)
                P.op("dve", lambda e, i2=i2, sub=sub, h=h: e.tensor_scalar(
                    out=LAb[i2][:], in0=ones[:], scalar1=la[:, sub, h:h + 1], scalar2=None, op0=ALU.mult),
                    reads=[("ones",), ("la", sub)], writes=[("LAb", i2)])
                P.op("pe", lambda e, i2=i2: e.matmul(ps_seg[i2][:], lhsT=LAb[i2][:], rhs=U[:], start=True, stop=True),
                     reads=[("LAb", i2), ("U",)], writes=[("psseg", i2)])
                P.op("act", lambda e, i2=i2: e.activation(out=ecl[i2][:], in_=ps_seg[i2][:, 127:128], func=AF.Exp),
                     writes=[("ecl", i2), ("psseg", i2)])
                P.op("dve", lambda e, i2=i2: e.tensor_tensor(out=decT[i2][:], in0=ps_seg[i2][:], in1=nm[:], op=ALU.add),
                     reads=[("psseg", i2), ("nm",)], writes=[("decT", i2)])
                P.op("act", lambda e, i2=i2, h=h: e.activation(
                    out=decT[i2][:], in_=decT[i2][:], func=AF.Exp, bias=ncum[:, h:h + 1], scale=1.0),
                    reads=[("decT", i2), ("ncum",)], writes=[("decT", i2)])
                P.op("dve", lambda e, i2=i2, g=g: e.tensor_tensor(out=MT[i2][:], in0=decT[i2][:], in1=cbT[:, g, :], op=ALU.mult),
                     reads=[("decT", i2), ("cbT", g)], writes=[("MT", i2)])
                if dbg < 4.2:
                    continue
                P.op("pool", lambda e, i2=i2, sub=sub, h=h, hs=hs: e.tensor_scalar(
                    out=xd[i2][:], in0=xs_tm[:, hs], scalar1=dt[:, sub, h:h + 1], scalar2=None, op0=ALU.mult),
                    reads=[("xstm",), ("dt", sub)], writes=[("xd", i2)])
                P.op("dve", lambda e, i2=i2: e.tensor_scalar(
                    out=wxd[i2][:], in0=xd[i2][:], scalar1=decT[i2][:, 127:128], scalar2=None, op0=ALU.mult),
                    reads=[("xd", i2), ("decT", i2)], writes=[("wxd", i2)])
                if dbg < 4.3:
                    continue
                py = ps_y[i2]
                P.op("pe", lambda e, py=py, i2=i2: e.matmul(py[:, 0:64], lhsT=MT[i2][:], rhs=xd[i2][:], start=True, stop=True),
                     reads=[("MT", i2), ("xd", i2)], writes=[("psy", i2)])
                P.op("pe", lambda e, py=py, g=g, cs=cs, h=h: e.matmul(
                    py[:, 64:128], lhsT=xact[:, 6 + g, cs], rhs=state_b[:, h, :], start=True, stop=True),
                    reads=[("xact", 6 + g), ("state_b", h)], writes=[("psy", i2)])
                P.op("pe", lambda e, py=py, g=g, i2=i2: e.matmul(
                    py[:, 128:192], lhsT=Btm[:, g, :], rhs=wxd[i2][:], start=True, stop=True),
                    reads=[("Btm",), ("wxd", i2)], writes=[("psy", i2)])
                if dbg < 4.4:
                    continue
                P.op("act", lambda e, py=py, i2=i2: e.copy(out=ysb[i2][:], in_=py[:, 0:64]),
                     writes=[("ysb", i2), ("psy", i2)])
                P.op("dve", lambda e, py=py, i2=i2, h=h, hs=hs: e.scalar_tensor_tensor(
                    out=y_all[:, hs], in0=py[:, 64:128], scalar=ecum[:, h:h + 1], in1=ysb[i2][:], op0=ALU.mult, op1=ALU.add),
                    reads=[("psy", i2), ("ecum",), ("ysb", i2)], writes=[("yall", h)])
                P.op("dve", lambda e, h=h, hs=hs: e.scalar_tensor_tensor(
                    out=y_all[:, hs], in0=xs_tm[:, hs], scalar=hp[:, 16 + h:17 + h], in1=y_all[:, hs], op0=ALU.mult, op1=ALU.add),
                    reads=[("xstm",), ("hp",), ("yall", h)], writes=[("yall", h)])
                P.op("dve", lambda e, py=py, i2=i2, h=h: e.scalar_tensor_tensor(
                    out=state[:, h, :], in0=state[:, h, :], scalar=ecl[i2][:, 0:1], in1=py[:, 128:192], op0=ALU.mult, op1=ALU.add),
                    reads=[("psy", i2), ("ecl", i2), ("state", h)], writes=[("state", h)])
                P.op("act", lambda e, h=h: e.copy(out=state_b[:, h, :], in_=state[:, h, :]),
                     reads=[("state", h)], writes=[("state_b", h)])
            if dbg < 5:
                continue
            yres = [("yall", h) for h in range(SH)]
            P.op("dve", lambda e, sub=sub: e.tensor_tensor(out=y_all[:], in0=y_all[:], in1=zs[:, sub, :], op=ALU.mult),
                 reads=yres + [("zs", sub)], writes=yres)
            for g in range(2):
                P.op("act", lambda e, g=g: e.activation(out=sqj[:], in_=y_all[:, g * 256:(g + 1) * 256], func=AF.Square,
                                                        accum_out=ss[:, g:g + 1]),
                     reads=yres, writes=[("sqj",), ("ss", g)])
            P.op("act", lambda e: e.activation(out=ss[:], in_=ss[:], func=AF.Ln, bias=epsb[:], scale=1.0 / 256),
                 reads=[("ss", 0), ("ss", 1), ("epsb",)], writes=[("ss", 0), ("ss", 1)])
            P.op("act", lambda e: e.activation(out=ss[:], in_=ss[:], func=AF.Exp, scale=-0.5),
                 reads=[("ss", 0), ("ss", 1)], writes=[("ss", 0), ("ss", 1)])
            for g in range(2):
                gs = slice(g * 256, (g + 1) * 256)
                P.op("dve", lambda e, g=g, gs=gs: e.scalar_tensor_tensor(
                    out=o_tm[:, gs], in0=y_all[:, gs], scalar=ss[:, g:g + 1], in1=nw[:, gs], op0=ALU.mult, op1=ALU.mult),
                    reads=yres + [("ss", g), ("nw",)], writes=[("otm", g)])
            for c in range(4):
                P.op("pe", lambda e, c=c: e.transpose(ps_tr[:, c * 128:(c + 1) * 128], o_tm[:, c * 128:(c + 1) * 128], ident[:]),
                     reads=[("otm", 0), ("otm", 1), ("ident",), ("xstm",)], writes=[("pstr",)])
            P.op("act", lambda e, cs=cs: e.copy(out=o_fm[:, :, cs], in_=ps_tr[:].rearrange("p (c t) -> p c t", c=4)),
                 reads=[("pstr",)], writes=[("ofm",)])
        for c in range(4):
            P.dma("pool", lambda e, c=c, ts=ts: e.dma_start(out=oT[c * 128:(c + 1) * 128, ts], in_=o_fm[:, c, :]),
                  "os", reads=[("ofm",)], writes=[("dram", "oT")])
    extra.flush()
    P.emit()
    return nc


EV_OFF = {}
_o = 0
for _n, _s in zip(("q", "kc", "vc", "ksl", "vsl", "kwn", "vwn", "gate", "z", "xbc", "dt"),
                  (1024, 256, 256, 256, 256, 256, 256, 48, 1024, 2048, 16)):
    EV_OFF[_n] = _o
    _o += _s


def ssd_inputs(hm_bf_T, w_in, conv_w, conv_b, dt_bias, a_log, d_skip, ssm_norm, hh):
    import ml_dtypes
    xo = EV_OFF["xbc"]
    xs_cols = np.arange(xo + hh * 512, xo + (hh + 1) * 512)
    b_cols = np.arange(xo + 1024 + hh * 256, xo + 1024 + (hh + 1) * 256)
    c_cols = np.arange(xo + 1536 + hh * 256, xo + 1536 + (hh + 1) * 256)
    cols = np.concatenate([xs_cols, b_cols, c_cols])
    wfm = tile_w_fm(w_in[:, cols])
    wz = tile_w_tm(w_in[:, EV_OFF["z"] + hh * 512: EV_OFF["z"] + (hh + 1) * 512], 512)
    wdt = tile_w_tm(w_in[:, EV_OFF["dt"] + hh * 8: EV_OFF["dt"] + (hh + 1) * 8], 8)
    cc = cols - xo
    cw = conv_w[:, cc]
    cb = conv_b[cc]
    c_conv = np.zeros((128, 40), np.float32)
    for c in range(8):
        c_conv[:, 5 * c:5 * c + 4] = cw[:, c * 128:(c + 1) * 128].T
        c_conv[:, 5 * c + 4] = cb[c * 128:(c + 1) * 128]
    hsl = slice(hh * 8, (hh + 1) * 8)
    c_hp = np.concatenate([np.tile(dt_bias[hsl][None], (128, 1)), np.tile(a_log[hsl][None], (128, 1)),
                           np.tile(d_skip[hsl][None], (128, 1))], axis=1).astype(np.float32)
    c_nw = np.tile(ssm_norm[hh * 512:(hh + 1) * 512][None], (128, 1)).astype(np.float32)
    r = np.arange(128)
    U = (r[:, None] <= r[None, :]).astype(np.float32)
    nm = np.where(r[:, None] <= r[None, :], 0.0, NEG).astype(np.float32)
    return {"hmT": hm_bf_T, "wfm": wfm, "wz": wz, "wdt": wdt, "c_conv": c_conv, "c_hp": c_hp, "c_nw": c_nw,
            "c_U": U, "c_nm": nm, "c_idf": np.eye(128, dtype=np.float32),
            "c_ident": np.eye(128, dtype=np.float32).astype(ml_dtypes.bfloat16)}


NFM = 19
NTM = 280


def build_nsa(S_len, P=None):
    if P is None:
        P = Prog(bass.Bass("TRN2", target_bir_lowering=False))
    nc = P.nc
    n_tiles = S_len // TT
    NB = S_len // 16
    NBLK = S_len // 64
    NKT = S_len // 128
    NBC = (NB + 127) // 128
    hmT = P.dram("hmT", [D, S_len], BF16, "ExternalInput")
    wfm_f = P.dram("wfm", [NFM, 128, D], F32, "ExternalInput")
    wtm_f = P.dram("wtm", [1, 128, KD * NTM], F32, "ExternalInput")
    w1_f = P.dram("w1", [2, 128, 32 * 256], F32, "ExternalInput")
    w2k_f = P.dram("w2k", [128, 2 * 128], F32, "ExternalInput")
    w2v_f = P.dram("w2v", [128, 2 * 64], F32, "ExternalInput")
    posT_f = P.dram("posT", [128, 2 * 32], F32, "ExternalInput")
    wfm_b = P.dram("wfm_b", [NFM, 128, D], BF16)
    wtm_b = P.dram("wtm_b", [1, 128, KD * NTM], BF16)
    w1_b = P.dram("w1_b", [2, 128, 32 * 256], BF16)
    t_cq = P.dram("t_cq", [128, S_len], F32, "ExternalInput")
    t_sq = P.dram("t_sq", [128, S_len], F32, "ExternalInput")
    t_ck = P.dram("t_ck", [128, S_len], F32, "ExternalInput")
    t_sk = P.dram("t_sk", [128, S_len], F32, "ExternalInput")
    c_ident = P.dram("c_ident", [128, 128], BF16, "ExternalInput")
    c_E = P.dram("c_E", [128, S_len], BF16, "ExternalInput")
    c_Wb = P.dram("c_Wb", [128, 8 * 512], BF16, "ExternalInput")
    c_Cw = P.dram("c_Cw", [128, 1024], BF16, "ExternalInput")
    c_AB = P.dram("c_AB", [128, 512], F32, "ExternalInput")
    oT = P.dram("oT", [512, S_len], BF16, "ExternalOutput")
    QT = P.dram("QT", [4, 128, S_len], BF16)
    KslT = P.dram("KslT", [2, 128, S_len], BF16)
    KwnT = P.dram("KwnT", [2, 128, S_len], BF16)
    KcT = P.dram("KcT", [128, S_len], BF16)
    VcT = P.dram("VcT", [128, S_len], BF16)
    Vsl = P.dram("Vsl", [2, S_len, 64], BF16)
    Vwn = P.dram("Vwn", [2, S_len, 64], BF16)
    Gate = P.dram("Gate", [S_len, 24], F32)

    ident = P.sb("n_ident", [128, 128], BF16)
    zer = P.sb("n_zer", [128, 260], BF16)
    ps_st = [P.ps(f"n_psst{i}", [128, 512]) for i in range(3)]
    ps_acc = [P.ps(f"n_psacc{i}", [128, 4, 65]) for i in range(2)]
    ps_c = P.ps("n_psc", [128, 512])
    ps_tr = P.ps("n_pstr", [128, 512], BF16)
    ps_oc = P.ps("n_psoc", [128, 64])
    P.dma("pool", lambda e: e.dma_start(out=ident[:], in_=c_ident), "cl", writes=[("ident",)])
    P.op("dve", lambda e: e.memset(zer[:], 0.0), writes=[("zer",)])

    hm = P.sb("n_hm", [128, KD, TT], BF16)
    wfm = [P.sb(f"n_wfm{i}", [128, D], BF16) for i in range(2)]
    wtm = P.sb("n_wtm", [128, KD * NTM], BF16)
    tab = P.sb("n_tab", [128, 4, TT], F32)
    raw = P.sb("n_raw", [128, TT], F32)
    tmp = P.sb("n_tmp", [128, TT], F32)
    fmo = [P.sb(f"n_fmo{i}", [128, TT], BF16) for i in range(2)]
    vg = [P.sb(f"n_vg{i}", [128, 256], BF16) for i in range(2)]
    gt = [P.sb(f"n_gt{i}", [128, 24], F32) for i in range(2)]
    cv = Conv(P)
    for c in range(NFM):
        cv.run(wfm_b[c], wfm_f[c], ("wfm", c))
    cv.run(wtm_b[0], wtm_f[0], "wtm")
    for kv in range(2):
        cv.run(w1_b[kv], w1_f[kv], ("w1", kv))
    P.dma("sp", lambda e: e.dma_start(out=wtm[:], in_=wtm_b[0]), "wl", reads=P.dkeys["wtm"], writes=[("wtm",)])
    cnt = {"wfm": 0, "pst": 0, "fmo": 0, "vg": 0}

    def fm_chunk(c):
        j = cnt["wfm"] % 2
        cnt["wfm"] += 1
        wt = wfm[j]
        P.dma("sp", lambda e, wt=wt, c=c: e.dma_start(out=wt[:], in_=wfm_b[c]),
              "wl", reads=P.dkeys[("wfm", c)], writes=[("wfm", j)])
        pj = cnt["pst"] % 2
        cnt["pst"] += 1
        pa = ps_st[pj]
        for k in range(KD):
            P.op("pe", lambda e, pa=pa, wt=wt, k=k: e.matmul(
                pa[:], lhsT=wt[:, k * 128:(k + 1) * 128], rhs=hm[:, k, :], start=(k == 0), stop=(k == KD - 1)),
                reads=[("wfm", j), ("hm",)], writes=[("psst", pj)])
        return pa, pj

    def roped(c_main, c_swap, ci, si, dst_ap, dst_key, ts):
        pa, pj = fm_chunk(c_main)
        P.op("dve", lambda e, pa=pa, ci=ci: e.tensor_tensor(out=raw[:], in0=pa[:], in1=tab[:, ci, :], op=ALU.mult),
             reads=[("psst", pj), ("tab",)], writes=[("raw",)])
        pb, pk = fm_chunk(c_swap)
        P.op("dve", lambda e, pb=pb, si=si: e.tensor_tensor(out=tmp[:], in0=pb[:], in1=tab[:, si, :], op=ALU.mult),
             reads=[("psst", pk), ("tab",)], writes=[("tmp",)])
        fj = cnt["fmo"] % 2
        cnt["fmo"] += 1
        fo = fmo[fj]
        P.op("dve", lambda e, fo=fo: e.tensor_tensor(out=fo[:], in0=raw[:], in1=tmp[:], op=ALU.add),
             reads=[("raw",), ("tmp",)], writes=[("fmo", fj)])
        P.dma("pool", lambda e, fo=fo, dst_ap=dst_ap: e.dma_start(out=dst_ap, in_=fo[:]),
              "sc", reads=[("fmo", fj)], writes=[("dram", dst_key)])

    for t in range(n_tiles):
        ts = slice(t * TT, (t + 1) * TT)
        for k in range(KD):
            P.dma("pool", lambda e, k=k, ts=ts: e.dma_start(out=hm[:, k, :], in_=hmT[k * 128:(k + 1) * 128, ts]),
                  "hl", writes=[("hm",)])
        for i, src in enumerate((t_cq, t_sq, t_ck, t_sk)):
            P.dma("pool", lambda e, i=i, src=src, ts=ts: e.dma_start(out=tab[:, i, :], in_=src[:, ts]),
                  "tl", writes=[("tab",)])
        for j in range(4):
            roped(j, 4 + j, 0, 1, QT[j][:, ts], ("QT", j, t), ts)
        for g in range(2):
            roped(8 + g, 10 + g, 2, 3, KslT[g][:, ts], ("KslT", g, t), ts)
            roped(12 + g, 14 + g, 2, 3, KwnT[g][:, ts], ("KwnT", g, t), ts)
        roped(16, 17, 2, 3, KcT[:, ts], ("KcT", t), ts)
        pa, pj = fm_chunk(18)
        fj = cnt["fmo"] % 2
        cnt["fmo"] += 1
        fo = fmo[fj]
        P.op("act", lambda e, pa=pa, fo=fo: e.copy(out=fo[:], in_=pa[:]), reads=[("psst", pj)], writes=[("fmo", fj)])
        P.dma("pool", lambda e, fo=fo, ts=ts: e.dma_start(out=VcT[:, ts], in_=fo[:]),
              "sc", reads=[("fmo", fj)], writes=[("dram", ("VcT", t))])
        for sub in range(4):
            ssl = slice(sub * 128, (sub + 1) * 128)
            r0 = t * TT + sub * 128
            pj = cnt["pst"] % 2
            cnt["pst"] += 1
            pa = ps_st[pj]
            for k in range(KD):
                P.op("pe", lambda e, pa=pa, k=k, ssl=ssl: e.matmul(
                    pa[:, 0:NTM], lhsT=hm[:, k, ssl], rhs=wtm[:, k * NTM:(k + 1) * NTM], start=(k == 0), stop=(k == KD - 1)),
                    reads=[("wtm",), ("hm",)], writes=[("psst", pj)])
            vj = cnt["vg"] % 2
            cnt["vg"] += 1
            P.op("act", lambda e, pa=pa, vj=vj: e.copy(out=vg[vj][:], in_=pa[:, 0:256]),
                 reads=[("psst", pj)], writes=[("vg", vj)])
            P.op("act", lambda e, pa=pa, vj=vj: e.activation(out=gt[vj][:], in_=pa[:, 256:280], func=AF.Sigmoid),
                 reads=[("psst", pj)], writes=[("gt", vj)])
            for g in range(2):
                P.dma("pool", lambda e, vj=vj, g=g, r0=r0: e.dma_start(out=Vsl[g][r0:r0 + 128, :], in_=vg[vj][:, g * 64:(g + 1) * 64]),
                      "sc", reads=[("vg", vj)], writes=[("dram", ("Vsl", g, t, sub))])
                P.dma("pool", lambda e, vj=vj, g=g, r0=r0: e.dma_start(out=Vwn[g][r0:r0 + 128, :], in_=vg[vj][:, 128 + g * 64:128 + (g + 1) * 64]),
                      "sc", reads=[("vg", vj)], writes=[("dram", ("Vwn", g, t, sub))])
            P.dma("pool", lambda e, vj=vj, r0=r0: e.dma_start(out=Gate[r0:r0 + 128, :], in_=gt[vj][:]),
                  "sc", reads=[("gt", vj)], writes=[("dram", ("Gate", t, sub))])

    kcv = P.sb("n_kcv", [128, 2, S_len + 16], BF16)
    w1 = P.sb("n_w1", [128, 32 * 256], BF16)
    w2k = P.sb("n_w2k", [128, 256], BF16)
    w2v = P.sb("n_w2v", [128, 128], BF16)
    w2f = P.sb("n_w2f", [128, 256], F32)
    posT = P.sb("n_posT", [128, 64], BF16)
    posf = P.sb("n_posf", [128, 64], F32)
    c1 = P.sb("n_c1", [128, 2], F32)
    hid = P.sb("n_hid", [128, 2, NB], BF16)
    KcmpT = P.sb("n_KcmpT", [128, 2, NB], BF16)
    Vcmp = P.sb("n_Vcmp", [128, 2, NBC, 64], BF16)
    P.op("dve", lambda e: e.memset(kcv[:, :, S_len:], 0.0), writes=[("kcv_pad",)])
    P.dma("pool", lambda e: e.dma_start(out=kcv[:, 0, 0:S_len], in_=KcT), "kl",
          reads=[("dram", ("KcT", t)) for t in range(n_tiles)], writes=[("kcv", 0)])
    P.dma("pool", lambda e: e.dma_start(out=kcv[:, 1, 0:S_len], in_=VcT), "kl",
          reads=[("dram", ("VcT", t)) for t in range(n_tiles)], writes=[("kcv", 1)])
    P.dma("pool", lambda e: e.dma_start(out=posf[:], in_=posT_f), "cl", writes=[("posf",)])
    P.op("dve", lambda e: e.tensor_copy(out=posT[:], in_=posf[:]), reads=[("posf",)], writes=[("posT",)])
    P.dma("pool", lambda e: e.dma_start(out=w2f[:], in_=w2k_f), "cl", writes=[("w2f",)])
    P.op("dve", lambda e: e.tensor_copy(out=w2k[:], in_=w2f[:]), reads=[("w2f",)], writes=[("w2k",)])
    P.dma("pool", lambda e: e.dma_start(out=w2f[:, 0:128], in_=w2v_f), "cl", reads=[("w2k",)], writes=[("w2f",)])
    P.op("dve", lambda e: e.tensor_copy(out=w2v[:], in_=w2f[:, 0:128]), reads=[("w2f",)], writes=[("w2v",)])
    for kv in range(2):
        P.dma("sp", lambda e, kv=kv: e.dma_start(out=w1[:], in_=w1_b[kv]), "wl",
              reads=P.dkeys[("w1", kv)], writes=[("w1",)])
        for hc in range(2):
            for l in range(32):
                P.op("pe", lambda e, hc=hc, l=l, kv=kv: e.matmul(
                    ps_oc[:, hc:hc + 1], lhsT=w1[0:64, l * 256 + hc * 128:l * 256 + (hc + 1) * 128],
                    rhs=posT[0:64, kv * 32 + l:kv * 32 + l + 1], start=(l == 0), stop=(l == 31)),
                    reads=[("w1",), ("posT",)], writes=[("psoc",)])
        P.op("dve", lambda e: e.tensor_copy(out=c1[:], in_=ps_oc[:, 0:2]), reads=[("psoc",)], writes=[("c1",)])
        for g in range(2):
            gp = slice(g * 64, (g + 1) * 64)
            for hc in range(2):
                for l in range(32):
                    P.op("pe", lambda e, hc=hc, l=l, kv=kv, gp=gp: e.matmul(
                        ps_c[:, 0:NB], lhsT=w1[gp, l * 256 + hc * 128:l * 256 + (hc + 1) * 128],
                        rhs=kcv[gp, kv, l:l + 16 * (NB - 1) + 1:16], start=(l == 0), stop=(l == 31)),
                        reads=[("w1",), ("kcv", kv), ("kcv_pad",)], writes=[("psc",)])
                P.op("act", lambda e, hc=hc: e.activation(out=hid[:, hc, :], in_=ps_c[:, 0:NB], func=AF.Silu,
                                                          bias=c1[:, hc:hc + 1], scale=1.0),
                     reads=[("psc",), ("c1",)], writes=[("hid", hc)])
            if kv == 0:
                for hc in range(2):
                    P.op("pe", lambda e, hc=hc: e.matmul(ps_c[:, 0:NB], lhsT=w2k[:, hc * 128:(hc + 1) * 128], rhs=hid[:, hc, :],
                                                         start=(hc == 0), stop=(hc == 1)),
                         reads=[("w2k",), ("hid", 0), ("hid", 1)], writes=[("psc",)])
                P.op("act", lambda e, g=g: e.copy(out=KcmpT[:, g, :], in_=ps_c[:, 0:NB]), reads=[("psc",)], writes=[("KcmpT", g)])
            else:
                for ncn in range(NBC):
                    nn = min(128, NB - ncn * 128)
                    for hc in range(2):
                        P.op("pe", lambda e, hc=hc, ncn=ncn, nn=nn: e.matmul(
                            ps_oc[0:nn, :], lhsT=hid[:, hc, ncn * 128:ncn * 128 + nn], rhs=w2v[:, hc * 64:(hc + 1) * 64],
                            start=(hc == 0), stop=(hc == 1)),
                            reads=[("w2v",), ("hid", 0), ("hid", 1)], writes=[("psoc",)])
                    P.op("act", lambda e, g=g, ncn=ncn, nn=nn: e.copy(out=Vcmp[0:nn, g, ncn, :], in_=ps_oc[0:nn, :]),
                         reads=[("psoc",)], writes=[("Vcmp", g)])

    E = hm[:].rearrange("p k t -> p (k t)")[:, 0:S_len]
    Wb = P.sb("n_Wb", [128, 8, 512], BF16)
    Cw = P.sb("n_Cw", [128, 1024], BF16)
    AB = P.sb("n_AB", [128, 512], F32)
    P.dma("pool", lambda e: e.dma_start(out=E, in_=c_E), "cl", writes=[("E",), ("hm",)])
    for dst, src, tag in ((Wb, c_Wb, "Wb"), (Cw, c_Cw, "Cw"), (AB, c_AB, "AB")):
        P.dma("pool", lambda e, dst=dst, src=src: e.dma_start(
            out=dst[:] if len(dst.shape) == 2 else dst[:].rearrange("p a b -> p (a b)"), in_=src), "cl", writes=[(tag,)])
    Ks = kcv[:, 0, 0:S_len]
    Kw = kcv[:, 1, 0:S_len]
    Vs1 = P.sb("n_Vs1", [128, NKT, 65], BF16)
    Vw1 = P.sb("n_Vw1", [128, NKT, 65], BF16)
    Qt = P.sb("n_Qt", [128, 2, TT], BF16)
    gtile = P.sb("n_gtile", [128, 4, 24], F32)
    pc = P.sb("n_pc", [128, NB], F32)
    pcb = P.sb("n_pcb", [128, NBC * 128], BF16)
    pcT = P.sb("n_pcT", [128, NBC * 128], BF16)
    imp = P.sb("n_imp", [128, NB], F32)
    rs = P.sb("n_rs", [128, 2], F32)
    vals = P.sb("n_vals", [128, NBLK], F32)
    vtmp = P.sb("n_vtmp", [128, NBLK], F32)
    m8 = P.sb("n_m8", [128, 16], F32)
    sel = P.sb("n_sel", [128, NBLK], F32)
    negb = P.sb("n_negb", [128, 128], BF16)
    negbT = P.sb("n_negbT", [128, TT], BF16)
    PT = [P.sb(f"n_PT{i}", [128, TT], BF16) for i in range(3)]
    oacc = P.sb("n_oacc", [128, 4, 4, 64], F32)
    oaccb = P.sb("n_oaccb", [128, 4, 256], BF16)
    w4 = P.sb("n_w4", [128, 4], F32)
    o_fm = P.sb("n_ofm", [128, 2, TT], BF16)
    P.op("dve", lambda e: e.memset(Vs1[:], 1.0), writes=[("Vs1",)])
    P.op("dve", lambda e: e.memset(Vw1[:], 1.0), writes=[("Vw1",)])
    P.op("dve", lambda e: e.memset(negb[:], 0.0), writes=[("negb",)])
    cnt.update({"acc": 0, "PT": 0, "pst3": 0})

    for g in range(2):
        P.dma("pool", lambda e, g=g: e.dma_start(out=Ks, in_=KslT[g]), "kl",
              reads=[("dram", ("KslT", g, t)) for t in range(n_tiles)], writes=[("Ks",), ("kcv", 0), ("kcv_pad",)])
        P.dma("pool", lambda e, g=g: e.dma_start(out=Kw, in_=KwnT[g]), "kl",
              reads=[("dram", ("KwnT", g, t)) for t in range(n_tiles)], writes=[("Kw",), ("kcv", 1), ("kcv_pad",)])
        vdeps = lambda nm: [("dram", (nm, g, t, s)) for t in range(n_tiles) for s in range(4)]
        P.dma("pool", lambda e, g=g: e.dma_start(out=Vs1[:, :, 0:64], in_=Vsl[g].rearrange("(k p) d -> p k d", p=128)), "kl",
              reads=vdeps("Vsl"), writes=[("Vs1",)])
        P.dma("pool", lambda e, g=g: e.dma_start(out=Vw1[:, :, 0:64], in_=Vwn[g].rearrange("(k p) d -> p k d", p=128)), "kl",
              reads=vdeps("Vwn"), writes=[("Vw1",)])
        for qt in range(n_tiles):
            T0 = qt * TT
            ts = slice(T0, T0 + TT)
            for j in range(2):
                P.dma("pool", lambda e, j=j, g=g, ts=ts: e.dma_start(out=Qt[:, j, :], in_=QT[2 * g + j][:, ts]), "ql",
                      reads=[("dram", ("QT", 2 * g + j, qt))], writes=[("Qt",)])
            P.dma("pool", lambda e, ts=ts: e.dma_start(out=gtile[:], in_=Gate[ts, :].rearrange("(s p) c -> p s c", p=128)), "ql",
                  reads=[("dram", ("Gate", qt, s)) for s in range(4)], writes=[("gtile",)])
            NBv = min(NB, 32 * (qt + 1))
            nch = (NBv + 127) // 128
            for sub in range(4):
                m = 4 * qt + sub
                qs = slice(sub * 128, (sub + 1) * 128)
                P.op("pool", lambda e: e.memset(imp[:], 0.0), writes=[("imp",)])
                for hl in range(4):
                    j, half = hl // 2, hl % 2
                    hp_ = slice(half * 64, (half + 1) * 64)
                    gcol = 3 * (4 * g + hl)
                    P.op("pe", lambda e, j=j, hp_=hp_, qs=qs, g=g, NBv=NBv: e.matmul(
                        ps_c[:, 0:NBv], lhsT=Qt[hp_, j, qs], rhs=KcmpT[hp_, g, 0:NBv], start=True, stop=False),
                        reads=[("Qt",), ("KcmpT", g)], writes=[("psc",)])
                    P.op("pe", lambda e, m=m, NBv=NBv: e.matmul(
                        ps_c[:, 0:NBv], lhsT=ident[:], rhs=Cw[:, 512 - 8 * m:512 - 8 * m + NBv], start=False, stop=True),
                        reads=[("ident",), ("Cw",)], writes=[("psc",)])
                    P.op("act", lambda e, NBv=NBv: e.activation(out=pc[:, 0:NBv], in_=ps_c[:, 0:NBv], func=AF.Exp,
                                                                accum_out=rs[:, 0:1]),
                         reads=[("psc",)], writes=[("pc",), ("rs",)])
                    P.op("dve", lambda e: e.tensor_scalar(out=rs[:, 0:1], in0=rs[:, 0:1], scalar1=1e-30, scalar2=None, op0=ALU.max),
                         reads=[("rs",)], writes=[("rs",)])
                    P.op("dve", lambda e: e.reciprocal(out=rs[:, 1:2], in_=rs[:, 0:1]), reads=[("rs",)], writes=[("rs",)])
                    P.op("dve", lambda e, NBv=NBv: e.tensor_scalar(out=pc[:, 0:NBv], in0=pc[:, 0:NBv], scalar1=rs[:, 1:2],
                                                                   scalar2=None, op0=ALU.mult),
                         reads=[("pc",), ("rs",)], writes=[("pc",)])
                    P.op("pool", lambda e, NBv=NBv: e.tensor_tensor(out=imp[:, 0:NBv], in0=imp[:, 0:NBv], in1=pc[:, 0:NBv], op=ALU.add),
                         reads=[("pc",), ("imp",)], writes=[("imp",)])
                    P.op("act", lambda e, NBv=NBv: e.copy(out=pcb[:, 0:NBv], in_=pc[:, 0:NBv]), reads=[("pc",)], writes=[("pcb",)])
                    for c in range(nch):
                        nn = min(128, NBv - c * 128)
                        P.op("pe", lambda e, c=c, nn=nn: e.transpose(ps_tr[0:nn, c * 128:(c + 1) * 128], pcb[:, c * 128:c * 128 + nn], ident[:]),
                             reads=[("pcb",), ("ident",)], writes=[("pstr",)])
                    for c in range(nch):
                        nn = min(128, NBv - c * 128)
                        P.op("act", lambda e, c=c, nn=nn: e.copy(out=pcT[0:nn, c * 128:(c + 1) * 128], in_=ps_tr[0:nn, c * 128:(c + 1) * 128]),
                             reads=[("pstr",)], writes=[("pcT",)])
                    for c in range(nch):
                        nn = min(128, NBv - c * 128)
                        P.op("pe", lambda e, c=c, nn=nn, g=g: e.matmul(
                            ps_oc[:], lhsT=pcT[0:nn, c * 128:(c + 1) * 128], rhs=Vcmp[0:nn, g, c, :], start=(c == 0), stop=(c == nch - 1)),
                            reads=[("pcT",), ("Vcmp", g)], writes=[("psoc",)])
                    P.op("dve", lambda e, sub=sub, hl=hl, gcol=gcol: e.tensor_scalar(
                        out=oacc[:, sub, hl, :], in0=ps_oc[:], scalar1=gtile[:, sub, gcol:gcol + 1], scalar2=None, op0=ALU.mult),
                        reads=[("psoc",), ("gtile",)], writes=[("oacc", sub, hl)])
                P.op("dve", lambda e: e.tensor_reduce(out=vals[:], in_=imp[:].rearrange("p (j f) -> p j f", f=4), axis=AX.X, op=ALU.add),
                     reads=[("imp",)], writes=[("vals",)])
                a0 = 128 - 2 * m
                P.op("dve", lambda e, a0=a0: e.tensor_tensor(out=vals[:], in0=vals[:], in1=AB[:, a0:a0 + NBLK], op=ALU.mult),
                     reads=[("vals",), ("AB",)], writes=[("vals",)])
                P.op("dve", lambda e, a0=a0: e.tensor_tensor(out=vals[:], in0=vals[:], in1=AB[:, 256 + a0:256 + a0 + NBLK], op=ALU.add),
                     reads=[("vals",), ("AB",)], writes=[("vals",)])
                P.op("dve", lambda e: e.memset(vals[:, 0:1], 1e4), reads=[("vals",)], writes=[("vals",)])
                P.op("dve", lambda e: e.max(out=m8[:, 0:8], in_=vals[:]), reads=[("vals",)], writes=[("m8",)])
                P.op("dve", lambda e: e.match_replace(out=vtmp[:], in_to_replace=m8[:, 0:8], in_values=vals[:], imm_value=-2.0),
                     reads=[("vals",), ("m8",)], writes=[("vtmp",)])
                P.op("dve", lambda e: e.max(out=m8[:, 8:16], in_=vtmp[:]), reads=[("vtmp",)], writes=[("m8",)])
                P.op("dve", lambda e: e.tensor_scalar(out=sel[:], in0=vals[:], scalar1=m8[:, 15:16], scalar2=None, op0=ALU.is_ge),
                     reads=[("vals",), ("m8",)], writes=[("sel",)])
                P.op("dve", lambda e: e.tensor_scalar(out=vtmp[:], in0=vals[:], scalar1=0.0, scalar2=None, op0=ALU.is_ge),
                     reads=[("vals",), ("sel",)], writes=[("vtmp",)])
                P.op("dve", lambda e: e.tensor_tensor(out=sel[:], in0=sel[:], in1=vtmp[:], op=ALU.mult),
                     reads=[("sel",), ("vtmp",)], writes=[("sel",)])
                P.op("dve", lambda e: e.tensor_scalar(out=negb[:, 0:NBLK], in0=sel[:], scalar1=-1.0, scalar2=-NEG, op0=ALU.add, op1=ALU.mult),
                     reads=[("sel",)], writes=[("negb",)])
                P.op("pe", lambda e: e.transpose(ps_tr[:, 0:128], negb[:], ident[:]), reads=[("negb",), ("ident",)], writes=[("pstr",)])
                P.op("act", lambda e, qs=qs: e.copy(out=negbT[:, qs], in_=ps_tr[:, 0:128]), reads=[("pstr",)], writes=[("negbT",)])
            for hl in range(4):
                j, half = hl // 2, hl % 2
                hp_ = slice(half * 64, (half + 1) * 64)
                for br in range(2):
                    gcol = 3 * (4 * g + hl) + 1 + br
                    Kt, V1 = (Ks, Vs1) if br == 0 else (Kw, Vw1)
                    kres, vres = (("Ks",), ("Vs1",)) if br == 0 else (("Kw",), ("Vw1",))
                    if br == 0:
                        units = [(kt, kt - 4 * qt + 4) for kt in range(4 * qt + 4)]
                    else:
                        units = [(4 * qt - 4 + o, o) for o in range(8) if 4 * qt - 4 + o >= 0]
                    aj = cnt["acc"] % 2
                    cnt["acc"] += 1
                    acc = ps_acc[aj]
                    P.op("pe", lambda e, acc=acc: e.matmul(acc[:].rearrange("p a b -> p (a b)"), lhsT=zer[:, 0:128], rhs=zer[:],
                                                           start=True, stop=False),
                         reads=[("zer",)], writes=[("psacc", aj)])
                    pend = []

                    def emit_pv(item, last_u, acc=acc, aj=aj, V1=V1, vres=vres, br=br):
                        pt, tj, kt, o = item
                        subs = [s_ for s_ in range(4) if (o - 4 <= s_ <= o if br == 1 else s_ >= o - 4)]
                        for s_ in subs:
                            P.op("pe", lambda e, acc=acc, pt=pt, V1=V1, kt=kt, s_=s_, last=(last_u and s_ == subs[-1]): e.matmul(
                                acc[:, s_, :], lhsT=pt[:, s_ * 128:(s_ + 1) * 128], rhs=V1[:, kt, :], start=False, stop=last),
                                reads=[("PT", tj), vres], writes=[("psacc", aj)])

                    for ui, (kt, o) in enumerate(units):
                        pj = cnt["pst3"] % 3
                        cnt["pst3"] += 1
                        pst = ps_st[pj]
                        ks = slice(kt * 128, (kt + 1) * 128)
                        need_wb = (o >= 4) if br == 0 else True
                        P.op("pe", lambda e, pst=pst, Kt=Kt, hp_=hp_, ks=ks, j=j, br=br, need_wb=need_wb: e.matmul(
                            pst[:], lhsT=Kt[hp_, ks], rhs=Qt[hp_, j, :], start=True, stop=(br == 1 and not need_wb)),
                            reads=[kres, ("Qt",)], writes=[("psst", pj)])
                        if br == 0:
                            P.op("pe", lambda e, pst=pst, ks=ks, need_wb=need_wb: e.matmul(
                                pst[:], lhsT=E[0:NBLK, ks], rhs=negbT[0:NBLK, :], start=False, stop=not need_wb),
                                reads=[("E",), ("negbT",)], writes=[("psst", pj)])
                        if need_wb:
                            P.op("pe", lambda e, pst=pst, o=o: e.matmul(pst[:], lhsT=ident[:], rhs=Wb[:, o, :], start=False, stop=True),
                                 reads=[("ident",), ("Wb",)], writes=[("psst", pj)])
                        tj = cnt["PT"] % 3
                        cnt["PT"] += 1
                        pt = PT[tj]
                        P.op("act", lambda e, pt=pt, pst=pst: e.activation(out=pt[:], in_=pst[:], func=AF.Exp),
                             reads=[("psst", pj)], writes=[("PT", tj)])
                        pend.append((pt, tj, kt, o))
                        if len(pend) > 2:
                            emit_pv(pend.pop(0), False)
                    while pend:
                        emit_pv(pend.pop(0), len(pend) == 0)
                    P.op("dve", lambda e, acc=acc: e.tensor_scalar(out=w4[:], in0=acc[:, :, 64], scalar1=1e-30, scalar2=None, op0=ALU.max),
                         reads=[("psacc", aj)], writes=[("w4",)])
                    P.op("dve", lambda e: e.reciprocal(out=w4[:], in_=w4[:]), reads=[("w4",)], writes=[("w4",)])
                    P.op("dve", lambda e, gcol=gcol: e.tensor_tensor(out=w4[:], in0=w4[:], in1=gtile[:, :, gcol], op=ALU.mult),
                         reads=[("w4",), ("gtile",)], writes=[("w4",)])
                    for s in range(4):
                        P.op("dve", lambda e, acc=acc, s=s, hl=hl: e.scalar_tensor_tensor(
                            out=oacc[:, s, hl, :], in0=acc[:, s, 0:64], scalar=w4[:, s:s + 1], in1=oacc[:, s, hl, :],
                            op0=ALU.mult, op1=ALU.add),
                            reads=[("psacc", aj), ("w4",), ("oacc", s, hl)], writes=[("oacc", s, hl)])
            ores = [("oacc", s, hl) for s in range(4) for hl in range(4)]
            P.op("act", lambda e: e.copy(out=oaccb[:], in_=oacc[:].rearrange("p s h d -> p s (h d)")), reads=ores, writes=[("oaccb",)])
            for jj in range(2):
                for s in range(4):
                    P.op("pe", lambda e, jj=jj, s=s: e.transpose(ps_tr[:, s * 128:(s + 1) * 128], oaccb[:, s, jj * 128:(jj + 1) * 128], ident[:]),
                         reads=[("oaccb",), ("ident",)], writes=[("pstr",)])
                P.op("act", lambda e, jj=jj: e.copy(out=o_fm[:, jj, :], in_=ps_tr[:]), reads=[("pstr",)], writes=[("ofm", jj)])
                P.dma("pool", lambda e, jj=jj, g=g, ts=ts: e.dma_start(out=oT[(2 * g + jj) * 128:(2 * g + jj + 1) * 128, ts], in_=o_fm[:, jj, :]),
                      "os", reads=[("ofm", jj)], writes=[("dram", "oT")])
    P.emit()
    return nc


def nsa_inputs(hm_bf_T, w_in, cmp_pos, cmp_w1, cmp_w2, hh, S_len):
    import ml_dtypes
    bf = ml_dtypes.bfloat16
    sw = np.concatenate([np.arange(32, 64), np.arange(0, 32)])

    def head_cols(base, h):
        return base + h * 64 + np.arange(64)

    chunks = []
    qh = [8 * hh + i for i in range(8)]
    for j in range(4):
        chunks.append(np.concatenate([head_cols(EV_OFF["q"], qh[2 * j]), head_cols(EV_OFF["q"], qh[2 * j + 1])]))
    for j in range(4):
        chunks.append(np.concatenate([head_cols(EV_OFF["q"], qh[2 * j])[sw], head_cols(EV_OFF["q"], qh[2 * j + 1])[sw]]))
    gg = [2 * hh, 2 * hh + 1]
    for nm in ("ksl", "kwn"):
        for g in gg:
            c = head_cols(EV_OFF[nm], g)
            chunks.append(np.concatenate([c, c]))
        for g in gg:
            c = head_cols(EV_OFF[nm], g)[sw]
            chunks.append(np.concatenate([c, c]))
    chunks.append(np.concatenate([head_cols(EV_OFF["kc"], gg[0]), head_cols(EV_OFF["kc"], gg[1])]))
    chunks.append(np.concatenate([head_cols(EV_OFF["kc"], gg[0])[sw], head_cols(EV_OFF["kc"], gg[1])[sw]]))
    chunks.append(np.concatenate([head_cols(EV_OFF["vc"], gg[0]), head_cols(EV_OFF["vc"], gg[1])]))
    wfm = tile_w_fm(w_in[:, np.concatenate(chunks)])
    tmc = np.concatenate([head_cols(EV_OFF["vsl"], gg[0]), head_cols(EV_OFF["vsl"], gg[1]),
                          head_cols(EV_OFF["vwn"], gg[0]), head_cols(EV_OFF["vwn"], gg[1]),
                          EV_OFF["gate"] + 24 * hh + np.arange(24)])
    wtm = tile_w_tm(w_in[:, tmc], NTM)
    w1 = cmp_w1.reshape(2, 32, 64, 256).transpose(0, 2, 1, 3).reshape(2, 64, 32 * 256)
    w1 = np.ascontiguousarray(np.concatenate([w1, w1], axis=1))
    w2k = cmp_w2[0].reshape(2, 128, 64).transpose(1, 0, 2)
    w2k = np.ascontiguousarray(np.concatenate([w2k, w2k], axis=2).reshape(128, 256))
    w2v = np.ascontiguousarray(cmp_w2[1].reshape(2, 128, 64).transpose(1, 0, 2).reshape(128, 128))
    posT = cmp_pos.transpose(2, 0, 1).reshape(64, 64)
    posT = np.ascontiguousarray(np.concatenate([posT, posT], axis=0))
    inv = (1.0 / (10000.0 ** (np.arange(0, 64, 2, dtype=np.float32) / np.float32(64)))).astype(np.float32)
    ang = np.arange(S_len, dtype=np.float32)[:, None] * inv[None, :]
    cos, sin = np.cos(ang).astype(np.float32).T, np.sin(ang).astype(np.float32).T
    p = np.arange(128)
    Ct = cos[p % 32]
    St = sin[p % 32] * np.where((p % 64) < 32, -1.0, 1.0).astype(np.float32)[:, None]
    j = np.arange(128)
    E = (np.arange(S_len)[None, :] // 64 == j[:, None]).astype(np.float32).astype(bf)
    c = np.arange(512)
    Wb = np.zeros((128, 8, 512), np.float32)
    for o in range(8):
        dlt = c[None, :] + 512 - 128 * o - p[:, None]
        Wb[:, o, :] = np.where((dlt >= 0) & (dlt < 512), 0.0, NEG)
    x = np.arange(1024) - 512
    Cw = np.where(16 * x[None, :] + 31 <= p[:, None], 0.0, NEG).astype(np.float32)
    jr = np.arange(256) - 128
    hi = (p[:, None] >= 64).astype(np.int64)
    A = (jr[None, :] <= hi - 2).astype(np.float32)
    forced = (jr[None, :] == hi) | (jr[None, :] == hi - 1)
    B = np.where(forced, 1e4, np.where(jr[None, :] > hi, -1.0, 0.0)).astype(np.float32)
    return {"hmT": hm_bf_T, "wfm": wfm, "wtm": wtm, "w1": w1, "w2k": w2k, "w2v": w2v, "posT": posT,
            "t_cq": np.ascontiguousarray(Ct * np.float32(0.125)), "t_sq": np.ascontiguousarray(St * np.float32(0.125)),
            "t_ck": np.ascontiguousarray(Ct), "t_sk": np.ascontiguousarray(St),
            "c_ident": np.eye(128, dtype=np.float32).astype(bf), "c_E": E,
            "c_Wb": Wb.reshape(128, 4096).astype(bf), "c_Cw": Cw.astype(bf),
            "c_AB": np.ascontiguousarray(np.concatenate([A, B], axis=1))}


NCORES = 4
SEQ = 8192
_PROGS = {}


def build_fused(S_len):
    nc = bass.Bass("TRN2", target_bir_lowering=False)
    ext = []
    nt = S_len // TT
    xT = nc.dram_tensor("xT", [D, S_len], F32, kind="ExternalInput").ap()
    xo = nc.dram_tensor("xo", [D, S_len], F32, kind="ExternalOutput").ap()
    x1 = nc.dram_tensor("x1", [D, S_len], F32, kind="Internal").ap()
    x2 = nc.dram_tensor("x2", [D, S_len], F32, kind="Internal").ap()
    hm0 = nc.dram_tensor("hm0", [D, S_len], BF16, kind="Internal").ap()
    hm1 = nc.dram_tensor("hm1", [D, S_len], BF16, kind="Internal").ap()
    oT0 = nc.dram_tensor("oT0", [2048, S_len], BF16, kind="Internal").ap()
    oT1 = nc.dram_tensor("oT1", [4096, S_len], BF16, kind="Internal").ap()

    def phase(prefix, fn, bind):
        P = Prog(nc, prefix, bind, ext)
        fn(P)
        nc.all_engine_barrier()
        nc.clear_and_free_semaphores(P.sem_handles)
        nc.all_engine_barrier()

    def wb(prefix, names, mix_kc):
        d = {}
        for nm in names:
            shp = [KD, 128, mix_kc * 128] if nm == "wout" else ([KF, 128, KD * 128] if nm[:2] in ("wg", "wu") else [KD, 128, KF * 128])
            d[nm + "_b"] = nc.dram_tensor(prefix + nm + "_bf", shp, BF16, kind="Internal").ap()
        return d

    cw = wb("C_", ["wout", "wg0", "wu0", "wd0", "wg1", "wu1", "wd1"], 16)
    ew = wb("E_", ["wout", "wg0", "wu0", "wd0"], 32)
    c_split = [["wout", "wg0", "wu0", "wd0"], ["wg1", "wu1", "wd1"]]
    e_split = [["wout", "wg0"], ["wu0", "wd0"]]

    phase("A_", lambda P: build_tok(nt, 0, 1, True, P=P), {"xT": xT, "xo": x1, "hm": hm0})
    for hh in range(2):
        phase(f"N{hh}_", lambda P: build_nsa(S_len, P=P), {"hmT": hm0, "oT": oT0[hh * 512:(hh + 1) * 512, :]})
    for hh in range(2):
        phase(f"S{hh}_", lambda P: build_ssd(S_len, P=P, extra_fn=lambda P2: tok_cast_jobs(P2, c_split[hh], 16)),
              dict({"hmT": hm0, "oT": oT0[1024 + hh * 512:1024 + (hh + 1) * 512, :]}, **{k + "_b": cw[k + "_b"] for k in c_split[hh]}))
    phase("C_", lambda P: build_tok(nt, 16, 2, True, P=P, preconv=True), dict({"xT": x1, "oT": oT0, "xo": x2, "hm": hm1}, **cw))
    for hh in range(2):
        phase(f"R{hh}_", lambda P: build_ret(S_len, P=P, extra_fn=lambda P2: tok_cast_jobs(P2, e_split[hh], 32)),
              dict({"hmT": hm1, "oT": oT1[hh * 2048:(hh + 1) * 2048, :]}, **{k + "_b": ew[k + "_b"] for k in e_split[hh]}))
    phase("E_", lambda P: build_tok(nt, 32, 1, False, P=P, preconv=True), dict({"xT": x2, "oT": oT1, "xo": xo}, **ew))
    return nc, ext


def _ffn_maps(m, i, wg, wu, wd):
    m[f"wg{i}"] = tile_w_in_out(wg, KD, KF)
    m[f"wu{i}"] = tile_w_in_out(wu, KD, KF)
    m[f"wd{i}"] = tile_w_in_out(wd, KF, KD)


def fused_inputs(S_len, norm_g, wgs, wus, wds, ev_w_in, ev_cmp_pos, ev_cmp_w1, ev_cmp_w2, ev_conv_w, ev_conv_b,
                 ev_dt_bias, ev_a_log, ev_d_skip, ev_ssm_norm, ev_w_out, od_w_in, od_w_out):
    ph = {}
    a = {}
    _ffn_maps(a, 0, wgs[0, 0], wus[0, 0], wds[0, 0])
    a["g_all"] = np.concatenate([gain_cols(norm_g[0, 0]), gain_cols(norm_g[0, 1]), gain_cols(norm_g[0, 2])], axis=1)
    ph["A_"] = a
    for hh in range(2):
        ph[f"N{hh}_"] = nsa_inputs(None, ev_w_in, ev_cmp_pos, ev_cmp_w1, ev_cmp_w2, hh, S_len)
        ph[f"S{hh}_"] = ssd_inputs(None, ev_w_in, ev_conv_w, ev_conv_b, ev_dt_bias, ev_a_log, ev_d_skip, ev_ssm_norm, hh)
        ph[f"R{hh}_"] = ret_inputs(None, od_w_in, 4 * hh, S_len)
    c = {"wout": tile_w_in_out(ev_w_out, 16, KD)}
    _ffn_maps(c, 0, wgs[0, 1], wus[0, 1], wds[0, 1])
    _ffn_maps(c, 1, wgs[1, 0], wus[1, 0], wds[1, 0])
    c["g_all"] = np.concatenate([gain_cols(norm_g[0, 3]), gain_cols(norm_g[0, 4]), gain_cols(norm_g[0, 5]),
                                 gain_cols(norm_g[1, 0]), gain_cols(norm_g[1, 1]), gain_cols(norm_g[1, 2])], axis=1)
    ph["C_"] = c
    for k in ("wout", "wg0", "wu0", "wd0"):
        ph["S0_"][k] = c[k]
    for k in ("wg1", "wu1", "wd1"):
        ph["S1_"][k] = c[k]
    e = {"wout": tile_w_in_out(od_w_out, 32, KD)}
    _ffn_maps(e, 0, wgs[1, 1], wus[1, 1], wds[1, 1])
    e["g_all"] = np.concatenate([gain_cols(norm_g[1, 3]), gain_cols(norm_g[1, 4]), gain_cols(norm_g[1, 5])], axis=1)
    ph["E_"] = e
    for k in ("wout", "wg0"):
        ph["R0_"][k] = e[k]
    for k in ("wu0", "wd0"):
        ph["R1_"][k] = e[k]
    return ph


def kernel(x, norm_g, ffn_w_gate, ffn_w_up, ffn_w_down, ev_w_in, ev_cmp_pos, ev_cmp_w1, ev_cmp_w2,
           ev_conv_w, ev_conv_b, ev_dt_bias, ev_a_log, ev_d_skip, ev_ssm_norm, ev_w_out, od_w_in, od_w_out):
    f32 = np.float32
    A = lambda t: np.asarray(t, f32)
    x = A(x)
    B, S_len = x.shape[0], x.shape[1]
    if "F" not in _PROGS:
        _PROGS["F"] = build_fused(S_len)
    nc, ext = _PROGS["F"]
    ph = fused_inputs(S_len, A(norm_g), A(ffn_w_gate), A(ffn_w_up), A(ffn_w_down), A(ev_w_in)[0], A(ev_cmp_pos)[0],
                      A(ev_cmp_w1)[0], A(ev_cmp_w2)[0], A(ev_conv_w)[0], A(ev_conv_b)[0], A(ev_dt_bias)[0],
                      A(ev_a_log)[0], A(ev_d_skip)[0], A(ev_ssm_norm)[0], A(ev_w_out)[0], A(od_w_in)[0], A(od_w_out)[0])
    shared = {}
    for full, name in ext:
        prefix = full[:len(full) - len(name)]
        shared[full] = ph[prefix][name]
    maps = [dict(shared, xT=np.ascontiguousarray(x[b].T)) for b in range(B)]
    res = run_bass_kernel_spmd(nc, maps, core_ids=list(range(B)))
    out = np.empty((B, S_len, D), f32)
    for b in range(B):
        out[b] = res.results[b]["xo"].T
    return out
```
